# Optimizing a Trainium2 kernel written in Bass

```python
import math
import jax, jax.numpy as jnp
from jax import lax
import numpy as np

D_MODEL = 2048
BATCH = 8
SEQ = 2048
DEPTH = 1

MIX_WIDTH = D_MODEL
GDN_WIDTH = MIX_WIDTH // 2
POOL_WIDTH = MIX_WIDTH - GDN_WIDTH
GDN_HEAD_DIM = 128
GDN_HEADS = GDN_WIDTH // GDN_HEAD_DIM
CONV_K = 4
CHUNK = 64
POOL_WINDOWS = (2, 4, 8, 16)
POOL_GROUPS = len(POOL_WINDOWS)
POOL_GROUP_DIM = POOL_WIDTH // POOL_GROUPS
MEM_LEN = 256
XATTN_HEADS = 4
XATTN_HEAD_DIM = D_MODEL // XATTN_HEADS
D_FF = 4 * D_MODEL
IN_COLS = 4 * GDN_WIDTH + 2 * GDN_HEADS + POOL_WIDTH
DEEPNORM_ALPHA = (2.0 * DEPTH) ** 0.25
DEEPNORM_BETA = (8.0 * DEPTH) ** -0.25
LN_EPS = 1e-5
NORM_EPS = 1e-6

kernel_name = "hybrid_gdn_pool_deepnorm_layer"


def layer_norm(x, g, b):
    xf = x.astype(jnp.float32)
    mu = jnp.mean(xf, axis=-1, keepdims=True)
    xc = xf - mu
    var = jnp.mean(xc * xc, axis=-1, keepdims=True)
    y = xc * lax.rsqrt(var + LN_EPS) * g.astype(jnp.float32) + b.astype(jnp.float32)
    return y.astype(x.dtype)


def l2norm(x):
    return x * lax.rsqrt(jnp.sum(x * x, axis=-1, keepdims=True) + NORM_EPS)


def causal_dwconv(x, w):
    c = x.shape[-1]
    return lax.conv_general_dilated(
        x, w.astype(x.dtype)[:, None, :], window_strides=(1,), padding=[(CONV_K - 1, 0)],
        dimension_numbers=("NWC", "WIO", "NWC"), feature_group_count=c)


def chunk_gated_delta_rule(q, k, v, g, beta):
    bsz, t_len, h, dk = q.shape
    dv = v.shape[-1]
    n = t_len // CHUNK

    def to_chunks(u):
        return u.reshape(bsz, n, CHUNK, h, u.shape[-1]).transpose(1, 0, 3, 2, 4)

    q = to_chunks(q * (dk ** -0.5))
    k = to_chunks(k)
    v = to_chunks(v)
    g = g.reshape(bsz, n, CHUNK, h).transpose(1, 0, 3, 2)
    beta = beta.reshape(bsz, n, CHUNK, h).transpose(1, 0, 3, 2)
    g = jnp.cumsum(g, axis=-1)

    idx = jnp.arange(CHUNK)
    lower_incl = idx[:, None] >= idx[None, :]
    strict = idx[:, None] > idx[None, :]
    diff = g[..., :, None] - g[..., None, :]
    decay = jnp.where(lower_incl, jnp.exp(jnp.where(lower_incl, diff, 0.0)), 0.0)

    k_beta = k * beta[..., None]
    v_beta = v * beta[..., None]
    L = jnp.where(strict, jnp.einsum("nbhcd,nbhmd->nbhcm", k_beta, k) * decay, 0.0)
    eye = jnp.eye(CHUNK, dtype=jnp.float32)
    rhs = jnp.concatenate([v_beta, k_beta * jnp.exp(g)[..., None]], axis=-1)
    sol = lax.linalg.triangular_solve(eye + L, rhs, left_side=True, lower=True, unit_diagonal=True)
    u, w = sol[..., :dv], sol[..., dv:]
    attn_intra = jnp.where(lower_incl, jnp.einsum("nbhcd,nbhmd->nbhcm", q, k) * decay, 0.0)

    def step(state, inp):
        q_c, k_c, u_c, w_c, g_c, a_c = inp
        v_new = u_c - jnp.einsum("bhck,bhkv->bhcv", w_c, state)
        o = (jnp.einsum("bhck,bhkv->bhcv", q_c * jnp.exp(g_c)[..., None], state)
             + jnp.einsum("bhcm,bhmv->bhcv", a_c, v_new))
        g_last = g_c[..., -1]
        k_dec = k_c * jnp.exp(g_last[..., None] - g_c)[..., None]
        state = state * jnp.exp(g_last)[..., None, None] + jnp.einsum("bhck,bhcv->bhkv", k_dec, v_new)
        return state, o

    s0 = jnp.zeros((bsz, h, dk, dv), jnp.float32)
    _, o = lax.scan(step, s0, (q, k, u, w, g, attn_intra))
    return o.transpose(1, 0, 3, 2, 4).reshape(bsz, t_len, h, dv)


def gated_deltanet(qkv, z, b, a, conv_w, a_log, dt_bias, norm_w):
    bsz, t_len, _ = qkv.shape
    qkv = jax.nn.silu(causal_dwconv(qkv, conv_w)).astype(jnp.float32)
    q, k, v = jnp.split(qkv, 3, axis=-1)
    shp = (bsz, t_len, GDN_HEADS, GDN_HEAD_DIM)
    q = l2norm(q.reshape(shp))
    k = l2norm(k.reshape(shp))
    v = v.reshape(shp)
    beta = jax.nn.sigmoid(b.astype(jnp.float32))
    g = -jnp.exp(a_log.astype(jnp.float32)) * jax.nn.softplus(
        a.astype(jnp.float32) + dt_bias.astype(jnp.float32))
    o = chunk_gated_delta_rule(q, k, v, g, beta)
    o = o * lax.rsqrt(jnp.mean(o * o, axis=-1, keepdims=True) + NORM_EPS) * norm_w.astype(jnp.float32)
    o = o * jax.nn.silu(z.astype(jnp.float32).reshape(shp))
    return o.reshape(bsz, t_len, GDN_WIDTH).astype(z.dtype)


def multiscale_pool(p, pool_w, pool_scale):
    bsz, t_len, _ = p.shape
    pg = p.astype(jnp.float32).reshape(bsz, t_len, POOL_GROUPS, POOL_GROUP_DIM)
    cs = jnp.cumsum(pg, axis=1)
    pos = jnp.arange(t_len)
    means = []
    for gi, win in enumerate(POOL_WINDOWS):
        c = cs[:, :, gi]
        lag = jnp.pad(c[:, : t_len - win], ((0, 0), (win, 0), (0, 0)))
        cnt = jnp.minimum(pos + 1, win).astype(jnp.float32)[None, :, None]
        means.append((c - lag) / cnt)
    pooled = jnp.stack(means, axis=2) - pg
    mixed = jnp.einsum("btgc,gcd->btgd", pooled.astype(p.dtype), pool_w)
    return mixed.reshape(bsz, t_len, POOL_WIDTH) * pool_scale


def memory_cross_attention(h, mem, wq, wk, wv, wo):
    bsz, t_len, _ = h.shape
    q = (h @ wq).reshape(bsz, t_len, XATTN_HEADS, XATTN_HEAD_DIM)
    k = (mem @ wk).reshape(bsz, mem.shape[1], XATTN_HEADS, XATTN_HEAD_DIM)
    v = (mem @ wv).reshape(bsz, mem.shape[1], XATTN_HEADS, XATTN_HEAD_DIM)
    s = jnp.einsum("bqhd,bmhd->bhqm", q, k).astype(jnp.float32) * (XATTN_HEAD_DIM ** -0.5)
    p = jax.nn.softmax(s, axis=-1).astype(v.dtype)
    o = jnp.einsum("bhqm,bmhd->bqhd", p, v).reshape(bsz, t_len, D_MODEL)
    return o @ wo


def setup_inputs(seed: int = 0) -> dict:
    key = jax.random.key(seed)
    ks = jax.random.split(key, 24)
    f32 = jnp.float32
    nrm = lambda k, shape, scale: jax.random.normal(k, shape, f32) * scale
    x = nrm(ks[0], (BATCH, SEQ, D_MODEL), 1.0)
    mem = nrm(ks[1], (BATCH, MEM_LEN, D_MODEL), 1.0)
    w_in = nrm(ks[2], (DEPTH, D_MODEL, IN_COLS), D_MODEL ** -0.5)
    conv_w = nrm(ks[3], (DEPTH, CONV_K, 3 * GDN_WIDTH), CONV_K ** -0.5)
    a_log = jnp.log(jax.random.uniform(ks[4], (DEPTH, GDN_HEADS), f32, 1.0, 16.0))
    dt = jnp.exp(jax.random.uniform(ks[5], (DEPTH, GDN_HEADS), f32, math.log(1e-3), math.log(1e-1)))
    dt_bias = dt + jnp.log(-jnp.expm1(-dt))
    gdn_norm_w = 1.0 + nrm(ks[6], (DEPTH, GDN_HEAD_DIM), 0.02)
    pool_w = nrm(ks[7], (DEPTH, POOL_GROUPS, POOL_GROUP_DIM, POOL_GROUP_DIM), POOL_GROUP_DIM ** -0.5)
    pool_scale = 1.0 + nrm(ks[8], (DEPTH, POOL_WIDTH), 0.1)
    w_out = nrm(ks[9], (DEPTH, MIX_WIDTH, D_MODEL), MIX_WIDTH ** -0.5 * DEEPNORM_BETA)
    ln1_g = 1.0 + nrm(ks[10], (DEPTH, D_MODEL), 0.02)
    ln1_b = nrm(ks[11], (DEPTH, D_MODEL), 0.02)
    xq_w = nrm(ks[12], (DEPTH, D_MODEL, D_MODEL), D_MODEL ** -0.5)
    xk_w = nrm(ks[13], (DEPTH, D_MODEL, D_MODEL), D_MODEL ** -0.5)
    xv_w = nrm(ks[14], (DEPTH, D_MODEL, D_MODEL), D_MODEL ** -0.5)
    xo_w = nrm(ks[15], (DEPTH, D_MODEL, D_MODEL), D_MODEL ** -0.5 * DEEPNORM_BETA)
    ln2_g = 1.0 + nrm(ks[16], (DEPTH, D_MODEL), 0.02)
    ln2_b = nrm(ks[17], (DEPTH, D_MODEL), 0.02)
    w_up = nrm(ks[18], (DEPTH, D_MODEL, D_FF), D_MODEL ** -0.5)
    w_down = nrm(ks[19], (DEPTH, D_FF, D_MODEL), D_FF ** -0.5 * DEEPNORM_BETA)
    ln3_g = 1.0 + nrm(ks[20], (DEPTH, D_MODEL), 0.02)
    ln3_b = nrm(ks[21], (DEPTH, D_MODEL), 0.02)
    return {"x": x, "mem": mem, "w_in": w_in, "conv_w": conv_w, "a_log": a_log, "dt_bias": dt_bias,
            "gdn_norm_w": gdn_norm_w, "pool_w": pool_w, "pool_scale": pool_scale, "w_out": w_out,
            "ln1_g": ln1_g, "ln1_b": ln1_b, "xq_w": xq_w, "xk_w": xk_w, "xv_w": xv_w, "xo_w": xo_w,
            "ln2_g": ln2_g, "ln2_b": ln2_b, "w_up": w_up, "w_down": w_down, "ln3_g": ln3_g, "ln3_b": ln3_b}


def reference(x, mem, w_in, conv_w, a_log, dt_bias, gdn_norm_w, pool_w, pool_scale, w_out,
              ln1_g, ln1_b, xq_w, xk_w, xv_w, xo_w, ln2_g, ln2_b, w_up, w_down, ln3_g, ln3_b):
    W, H = GDN_WIDTH, GDN_HEADS
    h = x
    for l in range(DEPTH):
        proj = h @ w_in[l]
        qkv = proj[..., : 3 * W]
        z = proj[..., 3 * W: 4 * W]
        b = proj[..., 4 * W: 4 * W + H]
        a = proj[..., 4 * W + H: 4 * W + 2 * H]
        p = proj[..., 4 * W + 2 * H:]
        o_gdn = gated_deltanet(qkv, z, b, a, conv_w[l], a_log[l], dt_bias[l], gdn_norm_w[l])
        o_pool = multiscale_pool(p, pool_w[l], pool_scale[l])
        mix = jnp.concatenate([o_gdn, o_pool], axis=-1) @ w_out[l]
        h = layer_norm(DEEPNORM_ALPHA * h + mix, ln1_g[l], ln1_b[l])
        xa = memory_cross_attention(h, mem, xq_w[l], xk_w[l], xv_w[l], xo_w[l])
        h = layer_norm(DEEPNORM_ALPHA * h + xa, ln2_g[l], ln2_b[l])
        ff = jnp.square(jax.nn.relu(h @ w_up[l])) @ w_down[l]
        h = layer_norm(DEEPNORM_ALPHA * h + ff, ln3_g[l], ln3_b[l])
    return h
```

```python
import numpy as np
from contextlib import ExitStack
import concourse.bass as bass
import concourse.mybir as mybir
from concourse.bass_utils import run_bass_kernel_spmd

F32 = mybir.dt.float32
F32R = mybir.dt.float32r
AF = mybir.ActivationFunctionType
ALU = mybir.AluOpType
AX = mybir.AxisListType

EPOCH = 8000
D = 2048
SEQ = 2048
TT = 512
NDC = 16
MEM = 256
DFF = 8192
INC = 5136
ALPHA = 2.0 ** 0.25
LN_EPS = 1e-5
NORM_EPS = 1e-6
NSLOT = 3
NBIG = 14


class Prog:
    COMPUTE = ("pe", "act", "dve", "pool")

    def __init__(self, nc, stack):
        self.nc = nc
        self.stack = stack
        self.ops = {e: [] for e in ("pe", "act", "dve", "pool", "sp")}
        self.count = {e: 0 for e in self.COMPUTE}
        self.sems = {}
        self.known = {e: {} for e in self.ops}
        self.last_write = {}
        self.readers = {}
        self.dma_count = {}

    def _sem(self, name):
        if name not in self.sems:
            self.sems[name] = self.stack.enter_context(self.nc.semaphore(name))
        return self.sems[name]

    def _phys(self, cname, v):
        if cname in self.COMPUTE:
            return self._sem(f"s_{cname}_{(v - 1) // EPOCH}"), (v - 1) % EPOCH + 1
        return self._sem(cname), v

    def _collect(self, E, reads, writes):
        deps = []
        for k in reads:
            t = self.last_write.get(k)
            if t is not None:
                deps.append(t)
        for k in writes:
            t = self.last_write.get(k)
            if t is not None:
                deps.append(t)
            deps.extend(self.readers.get(k, ()))
        deps.sort(key=lambda t: -t[1])
        kn = self.known[E]
        waits = {}
        for (cname, v, snap) in deps:
            if cname == E and E == "pe":
                continue
            if kn.get(cname, 0) >= v:
                continue
            waits[cname] = max(waits.get(cname, 0), v)
            kn[cname] = v
            for kk, vv in snap.items():
                if kn.get(kk, 0) < vv:
                    kn[kk] = vv
        return [self._phys(c, v) for c, v in waits.items()]

    @staticmethod
    def _excl(reads, writes):
        pr = [k for k in reads if k[0] == "p" and k[1:].isdigit()]
        if pr:
            reads = [k for k in reads if k not in pr]
            writes = list(writes) + pr
        return reads, writes

    def op(self, E, fn, reads=(), writes=()):
        reads, writes = self._excl(reads, writes)
        waits = self._collect(E, reads, writes)
        idx = self.count[E]
        self.count[E] += 1
        snap = dict(self.known[E])
        snap[E] = idx + 1
        tok = (E, idx + 1, snap)
        sem, _ = self._phys(E, idx + 1)
        self.ops[E].append((waits, fn, sem, 1))
        self._record(tok, reads, writes)
        return tok

    def dma(self, slot, fn, reads=(), writes=(), E="sp"):
        waits = self._collect(E, reads, writes)
        cname = "dma_" + E + "_" + slot
        self.dma_count[cname] = self.dma_count.get(cname, 0) + 16
        tok = (cname, self.dma_count[cname], dict(self.known[E]))
        self.ops[E].append((waits, fn, self._sem(cname), 16))
        self._record(tok, reads, writes)
        return tok

    def _record(self, tok, reads, writes):
        for k in reads:
            self.readers.setdefault(k, []).append(tok)
        for k in writes:
            self.last_write[k] = tok
            self.readers[k] = []

    def wait_tokens(self, E, toks):
        kn = self.known[E]
        waits = []
        for (cname, v, snap) in toks:
            if kn.get(cname, 0) >= v:
                continue
            kn[cname] = v
            waits.append(self._phys(cname, v))
        self.ops[E].append((waits, None, None, 0))

    def emit(self):
        nc = self.nc
        engs = {"pe": "tensor", "act": "scalar", "dve": "vector", "pool": "gpsimd", "sp": "sync"}
        with nc.Block() as block:
            for E, attr in engs.items():
                ops = self.ops[E]

                def body(eng, ops=ops):
                    for (waits, fn, sem, n) in ops:
                        for (s, v) in waits:
                            eng.wait_ge(s, v)
                        if fn is not None:
                            ins = fn(eng)
                            ins.then_inc(sem, n)

                getattr(block, attr)(body)


def host_consts():
    i = np.arange(128)
    same = (i[:, None] // 64) == (i[None, :] // 64)
    c = {}
    c["ident"] = np.eye(128, dtype=np.float32)
    c["ones"] = np.ones((128, 128), np.float32)
    c["avg"] = np.full((128, 128), 1.0 / D, np.float32)
    c["tribd"] = (same & (i[:, None] <= i[None, :])).astype(np.float32)
    c["bd"] = same.astype(np.float32)
    c["msbd"] = (same & (i[:, None] > i[None, :])).astype(np.float32)
    c["minclt"] = (same & (i[:, None] <= i[None, :])).astype(np.float32) * np.float32(128.0 ** -0.5)
    rc = np.zeros((128, 4, 16), np.float32)
    for g, win in enumerate((2, 4, 8, 16)):
        rc[:, g, :] = 1.0 / np.minimum(np.arange(16) + 1, win).astype(np.float32)
    c["rc16"] = rc
    return c


def tile_slabs(stop_after=3):
    L = []
    for hp in range(4):
        for part in range(4):
            for hh in range(2):
                L.append(("w_in", 0, part * 1024 + hp * 256 + hh * 128))
    for g in range(4):
        for cc in range(2):
            L.append(("w_in", 0, 4112 + g * 256 + cc * 128))
    for c in range(16):
        L.append(("w_out", 0, c * 128))
    if stop_after >= 2:
        for c in range(16):
            L.append(("xq_w", 0, c * 128))
        for c in range(16):
            L.append(("xo_w", 0, c * 128))
    if stop_after >= 3:
        for g in range(4):
            for j in range(16):
                L.append(("w_up", 0, g * 2048 + j * 128))
            for j in range(16):
                L.append(("w_down", g * 2048, j * 128))
    return L


def build(ntiles=4, stop_after=3, dbg=9):
    nc = bass.Bass("TRN2", target_bir_lowering=False)
    T = ntiles * TT

    def din(name, shape):
        return nc.dram_tensor(name, list(shape), F32, kind="ExternalInput").ap()

    x_d = din("x", [T, D])
    mem_d = din("mem", [MEM, D])
    W = {"w_in": din("w_in", [D, INC]), "w_out": din("w_out", [D, D]), "xq_w": din("xq_w", [D, D]),
         "xk_w": din("xk_w", [D, D]), "xv_w": din("xv_w", [D, D]), "xo_w": din("xo_w", [D, D]),
         "w_up": din("w_up", [D, DFF]), "w_down": din("w_down", [DFF, D])}
    poolw_d = din("pool_w", [4, 256, 256])
    cw_d = din("cw", [128, 24, 4])
    alog_d = din("alog", [8, 1])
    dtb_d = din("dtb", [8, 1])
    normw_d = din("normw", [128, 128])
    pscale_d = din("pscale", [128, 8])
    lnp_d = din("lnp", [128, 6, 16])
    cst_d = {k: din(k, v.shape) for k, v in host_consts().items()}
    out_d = nc.dram_tensor("out", [T, D], F32, kind="ExternalOutput").ap()

    with ExitStack() as st:
        P = Prog(nc, st)

        def sb(name, shape, dt=F32):
            return st.enter_context(nc.sbuf_tensor(name, list(shape), dt))

        wsl = [sb(f"wsl{i}", [128, 16, 128], F32R) for i in range(NSLOT)]
        hT = sb("hT", [128, 16, TT], F32R)
        bufA = sb("bufA", [128, 16, TT], F32R)
        bufB = sb("bufB", [128, 16, TT], F32R)
        kxT = sb("kxT", [128, 16, MEM], F32R)
        vx = sb("vx", [128, 2, D], F32R)
        big = [sb(f"big{i}", [128, 512], F32) for i in range(NBIG)]
        raw = [sb(f"raw{i}", [128, 528], F32) for i in range(4)]
        lnsq = [sb(f"lnsq{i}", [128, 512], F32R) for i in range(2)]
        Sst = sb("Sst", [128, 8, 128], F32)
        halo = sb("halo", [128, 24, 3], F32)
        phalo = sb("phalo", [128, 8, 15], F32)
        wba = sb("wba", [128, 16, 16], F32R)
        cw = sb("cw_s", [128, 24, 4])
        alog = sb("alog_s", [8, 1]); dtb = sb("dtb_s", [8, 1]); nega = sb("nega", [8, 1])
        normw = sb("normw_s", [128, 128])
        pscale = sb("pscale_s", [128, 8])
        lnp = sb("lnp_s", [128, 6, 16])
        ident = sb("ident_s", [128, 128]); ones = sb("ones_s", [128, 128])
        avg = sb("avg_s", [128, 128], F32R)
        bdm = sb("bd_s", [128, 128]); tribd = sb("tribd_s", [128, 128]); msbd = sb("msbd_s", [128, 128]); minclt = sb("minclt_s", [128, 128])
        rc16 = sb("rc16_s", [128, 4, 16])
        toks = sb("toks", [128, 6, 4, 8])
        small = sb("small", [128, 16])
        ps = [st.enter_context(nc.psum_tensor(f"ps{i}", [128, 512], F32)) for i in range(8)]

        def f32(ap):
            return ap.bitcast(F32)

        def AK(pref, cs):
            return [f"{pref}{c}" for c in cs]

        def BQ(i, j, n=1):
            return big[i][:, j * 128:(j + n) * 128], [f"a{i}_{q}" for q in range(j, j + n)]

        def PQ(b, j, n=1):
            return ps[b][:, j * 128:(j + n) * 128], [f"p{b}"] * n

        CK = ["const"]
        for (dst, src) in ((cw, cw_d), (alog, alog_d), (dtb, dtb_d), (normw, normw_d), (pscale, pscale_d),
                           (lnp, lnp_d), (ident, cst_d["ident"]), (ones, cst_d["ones"]), (tribd, cst_d["tribd"]), (bdm, cst_d["bd"]),
                           (msbd, cst_d["msbd"]), (minclt, cst_d["minclt"]), (rc16, cst_d["rc16"])):
            P.dma("const", lambda e, dst=dst, src=src: e.dma_start(out=dst[:], in_=src), writes=CK)
        P.dma("const", lambda e: e.dma_start(out=avg[:], in_=cst_d["avg"]), writes=CK, E="pool")
        P.dma("const", lambda e: e.dma_start(
            out=wba[:], in_=W["w_in"][:, 4096:4112].rearrange("(kc p) c -> p kc c", p=128)), writes=CK, E="pool")
        P.op("dve", lambda e: e.memset(Sst[:], 0.0), writes=AK("S", range(8)))
        P.op("dve", lambda e: e.memset(halo[:], 0.0), writes=["halo"])
        P.op("dve", lambda e: e.memset(phalo[:], 0.0), writes=["phalo"])
        P.op("act", lambda e: e.activation(nega[:], alog[:], AF.Exp), reads=CK, writes=["nega"])
        P.op("dve", lambda e: e.tensor_scalar(nega[:], nega[:], -1.0, None, ALU.mult), reads=["nega"], writes=["nega"])

        sched = []
        ws_state = {"issued": 0, "pos": 0}

        def slab_src(name, r0, c0):
            return W[name][r0:r0 + 2048, c0:c0 + 128].rearrange("(kc p) c -> p kc c", p=128)

        def ws_issue_upto(n):
            while ws_state["issued"] < min(n, len(sched)):
                j = ws_state["issued"]
                s = j % NSLOT
                src = slab_src(*sched[j])
                P.dma(f"w{s}", lambda e, s=s, src=src: e.dma_start(out=wsl[s][:], in_=src), writes=[f"w{s}"], E="pool")
                ws_state["issued"] += 1

        def ws_use(expect):
            i = ws_state["pos"]
            assert sched[i] == expect, (i, sched[i], expect)
            ws_issue_upto(i + NSLOT)
            ws_state["pos"] += 1
            return wsl[i % NSLOT], f"w{i % NSLOT}"

        if stop_after >= 2:
            for c in range(16):
                sched.append(("xk_w", 0, c * 128))
            for c in range(16):
                sched.append(("xv_w", 0, c * 128))
        for _ in range(ntiles):
            sched.extend(tile_slabs(stop_after)[:{0: 0, 1: 0, 2: 32, 3: 40}.get(dbg, 32 if dbg < 2 else 10 ** 6)])

        pj = {"i": 0}

        def proj_bank():
            b = pj["i"] % 2
            pj["i"] += 1
            return ps[b], [f"p{b}"]

        def project(expect, act_tile, act_pref, n=TT):
            wt, wk = ws_use(expect)
            pb, pk = proj_bank()
            for kc in range(16):
                P.op("pe", lambda e, kc=kc, wt=wt, pb=pb: e.matmul(pb[:, 0:n], wt[:, kc, :], act_tile[:, kc, 0:n],
                                                                  start=(kc == 0), stop=(kc == 15)),
                     reads=[wk, f"{act_pref}{kc}"], writes=pk)
            return pb, pk

        def layer_norm(y, ypref, li):
            mean_s, mk = big[0], [f"a0_{q}" for q in range(4)]
            rstd_s, rk = big[1], [f"a1_{q}" for q in range(4)]
            pb, pk = proj_bank()
            for dc in range(16):
                P.op("pe", lambda e, dc=dc: e.matmul(pb[:], avg[:], y[:, dc, :], start=(dc == 0), stop=(dc == 15)),
                     reads=CK + [f"{ypref}{dc}"], writes=pk)
            P.op("act", lambda e: e.copy(mean_s[:], pb[:]), reads=pk, writes=mk)
            pb2, pk2 = proj_bank()
            for dc in range(16):
                P.op("dve", lambda e, dc=dc: e.tensor_tensor(y[:, dc, :], f32(y[:, dc, :]), mean_s[:], ALU.subtract),
                     reads=mk + [f"{ypref}{dc}"], writes=[f"{ypref}{dc}"])
                sq_t = lnsq[dc % 2]
                sqk = [f"lnsq{dc % 2}"]
                P.op("act", lambda e, dc=dc, sq_t=sq_t: e.activation(sq_t[:], f32(y[:, dc, :]), AF.Square),
                     reads=[f"{ypref}{dc}"], writes=sqk)
                P.op("pe", lambda e, dc=dc, sq_t=sq_t: e.matmul(pb2[:], avg[:], sq_t[:],
                                                              start=(dc == 0), stop=(dc == 15)),
                     reads=CK + sqk, writes=pk2)
            P.op("act", lambda e: e.activation(rstd_s[:], pb2[:], AF.Sqrt, bias=LN_EPS), reads=pk2, writes=rk)
            P.op("dve", lambda e: e.reciprocal(rstd_s[:], rstd_s[:]), reads=rk, writes=rk)
            for dc in range(16):
                eng = "dve" if dc % 2 == 0 else "pool"
                P.op(eng, lambda e, dc=dc: e.tensor_tensor(y[:, dc, :], f32(y[:, dc, :]), rstd_s[:], ALU.mult),
                     reads=rk + [f"{ypref}{dc}"], writes=[f"{ypref}{dc}"])
                P.op("act", lambda e, dc=dc: e.activation(hT[:, dc, :], f32(y[:, dc, :]), AF.Identity,
                                                          scale=lnp[:, 2 * li, dc:dc + 1], bias=lnp[:, 2 * li + 1, dc:dc + 1]),
                     reads=CK + [f"{ypref}{dc}"], writes=[f"H{dc}"])

        PRO = stop_after >= 2
        memtok = bufB[:].rearrange("p c t -> p (c t)")
        for mb in range(2 if PRO else 0):
            P.dma(f"xin{mb}", lambda e, mb=mb: e.dma_start(out=memtok[:, mb * D:(mb + 1) * D],
                                                      in_=mem_d[mb * 128:(mb + 1) * 128, :]),
                  writes=AK("B", range(4 * mb, 4 * mb + 4)), E="pool")
        memT = bufA[:].rearrange("p c t -> p (c t)")
        for dc in range(16 if PRO else 0):
            tb_, tk_ = PQ(3 + dc % 2, 0, 2)
            for mb in range(2):
                P.op("pe", lambda e, dc=dc, mb=mb, tb_=tb_: e.transpose(
                    tb_[:, mb * 128:(mb + 1) * 128], f32(memtok[:, mb * D + dc * 128: mb * D + (dc + 1) * 128]), ident[:]),
                    reads=CK + AK("B", [4 * mb + dc // 4]), writes=tk_)
            P.op("act" if dc % 2 else "dve", (lambda e, dc=dc, tb_=tb_: e.copy(memT[:, dc * 256:(dc + 1) * 256], tb_))
                 if dc % 2 else (lambda e, dc=dc, tb_=tb_: e.tensor_copy(memT[:, dc * 256:(dc + 1) * 256], tb_)),
                 reads=tk_, writes=AK("A", [dc // 2]))
        for c in range(16 if PRO else 0):
            wt, wk = ws_use(("xk_w", 0, c * 128))
            pb, pk = proj_bank()
            for kc in range(16):
                P.op("pe", lambda e, kc=kc, wt=wt, pb=pb: e.matmul(pb[:, 0:256], wt[:, kc, :], memT[:, kc * 256:(kc + 1) * 256],
                                                                  start=(kc == 0), stop=(kc == 15)),
                     reads=[wk] + AK("A", [kc // 2]), writes=pk)
            P.op("act", lambda e, c=c, pb=pb: e.copy(kxT[:, c, :], pb[:, 0:256]), reads=pk, writes=["kx"])
        for c in range(16 if PRO else 0):
            wt, wk = ws_use(("xv_w", 0, c * 128))
            pb, pk = proj_bank()
            for mb in range(2):
                for kc in range(16):
                    P.op("pe", lambda e, kc=kc, mb=mb, wt=wt, pb=pb: e.matmul(
                        pb[:, mb * 128:(mb + 1) * 128], memT[:, kc * 256 + mb * 128: kc * 256 + (mb + 1) * 128], wt[:, kc, :],
                        start=(kc == 0), stop=(kc == 15)),
                        reads=[wk] + AK("A", [kc // 2]), writes=pk)
            P.op("dve", lambda e, c=c, pb=pb: e.tensor_copy(
                vx[:, :, c * 128:(c + 1) * 128], pb[:, 0:256].rearrange("p (m c) -> p m c", m=2)), reads=pk, writes=["vx"])

        out_toks = []
        unit_ctr = {"u": 0}

        for ti in range(ntiles):
            t0 = ti * TT
            xtok = bufB[:].rearrange("p c t -> p (c t)")
            for tb in range(4):
                P.dma(f"xin{tb}", lambda e, tb=tb, t0=t0: e.dma_start(out=xtok[:, tb * D:(tb + 1) * D],
                                                          in_=x_d[t0 + tb * 128: t0 + (tb + 1) * 128, :]),
                      writes=AK("B", range(4 * tb, 4 * tb + 4)), E="pool")
            for dc in range(16):
                pb, pk = proj_bank()
                for tb in range(4):
                    P.op("pe", lambda e, dc=dc, tb=tb, pb=pb: e.transpose(
                        pb[:, tb * 128:(tb + 1) * 128], f32(xtok[:, tb * D + dc * 128: tb * D + (dc + 1) * 128]), ident[:]),
                        reads=CK + AK("B", [4 * tb + dc // 4]), writes=pk)
                if dc % 2:
                    P.op("act", lambda e, dc=dc, pb=pb: e.copy(hT[:, dc, :], pb[:]), reads=pk, writes=[f"H{dc}"])
                else:
                    P.op("dve", lambda e, dc=dc, pb=pb: e.tensor_copy(hT[:, dc, :], pb[:]), reads=pk, writes=[f"H{dc}"])

            bfm, bfk = big[12], [f"a12_{q}" for q in range(4)]
            gfm, gfk = big[13], [f"a13_{q}" for q in range(4)]
            for part, (dst, dk_) in enumerate(((bfm, bfk), (gfm, gfk)) if dbg >= 1 else ()):
                pb, pk = proj_bank()
                for kc in range(16):
                    P.op("pe", lambda e, kc=kc, part=part, pb=pb: e.matmul(pb[0:8, :], wba[:, kc, part * 8:(part + 1) * 8],
                                                                          hT[:, kc, :], start=(kc == 0), stop=(kc == 15)),
                         reads=CK + [f"H{kc}"], writes=pk)
                if part == 0:
                    P.op("act", lambda e, pb=pb, dst=dst: e.activation(dst[0:8, :], pb[0:8, :], AF.Sigmoid), reads=pk, writes=dk_)
                else:
                    P.op("act", lambda e, pb=pb, dst=dst: e.activation(dst[0:8, :], pb[0:8, :], AF.Exp, bias=dtb[:, 0:1]),
                         reads=pk + CK, writes=dk_)
                    P.op("act", lambda e, dst=dst: e.activation(dst[0:8, :], dst[0:8, :], AF.Ln, bias=1.0), reads=dk_, writes=dk_)
                    P.op("dve", lambda e, dst=dst: e.tensor_scalar(dst[0:8, :], dst[0:8, :], nega[:, 0:1], None, ALU.mult),
                         reads=dk_ + ["nega"], writes=dk_)
            for tb in range(4 if dbg >= 1 else 0):
                tq, tqk = PQ(3 + tb % 2, 0)
                P.op("pe", lambda e, tb=tb, tq=tq: e.transpose(tq[:, 0:8], bfm[0:8, tb * 128:(tb + 1) * 128], ident[0:8, 0:8]),
                     reads=CK + bfk, writes=tqk)
                P.op("pe", lambda e, tb=tb, tq=tq: e.transpose(tq[:, 8:16], gfm[0:8, tb * 128:(tb + 1) * 128], ident[0:8, 0:8]),
                     reads=CK + gfk, writes=tqk)
                P.op("dve", lambda e, tb=tb, tq=tq: e.tensor_copy(toks[:, 0:2, tb, :], tq[:, 0:16].rearrange("p (a h) -> p a h", a=2)),
                     reads=tqk, writes=["toks"])
                tq2, tqk2 = PQ(3 + tb % 2, 1)
                P.op("pe", lambda e, tb=tb, tq2=tq2: e.matmul(tq2[:, 0:8], tribd[:], toks[:, 1, tb, :], start=True, stop=True),
                     reads=CK + ["toks"], writes=tqk2)
                P.op("pe", lambda e, tb=tb, tq2=tq2: e.matmul(tq2[:, 8:16], bdm[:], toks[:, 1, tb, :], start=True, stop=True),
                     reads=CK + ["toks"], writes=tqk2)
                P.op("dve", lambda e, tb=tb, tq2=tq2: e.tensor_copy(toks[:, 2:4, tb, :], tq2[:, 0:16].rearrange("p (a h) -> p a h", a=2)),
                     reads=tqk2, writes=["toks"])
            P.op("act", lambda e: e.activation(toks[:, 4, :, :], toks[:, 2, :, :], AF.Exp), reads=["toks"], writes=["toks"])
            P.op("dve", lambda e: e.tensor_tensor(toks[:, 4, :, :], toks[:, 4, :, :], toks[:, 0, :, :], ALU.mult),
                 reads=["toks"], writes=["toks"])
            P.op("dve", lambda e: e.tensor_tensor(toks[:, 5, :, :], toks[:, 3, :, :], toks[:, 2, :, :], ALU.subtract),
                 reads=["toks"], writes=["toks"])
            P.op("act", lambda e: e.activation(toks[:, 5, :, :], toks[:, 5, :, :], AF.Exp), reads=["toks"], writes=["toks"])

            fin = lambda part, hh: (bufB[:, part * 2 + hh, :], f"B{part * 2 + hh}")
            for hp in range(4 if dbg >= 1.3 else 0):
                for part in range(4):
                    for hh in range(2):
                        h = 2 * hp + hh
                        pb, pk = project(("w_in", 0, part * 1024 + hp * 256 + hh * 128), hT, "H")
                        dstw, dkey = fin(part, hh)
                        dst = f32(dstw)
                        if part == 3:
                            P.op("act", lambda e, pb=pb, dstw=dstw: e.copy(dstw, pb[:]), reads=pk, writes=[dkey])
                            continue
                        ch = part * 8 + h
                        r = raw[(part * 2 + hh) % 2]
                        rk_ = f"raw{(part * 2 + hh) % 2}"
                        P.op("act", lambda e, pb=pb, r=r: e.copy(r[:, 3:515], pb[:]), reads=pk, writes=[rk_])
                        P.op("pool", lambda e, r=r, ch=ch: e.tensor_copy(r[:, 0:3], halo[:, ch, :]), reads=["halo"], writes=[rk_])
                        P.op("pool", lambda e, r=r, ch=ch: e.tensor_copy(halo[:, ch, :], r[:, 512:515]), reads=[rk_], writes=["halo"])
                        eng = "dve" if hh == 0 else "pool"
                        P.op(eng, lambda e, r=r, ch=ch, dstw=dstw: e.tensor_scalar(dstw, r[:, 0:512], cw[:, ch, 0:1], None, ALU.mult),
                             reads=CK + [rk_], writes=[dkey])
                        for j in range(1, 4):
                            P.op("dve", lambda e, r=r, ch=ch, dst=dst, dstw=dstw, j=j: e.scalar_tensor_tensor(
                                dstw, r[:, j:j + 512], cw[:, ch, j:j + 1], dst, ALU.mult, ALU.add),
                                reads=CK + [rk_, dkey], writes=[dkey])
                        P.op("act", lambda e, dst=dst, dstw=dstw: e.activation(dstw, dst, AF.Silu), reads=[dkey], writes=[dkey])
                        if part < 2:
                            sqb, sqk = big[2 + hh], [f"a{2 + hh}_{q}" for q in range(4)]
                            P.op("act", lambda e, dst=dst, sqb=sqb: e.activation(sqb[:], dst, AF.Square), reads=[dkey], writes=sqk)
                            pb2, pk2 = proj_bank()
                            P.op("pe", lambda e, pb2=pb2, sqb=sqb: e.matmul(pb2[:], ones[:], sqb[:], start=True, stop=True),
                                 reads=CK + sqk, writes=pk2)
                            P.op("act", lambda e, pb2=pb2, sqb=sqb: e.activation(sqb[:], pb2[:], AF.Sqrt, bias=NORM_EPS),
                                 reads=pk2, writes=sqk)
                            P.op("dve", lambda e, sqb=sqb: e.reciprocal(sqb[:], sqb[:]), reads=sqk, writes=sqk)
                            P.op(eng, lambda e, dst=dst, dstw=dstw, sqb=sqb: e.tensor_tensor(dstw, dst, sqb[:], ALU.mult),
                                 reads=sqk + [dkey], writes=[dkey])
                qf = [f32(fin(0, hh)[0]) for hh in range(2)]
                kf = [f32(fin(1, hh)[0]) for hh in range(2)]
                vf = [f32(fin(2, hh)[0]) for hh in range(2)]
                zf = [f32(fin(3, hh)[0]) for hh in range(2)]
                qk_ = [fin(0, hh)[1] for hh in range(2)]
                kk_ = [fin(1, hh)[1] for hh in range(2)]
                vk_ = [fin(2, hh)[1] for hh in range(2)]
                zk_ = [fin(3, hh)[1] for hh in range(2)]
                for tb in range(4 if dbg >= 1.6 else 0):
                    wp = tb % 2
                    cs = slice(tb * 128, (tb + 1) * 128)
                    Rw, Rk = BQ(4 + 2 * wp, 0, 2)
                    W1, W1k = BQ(4 + 2 * wp, 2, 2)
                    W2, W2k = BQ(5 + 2 * wp, 0, 2)
                    W3, W3k = BQ(5 + 2 * wp, 2, 2)
                    sdec = small[:, 4 * wp:4 * wp + 4]
                    sdk = [f"sdec{wp}"]
                    P.op("pool", lambda e, Rw=Rw, tb=tb, hp=hp: e.tensor_tensor(
                        Rw.rearrange("p (a t) -> p a t", a=2), tribd[:].unsqueeze(1).to_broadcast([128, 2, 128]),
                        toks[:, 1, tb, 2 * hp:2 * hp + 2].unsqueeze(2).to_broadcast([128, 2, 128]), ALU.mult),
                        reads=CK + ["toks"], writes=Rk)
                    rb, rbk = PQ(7, 2 * wp, 2)
                    P.op("pe", lambda e, rb=rb, Rw=Rw: e.matmul(rb, ones[:], Rw, start=True, stop=True), reads=CK + Rk, writes=rbk)
                    for hh in range(2):
                        h = 2 * hp + hh
                        P.op("dve", lambda e, hh=hh, h=h, tb=tb, rb=rb, W1=W1: e.tensor_scalar(
                            W1[:, hh * 128:(hh + 1) * 128], rb[:, hh * 128:(hh + 1) * 128], toks[:, 2, tb, h:h + 1], 0.0,
                            ALU.subtract, ALU.max), reads=rbk + ["toks"], writes=W1k)
                        P.op("dve", lambda e, hh=hh, h=h, tb=tb, rb=rb, W2=W2: e.tensor_scalar(
                            W2[:, hh * 128:(hh + 1) * 128], rb[:, hh * 128:(hh + 1) * 128], toks[:, 2, tb, h:h + 1], 0.0,
                            ALU.subtract, ALU.min), reads=rbk + ["toks"], writes=W2k)
                    P.op("act", lambda e, W1=W1: e.activation(W1, W1, AF.Exp, scale=-1.0), reads=W1k, writes=W1k)
                    P.op("act", lambda e, W2=W2: e.activation(W2, W2, AF.Exp), reads=W2k, writes=W2k)
                    P.op("act", lambda e, W3=W3, rb=rb: e.activation(W3, rb, AF.Exp, bias=float(-0.5 * np.log(128.0))), reads=rbk, writes=W3k)
                    P.op("act", lambda e, rb=rb, sdec=sdec: e.activation(
                        sdec.rearrange("p (a c) -> p a c", a=2), rb.rearrange("p (a c t) -> p a c t", a=2, c=2)[:, :, :, 63], AF.Exp),
                        reads=rbk, writes=sdk)
                    P.op("pool", lambda e, W1=W1: e.tensor_tensor(W1.rearrange("p (a t) -> p a t", a=2), W1.rearrange("p (a t) -> p a t", a=2),
                                                                  msbd[:].unsqueeze(1).to_broadcast([128, 2, 128]), ALU.mult),
                         reads=CK + W1k, writes=W1k)
                    P.op("pool", lambda e, W2=W2: e.tensor_tensor(W2.rearrange("p (a t) -> p a t", a=2), W2.rearrange("p (a t) -> p a t", a=2),
                                                                  minclt[:].unsqueeze(1).to_broadcast([128, 2, 128]), ALU.mult),
                         reads=CK + W2k, writes=W2k)
                    for hh in range(2 if dbg >= 1.7 else 0):
                        h = 2 * hp + hh
                        up = unit_ctr["u"] % 2
                        unit_ctr["u"] += 1
                        base = 8 + 3 * up
                        names = ["kbg", "kdec", "vb", "sz", "L", "LTs", "AT", "NTa", "NTb", "Pa", "Pb", "PTa"]
                        Tl = {}
                        for n_i, nm in enumerate(names):
                            Tl[nm] = BQ(base + n_i // 4, n_i % 4)
                        extra = ["PTb", "u", "wT", "qg", "vnew", "og"]
                        ebase = (0, 1, 2, 3)
                        for n_i, nm in enumerate(extra):
                            Tl[nm] = BQ(ebase[(n_i + 6 * up) // 4], (n_i + 6 * up) % 4)
                        Dm = W1[:, hh * 128:(hh + 1) * 128]; Dmk = [W1k[hh]]
                        DTm = W2[:, hh * 128:(hh + 1) * 128]; DTmk = [W2k[hh]]
                        egr = W3[:, hh * 128:(hh + 1) * 128]; egk = [W3k[hh]]
                        bt = toks[:, 0, tb, h:h + 1]; bg2 = toks[:, 4, tb, h:h + 1]; kd = toks[:, 5, tb, h:h + 1]
                        qb_, kb_, vb_, zb_ = qf[hh][:, cs], kf[hh][:, cs], vf[hh][:, cs], zf[hh][:, cs]
                        SK = [f"S{h}"]
                        Sh = Sst[:, h, :]
                        pt, ptk = PQ(3 + up, 0, 3)
                        P.op("pe", lambda e, pt=pt, kb_=kb_: e.transpose(pt[:, 0:128], kb_, ident[:]), reads=CK + [kk_[hh]], writes=[ptk[0]])
                        P.op("pe", lambda e, pt=pt, vb_=vb_: e.transpose(pt[:, 128:256], vb_, ident[:]), reads=CK + [vk_[hh]], writes=[ptk[1]])
                        P.op("pe", lambda e, pt=pt, zb_=zb_: e.transpose(pt[:, 256:384], zb_, ident[:]), reads=CK + [zk_[hh]], writes=[ptk[2]])
                        (kbg, kbgk), (kdec, kdeck), (vb, vbk), (sz, szk) = Tl["kbg"], Tl["kdec"], Tl["vb"], Tl["sz"]
                        P.op("dve", lambda e, pt=pt, kbg=kbg, bg2=bg2: e.tensor_scalar(kbg, pt[:, 0:128], bg2, None, ALU.mult),
                             reads=[ptk[0], "toks"], writes=kbgk)
                        P.op("act", lambda e, pt=pt, kdec=kdec, kd=kd: e.activation(kdec, pt[:, 0:128], AF.Copy, scale=kd),
                             reads=[ptk[0], "toks"], writes=kdeck)
                        P.op("dve", lambda e, pt=pt, vb=vb, bt=bt: e.tensor_scalar(vb, pt[:, 128:256], bt, None, ALU.mult),
                             reads=[ptk[1], "toks"], writes=vbk)
                        P.op("act", lambda e, pt=pt, sz=sz: e.activation(sz, pt[:, 256:384], AF.Silu), reads=[ptk[2]], writes=szk)
                        pk2_, pk2k = PQ(5 + up, 0, 2)
                        P.op("pe", lambda e, pk2_=pk2_, kb_=kb_: e.matmul(pk2_[:, 0:128], kb_, kb_, start=True, stop=True),
                             reads=[kk_[hh]], writes=[pk2k[0]])
                        P.op("pe", lambda e, pk2_=pk2_, kb_=kb_, qb_=qb_: e.matmul(pk2_[:, 128:256], kb_, qb_, start=True, stop=True),
                             reads=[kk_[hh], qk_[hh]], writes=[pk2k[1]])
                        (Lt, Lk), (LTs, LTsk), (AT, ATk) = Tl["L"], Tl["LTs"], Tl["AT"]
                        P.op("dve", lambda e, Lt=Lt, pk2_=pk2_, bt=bt, Dm=Dm: e.scalar_tensor_tensor(
                            Lt, pk2_[:, 0:128], bt, Dm, ALU.mult, ALU.mult), reads=[pk2k[0], "toks"] + Dmk, writes=Lk)
                        P.op("dve", lambda e, AT=AT, pk2_=pk2_, DTm=DTm: e.tensor_tensor(AT, pk2_[:, 128:256], DTm, ALU.mult),
                             reads=[pk2k[1]] + DTmk, writes=ATk)
                        if dbg < 1.72:
                            continue
                        import os as _os
                        _v2 = _os.environ.get("VAR2", "")
                        plt, pltk = PQ(5 + up, 2) if _v2 != "bank2" else PQ(2, 2 * up)
                        if _v2 == "mm":
                            P.op("pe", lambda e, plt=plt, Lt=Lt: e.matmul(plt, Lt, ident[:], start=True, stop=True), reads=CK + Lk, writes=pltk)
                        else:
                            P.op("pe", lambda e, plt=plt, Lt=Lt: e.transpose(plt, Lt, ident[:]), reads=CK + Lk, writes=pltk)
                        NT = [Tl["NTa"], Tl["NTb"]]
                        Pm = [Tl["Pa"], Tl["Pb"]]
                        PTm = [Tl["PTa"], Tl["PTb"]]
                        import os as _os
                        _v = _os.environ.get("VAR", "")
                        if _v != "V1" and _v != "V2":
                            P.op("dve", lambda e, plt=plt, nt=NT[0][0]: e.scalar_tensor_tensor(nt, plt, -1.0, ident[:], ALU.mult, ALU.add),
                                 reads=CK + pltk, writes=NT[0][1])
                        if _v != "V1" and _v != "V3":
                            P.op("act", lambda e, plt=plt, LTs=LTs: e.copy(LTs, plt), reads=pltk, writes=LTsk)
                        if dbg < 1.74:
                            continue
                        pq1, pq1k = PQ(5 + up, 3)
                        pq2, pq2k = PQ(3 + up, 3)
                        P.op("pe", lambda e, pq1=pq1, LTs=LTs, Lt=Lt: e.matmul(pq1, LTs, Lt, start=True, stop=True),
                             reads=LTsk + Lk, writes=pq1k)
                        P.op("pe", lambda e, pq2=pq2, LTs=LTs, Lt=Lt: e.matmul(pq2, Lt, LTs, start=True, stop=True),
                             reads=LTsk + Lk, writes=pq2k)
                        P.op("dve", lambda e, pq1=pq1, d=Pm[0][0]: e.tensor_copy(d, pq1), reads=pq1k, writes=Pm[0][1])
                        P.op("act", lambda e, pq2=pq2, d=PTm[0][0]: e.copy(d, pq2), reads=pq2k, writes=PTm[0][1])
                        cur = 0
                        if dbg < 1.76:
                            continue
                        for lev in range(5 if dbg >= 1.8 else 1):
                            Pc, Pck = Pm[lev % 2]
                            PTc, PTck = PTm[lev % 2]
                            ntc, ntck = NT[cur]
                            ntn, ntnk = NT[1 - cur]
                            pu, puk = PQ(5 + up, 2)
                            P.op("pe", lambda e, pu=pu, Pc=Pc, ntc=ntc: e.matmul(pu, Pc, ntc, start=True, stop=True),
                                 reads=Pck + ntck, writes=puk)
                            P.op("dve", lambda e, pu=pu, ntc=ntc, ntn=ntn: e.tensor_tensor(ntn, pu, ntc, ALU.add),
                                 reads=puk + ntck, writes=ntnk)
                            cur = 1 - cur
                            if lev < 4:
                                Pn, Pnk = Pm[(lev + 1) % 2]
                                PTn, PTnk = PTm[(lev + 1) % 2]
                                P.op("pe", lambda e, pq1=pq1, PTc=PTc, Pc=Pc: e.matmul(pq1, PTc, Pc, start=True, stop=True),
                                     reads=PTck + Pck, writes=pq1k)
                                P.op("pe", lambda e, pq2=pq2, PTc=PTc, Pc=Pc: e.matmul(pq2, Pc, PTc, start=True, stop=True),
                                     reads=PTck + Pck, writes=pq2k)
                                P.op("dve", lambda e, pq1=pq1, Pn=Pn: e.tensor_copy(Pn, pq1), reads=pq1k, writes=Pnk)
                                P.op("act", lambda e, pq2=pq2, PTn=PTn: e.copy(PTn, pq2), reads=pq2k, writes=PTnk)
                        TTm, TTk = NT[cur]
                        if dbg < 1.9:
                            continue
                        (u_, uk), (wT, wTk), (qg, qgk), (vnew, vnk), (og, ogk) = Tl["u"], Tl["wT"], Tl["qg"], Tl["vnew"], Tl["og"]
                        P.op("pe", lambda e, pq1=pq1, TTm=TTm, vb=vb: e.matmul(pq1, TTm, vb, start=True, stop=True),
                             reads=TTk + vbk, writes=pq1k)
                        P.op("pe", lambda e, pq2=pq2, TTm=TTm, kbg=kbg: e.matmul(pq2, kbg, TTm, start=True, stop=True),
                             reads=TTk + kbgk, writes=pq2k)
                        P.op("act", lambda e, pq1=pq1, u_=u_: e.copy(u_, pq1), reads=pq1k, writes=uk)
                        P.op("dve", lambda e, pq2=pq2, wT=wT: e.tensor_copy(wT, pq2), reads=pq2k, writes=wTk)
                        P.op("pool", lambda e, qg=qg, qb_=qb_, egr=egr: e.tensor_tensor(qg, qb_, egr, ALU.mult),
                             reads=[qk_[hh]] + egk, writes=qgk)
                        po, pok = PQ(2, 2 * up)
                        pws, pwsk = PQ(2, 2 * up + 1)
                        pds, pdsk = PQ(5 + up, 2)
                        for c in range(2):
                            rs = slice(64 * c, 64 * c + 64)
                            P.op("pe", lambda e, pws=pws, wT=wT, rs=rs, Sh=Sh: e.matmul(pws[rs, :], wT[:, rs], Sh, start=True, stop=True),
                                 reads=wTk + SK, writes=pwsk)
                            P.op("dve", lambda e, vnew=vnew, u_=u_, pws=pws, rs=rs: e.tensor_tensor(vnew[rs, :], u_[rs, :], pws[rs, :], ALU.subtract),
                                 reads=uk + pwsk, writes=vnk)
                            P.op("pe", lambda e, po=po, qg=qg, rs=rs, Sh=Sh: e.matmul(po[rs, :], qg[:, rs], Sh, start=True, stop=False),
                                 reads=qgk + SK, writes=pok)
                            P.op("pe", lambda e, po=po, AT=AT, rs=rs, vnew=vnew: e.matmul(po[rs, :], AT[rs, rs], vnew[rs, :], start=False, stop=True),
                                 reads=ATk + vnk, writes=pok)
                            P.op("pe", lambda e, pds=pds, kdec=kdec, rs=rs, vnew=vnew: e.matmul(pds, kdec[rs, :], vnew[rs, :], start=True, stop=True),
                                 reads=kdeck + vnk, writes=pdsk)
                            P.op("dve", lambda e, Sh=Sh, sdec=sdec, hh=hh, c=c, pds=pds: e.scalar_tensor_tensor(
                                Sh, Sh, sdec[:, 2 * hh + c:2 * hh + c + 1], pds, ALU.mult, ALU.add),
                                reads=SK + sdk + pdsk, writes=SK)
                        if dbg < 1.95:
                            continue
                        ss = small[:, 8 + up:9 + up]
                        ssk = [f"ss{up}"]
                        P.op("dve", lambda e, ss=ss: e.memset(ss, 0.0), writes=ssk)
                        P.op("act", lambda e, og=og, po=po, ss=ss: e.activation(og, po, AF.Square, accum_out=ss), reads=pok + ssk, writes=ogk + ssk)
                        P.op("act", lambda e, ss=ss: e.activation(ss, ss, AF.Sqrt, scale=1.0 / 128.0, bias=NORM_EPS), reads=ssk, writes=ssk)
                        P.op("dve", lambda e, ss=ss: e.reciprocal(ss, ss), reads=ssk, writes=ssk)
                        P.op("dve", lambda e, og=og, po=po, ss=ss: e.scalar_tensor_tensor(og, po, ss, normw[:], ALU.mult, ALU.mult),
                             reads=CK + pok + ssk, writes=ogk)
                        P.op("pool", lambda e, og=og, sz=sz: e.tensor_tensor(og, og, sz, ALU.mult), reads=ogk + szk, writes=ogk)
                        P.op("pe", lambda e, pws=pws, og=og: e.transpose(pws, og, ident[:]), reads=CK + ogk, writes=pwsk)
                        P.op("act", lambda e, pws=pws, h=h, cs=cs: e.copy(bufA[:, h, cs], pws), reads=pwsk, writes=[f"A{h}"])

            pwv = bufB[:].rearrange("p c t -> p (c t)")[:, 0:2048].rearrange("p (g k d) -> p g k d", g=4, k=2)
            if dbg >= 3:
                P.dma("poolw", lambda e: e.dma_start(out=pwv, in_=poolw_d.rearrange("g (k p) d -> p g k d", p=128)),
                      writes=AK("B", range(4)), E="pool")
            for g in range(4 if dbg >= 3 else 0):
                win = 2 ** (g + 1)
                for cc in range(2):
                    pc = 2 * g + cc
                    pb, pk = project(("w_in", 0, 4112 + g * 256 + cc * 128), hT, "H")
                    r = raw[cc]; rk_ = f"raw{cc}"
                    P.op("act", lambda e, pb=pb, r=r: e.copy(r[:, 15:527], pb[:]), reads=pk, writes=[rk_])
                    P.op("pool", lambda e, r=r, pc=pc: e.tensor_copy(r[:, 0:15], phalo[:, pc, :]), reads=["phalo"], writes=[rk_])
                    P.op("pool", lambda e, r=r, pc=pc: e.tensor_copy(phalo[:, pc, :], r[:, 512:527]), reads=[rk_], writes=["phalo"])
                    src, srck = r, rk_
                    for lev in range(g + 1):
                        sh = 2 ** lev
                        lo = 2 * sh - 1
                        d_ = raw[2 + lev % 2]; dk_ = f"raw{2 + lev % 2}"
                        eng = "dve" if (lev + cc) % 2 == 0 else "pool"
                        P.op(eng, lambda e, d_=d_, src=src, lo=lo, sh=sh: e.tensor_tensor(d_[:, lo:527], src[:, lo:527], src[:, lo - sh:527 - sh], ALU.add),
                             reads=[srck], writes=[dk_])
                        src, srck = d_, dk_
                    dst = bufB[:, 8 + pc, :]
                    P.op("dve", lambda e, dst=dst, src=src, win=win, r=r: e.scalar_tensor_tensor(
                        dst, src[:, 15:527], 1.0 / win, r[:, 15:527], ALU.mult, ALU.subtract), reads=[srck, rk_], writes=[f"B{8 + pc}"])
                    if ti == 0:
                        P.op("dve", lambda e, dst=dst, src=src, g=g: e.tensor_tensor(dst[:, 0:16], src[:, 15:31], rc16[:, g, :], ALU.mult),
                             reads=CK + [srck, f"B{8 + pc}"], writes=[f"B{8 + pc}"])
                        P.op("dve", lambda e, dst=dst, r=r: e.tensor_tensor(dst[:, 0:16], f32(dst)[:, 0:16], r[:, 15:31], ALU.subtract),
                             reads=[rk_, f"B{8 + pc}"], writes=[f"B{8 + pc}"])
                for dch in range(2):
                    pb, pk = proj_bank()
                    for cc in range(2):
                        P.op("pe", lambda e, g=g, cc=cc, dch=dch, pb=pb: e.matmul(
                            pb[:], pwv[:, g, cc, dch * 128:(dch + 1) * 128], bufB[:, 8 + 2 * g + cc, :], start=(cc == 0), stop=(cc == 1)),
                            reads=AK("B", range(4)) + [f"B{8 + 2 * g + cc}"], writes=pk)
                    P.op("act", lambda e, g=g, dch=dch, pb=pb: e.activation(bufA[:, 8 + 2 * g + dch, :], pb[:], AF.Copy,
                                                                             scale=pscale[:, 2 * g + dch:2 * g + dch + 1]),
                         reads=CK + pk, writes=[f"A{8 + 2 * g + dch}"])

            for c in range(16 if dbg >= 4 else 0):
                pb, pk = project(("w_out", 0, c * 128), bufA, "A")
                P.op("dve", lambda e, c=c, pb=pb: e.scalar_tensor_tensor(bufB[:, c, :], f32(hT[:, c, :]), ALPHA, pb[:], ALU.mult, ALU.add),
                     reads=pk + [f"H{c}"], writes=[f"B{c}"])
            if dbg >= 4:
                layer_norm(bufB, "B", 0)

            if stop_after >= 2:
                for c in range(16):
                    pb, pk = project(("xq_w", 0, c * 128), hT, "H")
                    if c % 2:
                        P.op("act", lambda e, c=c, pb=pb: e.copy(bufA[:, c, :], pb[:]), reads=pk, writes=[f"A{c}"])
                    else:
                        P.op("dve", lambda e, c=c, pb=pb: e.tensor_copy(bufA[:, c, :], pb[:]), reads=pk, writes=[f"A{c}"])
                sc = 512.0 ** -0.5
                for hd in range(4):
                    PTh, PThk = big[4 + hd % 2], [f"a{4 + hd % 2}_{q}" for q in range(4)]
                    PTh2, PTh2k = big[6 + hd % 2], [f"a{6 + hd % 2}_{q}" for q in range(4)]
                    PTs = [(PTh, PThk), (PTh2, PTh2k)]
                    for qb in range(4):
                        u2 = (hd * 4 + qb) % 2
                        psc, psck = PQ(3 + u2, 0, 2)
                        for dc in range(4):
                            P.op("pe", lambda e, psc=psc, hd=hd, dc=dc, qb=qb: e.matmul(
                                psc, bufA[:, 4 * hd + dc, qb * 128:(qb + 1) * 128], kxT[:, 4 * hd + dc, :], start=(dc == 0), stop=(dc == 3)),
                                reads=[f"A{4 * hd + dc}", "kx"], writes=psck)
                        mx = small[:, 10 + u2:11 + u2]; mxk = [f"mx{u2}"]
                        sm = small[:, 12 + u2:13 + u2]; smk = [f"sm{u2}"]
                        pe_, pek = BQ(8 + u2, 0, 2)
                        P.op("dve", lambda e, mx=mx, psc=psc: e.reduce_max(mx, psc, AX.X), reads=psck, writes=mxk)
                        P.op("dve", lambda e, mx=mx: e.tensor_scalar(mx, mx, -sc, None, ALU.mult), reads=mxk, writes=mxk)
                        P.op("dve", lambda e, sm=sm: e.memset(sm, 0.0), writes=smk)
                        P.op("act", lambda e, pe_=pe_, psc=psc, mx=mx, sm=sm: e.activation(pe_, psc, AF.Exp, bias=mx, scale=sc, accum_out=sm),
                             reads=psck + mxk + smk, writes=pek + smk)
                        P.op("dve", lambda e, sm=sm: e.reciprocal(sm, sm), reads=smk, writes=smk)
                        P.op("pool", lambda e, pe_=pe_, sm=sm: e.tensor_scalar(pe_, pe_, sm, None, ALU.mult), reads=pek + smk, writes=pek)
                        ptt, pttk = PQ(3 + u2, 2, 2)
                        for mb in range(2):
                            P.op("pe", lambda e, ptt=ptt, pe_=pe_, mb=mb: e.transpose(ptt[:, mb * 128:(mb + 1) * 128], pe_[:, mb * 128:(mb + 1) * 128], ident[:]),
                                 reads=CK + pek, writes=pttk)
                        for mb in range(2):
                            if mb:
                                P.op("act", lambda e, ptt=ptt, mb=mb, qb=qb, d=PTs[mb][0]: e.copy(d[:, qb * 128:(qb + 1) * 128], ptt[:, mb * 128:(mb + 1) * 128]),
                                     reads=pttk, writes=[PTs[mb][1][qb]])
                            else:
                                P.op("dve", lambda e, ptt=ptt, mb=mb, qb=qb, d=PTs[mb][0]: e.tensor_copy(d[:, qb * 128:(qb + 1) * 128], ptt[:, mb * 128:(mb + 1) * 128]),
                                     reads=pttk, writes=[PTs[mb][1][qb]])
                    for dhc in range(4):
                        pb, pk = proj_bank()
                        for mb in range(2):
                            P.op("pe", lambda e, pb=pb, mb=mb, hd=hd, dhc=dhc, d=PTs[mb][0]: e.matmul(
                                pb[:], f32(vx[:, mb, hd * 512 + dhc * 128: hd * 512 + (dhc + 1) * 128]), d[:], start=(mb == 0), stop=(mb == 1)),
                                reads=["vx"] + PTs[mb][1], writes=pk)
                        oc = 4 * hd + dhc
                        if dhc % 2:
                            P.op("act", lambda e, oc=oc, pb=pb: e.copy(bufB[:, oc, :], pb[:]), reads=pk, writes=[f"B{oc}"])
                        else:
                            P.op("dve", lambda e, oc=oc, pb=pb: e.tensor_copy(bufB[:, oc, :], pb[:]), reads=pk, writes=[f"B{oc}"])
                for c in range(16):
                    pb, pk = project(("xo_w", 0, c * 128), bufB, "B")
                    P.op("dve", lambda e, c=c, pb=pb: e.scalar_tensor_tensor(bufA[:, c, :], f32(hT[:, c, :]), ALPHA, pb[:], ALU.mult, ALU.add),
                         reads=pk + [f"H{c}"], writes=[f"A{c}"])
                layer_norm(bufA, "A", 1)

            if stop_after >= 3:
                for g in range(4):
                    for j in range(16):
                        pb, pk = project(("w_up", 0, g * 2048 + j * 128), hT, "H")
                        rl, rlk = big[8 + j % 4], [f"a{8 + j % 4}_{q}" for q in range(4)]
                        P.op("act", lambda e, pb=pb, rl=rl: e.activation(rl[:], pb[:], AF.Relu), reads=pk, writes=rlk)
                        if j % 2:
                            P.op("act", lambda e, j=j, rl=rl: e.activation(bufB[:, j, :], rl[:], AF.Square), reads=rlk, writes=[f"B{j}"])
                        else:
                            P.op("dve", lambda e, j=j, rl=rl: e.tensor_tensor(bufB[:, j, :], rl[:], rl[:], ALU.mult), reads=rlk, writes=[f"B{j}"])
                    for j in range(16):
                        pb, pk = project(("w_down", g * 2048, j * 128), bufB, "B")
                        if g == 0:
                            P.op("dve", lambda e, j=j, pb=pb: e.scalar_tensor_tensor(bufA[:, j, :], f32(hT[:, j, :]), ALPHA, pb[:], ALU.mult, ALU.add),
                                 reads=pk + [f"H{j}"], writes=[f"A{j}"])
                        elif g < 3:
                            P.op("dve", lambda e, j=j, pb=pb: e.tensor_tensor(bufA[:, j, :], pb[:], f32(bufA[:, j, :]), ALU.add),
                                 reads=pk + [f"A{j}"], writes=[f"A{j}"])
                        else:
                            P.op("dve", lambda e, j=j, pb=pb: e.tensor_tensor(bufA[:, j, :], pb[:], f32(bufA[:, j, :]), ALU.add),
                                 reads=pk + [f"A{j}"], writes=[f"A{j}"])
                layer_norm(bufA, "A", 2)

            for tb in range(4):
                obig = (8, 9, 10, 11) if tb % 2 == 0 else (12, 13, 0, 1)
                for dq in range(4):
                    pb, pk = proj_bank()
                    for d4 in range(4):
                        dc = dq * 4 + d4
                        P.op("pe", lambda e, pb=pb, d4=d4, dc=dc, tb=tb: e.transpose(pb[:, d4 * 128:(d4 + 1) * 128], f32(hT[:, dc, tb * 128:(tb + 1) * 128]), ident[:]),
                             reads=CK + [f"H{dc}"], writes=pk)
                    dsto = big[obig[dq]]
                    dstk = [f"a{obig[dq]}_{q}" for q in range(4)]
                    if dq % 2:
                        P.op("act", lambda e, pb=pb, dsto=dsto: e.copy(dsto[:], pb[:]), reads=pk, writes=dstk)
                    else:
                        P.op("dve", lambda e, pb=pb, dsto=dsto: e.tensor_copy(dsto[:], pb[:]), reads=pk, writes=dstk)
                    t = P.dma(f"out{tb % 2}_{dq}", lambda e, tb=tb, dq=dq, dsto=dsto, t0=t0: e.dma_start(
                        out=out_d[t0 + tb * 128:t0 + (tb + 1) * 128, dq * 512:(dq + 1) * 512], in_=dsto[:]), reads=dstk)
                    out_toks.append(t)
        P.wait_tokens("sp", out_toks)
        P.emit()
    return nc


def host_inputs(inp, b, ntiles=4):
    T = ntiles * TT
    f = lambda a: np.ascontiguousarray(a, dtype=np.float32)
    m = {"x": f(inp["x"][b, :T]), "mem": f(inp["mem"][b])}
    for k in ("w_in", "w_out", "xq_w", "xk_w", "xv_w", "xo_w", "w_up", "w_down", "pool_w"):
        m[k] = f(inp[k][0])
    m["cw"] = f(np.asarray(inp["conv_w"][0]).T.reshape(24, 128, 4).transpose(1, 0, 2))
    m["alog"] = f(np.asarray(inp["a_log"][0]).reshape(8, 1))
    m["dtb"] = f(np.asarray(inp["dt_bias"][0]).reshape(8, 1))
    m["normw"] = f(np.broadcast_to(np.asarray(inp["gdn_norm_w"][0])[None, :], (128, 128)))
    m["pscale"] = f(np.asarray(inp["pool_scale"][0]).reshape(8, 128).T)
    lnp = np.stack([np.asarray(inp[k][0]).reshape(16, 128).T for k in ("ln1_g", "ln1_b", "ln2_g", "ln2_b", "ln3_g", "ln3_b")], axis=1)
    m["lnp"] = f(lnp)
    m.update(host_consts())
    return m


_NC_CACHE = {}


def kernel(**inputs):
    inp = {k: np.asarray(v) for k, v in inputs.items()}
    if "nc" not in _NC_CACHE:
        _NC_CACHE["nc"] = build(4, 3)
    nc = _NC_CACHE["nc"]
    in_maps = [host_inputs(inp, b) for b in range(8)]
    res = run_bass_kernel_spmd(nc, in_maps, core_ids=list(range(8)))
    out = np.stack([np.asarray(res.results[b]["out"]) for b in range(8)], axis=0)
    return out.astype(np.float32)
```

```python
import numpy as np
from contextlib import ExitStack
import concourse.bass as bass
import concourse.mybir as mybir
from concourse.bass_utils import run_bass_kernel_spmd

F32 = mybir.dt.float32
F32R = mybir.dt.float32r
AF = mybir.ActivationFunctionType
ALU = mybir.AluOpType
AX = mybir.AxisListType

EPOCH = 8000
D = 2048
SEQ = 2048
TT = 512
NDC = 16
MEM = 256
DFF = 8192
INC = 5136
ALPHA = 2.0 ** 0.25
LN_EPS = 1e-5
NORM_EPS = 1e-6
NSLOT = 3
NBIG = 14


class Prog:
    COMPUTE = ("pe", "act", "dve", "pool")

    def __init__(self, nc, stack):
        self.nc = nc
        self.stack = stack
        self.ops = {e: [] for e in ("pe", "act", "dve", "pool", "sp")}
        self.count = {e: 0 for e in self.COMPUTE}
        self.sems = {}
        self.known = {e: {} for e in self.ops}
        self.last_write = {}
        self.readers = {}
        self.dma_count = {}

    def _sem(self, name):
        if name not in self.sems:
            self.sems[name] = self.stack.enter_context(self.nc.semaphore(name))
        return self.sems[name]

    def _phys(self, cname, v):
        if cname in self.COMPUTE:
            return self._sem(f"s_{cname}_{(v - 1) // EPOCH}"), (v - 1) % EPOCH + 1
        return self._sem(cname), v

    def _collect(self, E, reads, writes):
        deps = []
        for k in reads:
            t = self.last_write.get(k)
            if t is not None:
                deps.append(t)
        for k in writes:
            t = self.last_write.get(k)
            if t is not None:
                deps.append(t)
            deps.extend(self.readers.get(k, ()))
        deps.sort(key=lambda t: -t[1])
        kn = self.known[E]
        waits = {}
        for (cname, v, snap) in deps:
            if cname == E and E == "pe":
                continue
            if kn.get(cname, 0) >= v:
                continue
            waits[cname] = max(waits.get(cname, 0), v)
            kn[cname] = v
            for kk, vv in snap.items():
                if kn.get(kk, 0) < vv:
                    kn[kk] = vv
        return [self._phys(c, v) for c, v in waits.items()]

    @staticmethod
    def _excl(reads, writes):
        pr = [k for k in reads if k[0] == "p" and k[1:].isdigit()]
        if pr:
            reads = [k for k in reads if k not in pr]
            writes = list(writes) + pr
        return reads, writes

    def op(self, E, fn, reads=(), writes=()):
        reads, writes = self._excl(reads, writes)
        waits = self._collect(E, reads, writes)
        idx = self.count[E]
        self.count[E] += 1
        snap = dict(self.known[E])
        snap[E] = idx + 1
        tok = (E, idx + 1, snap)
        sem, _ = self._phys(E, idx + 1)
        self.ops[E].append((waits, fn, sem, 1))
        self._record(tok, reads, writes)
        return tok

    def dma(self, slot, fn, reads=(), writes=(), E="sp"):
        waits = self._collect(E, reads, writes)
        cname = "dma_" + E + "_" + slot
        self.dma_count[cname] = self.dma_count.get(cname, 0) + 16
        tok = (cname, self.dma_count[cname], dict(self.known[E]))
        self.ops[E].append((waits, fn, self._sem(cname), 16))
        self._record(tok, reads, writes)
        return tok

    def _record(self, tok, reads, writes):
        for k in reads:
            self.readers.setdefault(k, []).append(tok)
        for k in writes:
            self.last_write[k] = tok
            self.readers[k] = []

    def wait_tokens(self, E, toks):
        kn = self.known[E]
        waits = []
        for (cname, v, snap) in toks:
            if kn.get(cname, 0) >= v:
                continue
            kn[cname] = v
            waits.append(self._phys(cname, v))
        self.ops[E].append((waits, None, None, 0))

    def emit(self):
        nc = self.nc
        engs = {"pe": "tensor", "act": "scalar", "dve": "vector", "pool": "gpsimd", "sp": "sync"}
        with nc.Block() as block:
            for E, attr in engs.items():
                ops = self.ops[E]

                def body(eng, ops=ops):
                    for (waits, fn, sem, n) in ops:
                        for (s, v) in waits:
                            eng.wait_ge(s, v)
                        if fn is not None:
                            ins = fn(eng)
                            ins.then_inc(sem, n)

                getattr(block, attr)(body)


def host_consts():
    i = np.arange(128)
    same = (i[:, None] // 64) == (i[None, :] // 64)
    c = {}
    c["ident"] = np.eye(128, dtype=np.float32)
    c["ones"] = np.ones((128, 128), np.float32)
    c["avg"] = np.full((128, 128), 1.0 / D, np.float32)
    c["tribd"] = (same & (i[:, None] <= i[None, :])).astype(np.float32)
    c["bd"] = same.astype(np.float32)
    c["msbd"] = (same & (i[:, None] > i[None, :])).astype(np.float32)
    c["minclt"] = (same & (i[:, None] <= i[None, :])).astype(np.float32) * np.float32(128.0 ** -0.5)
    rc = np.zeros((128, 4, 16), np.float32)
    for g, win in enumerate((2, 4, 8, 16)):
        rc[:, g, :] = 1.0 / np.minimum(np.arange(16) + 1, win).astype(np.float32)
    c["rc16"] = rc
    return c


def tile_slabs(stop_after=3):
    L = []
    for hp in range(4):
        for part in range(4):
            for hh in range(2):
                L.append(("w_in", 0, part * 1024 + hp * 256 + hh * 128))
    for g in range(4):
        for cc in range(2):
            L.append(("w_in", 0, 4112 + g * 256 + cc * 128))
    for c in range(16):
        L.append(("w_out", 0, c * 128))
    if stop_after >= 2:
        for c in range(16):
            L.append(("xq_w", 0, c * 128))
        for c in range(16):
            L.append(("xo_w", 0, c * 128))
    if stop_after >= 3:
        for g in range(4):
            for j in range(16):
                L.append(("w_up", 0, g * 2048 + j * 128))
            for j in range(16):
                L.append(("w_down", g * 2048, j * 128))
    return L


def build(ntiles=4, stop_after=3, dbg=9):
    nc = bass.Bass("TRN2", target_bir_lowering=False)
    T = ntiles * TT

    def din(name, shape):
        return nc.dram_tensor(name, list(shape), F32, kind="ExternalInput").ap()

    x_d = din("x", [T, D])
    mem_d = din("mem", [MEM, D])
    W = {"w_in": din("w_in", [D, INC]), "w_out": din("w_out", [D, D]), "xq_w": din("xq_w", [D, D]),
         "xk_w": din("xk_w", [D, D]), "xv_w": din("xv_w", [D, D]), "xo_w": din("xo_w", [D, D]),
         "w_up": din("w_up", [D, DFF]), "w_down": din("w_down", [DFF, D])}
    poolw_d = din("pool_w", [4, 256, 256])
    cw_d = din("cw", [128, 24, 4])
    alog_d = din("alog", [8, 1])
    dtb_d = din("dtb", [8, 1])
    normw_d = din("normw", [128, 128])
    pscale_d = din("pscale", [128, 8])
    lnp_d = din("lnp", [128, 6, 16])
    cst_d = {k: din(k, v.shape) for k, v in host_consts().items()}
    out_d = nc.dram_tensor("out", [T, D], F32, kind="ExternalOutput").ap()

    with ExitStack() as st:
        P = Prog(nc, st)

        def sb(name, shape, dt=F32):
            return st.enter_context(nc.sbuf_tensor(name, list(shape), dt))

        wsl = [sb(f"wsl{i}", [128, 16, 128], F32R) for i in range(NSLOT)]
        hT = sb("hT", [128, 16, TT], F32R)
        bufA = sb("bufA", [128, 16, TT], F32R)
        bufB = sb("bufB", [128, 16, TT], F32R)
        kxT = sb("kxT", [128, 16, MEM], F32R)
        vx = sb("vx", [128, 2, D], F32R)
        big = [sb(f"big{i}", [128, 512], F32) for i in range(NBIG)]
        raw = [sb(f"raw{i}", [128, 528], F32) for i in range(4)]
        lnsq = [sb(f"lnsq{i}", [128, 512], F32R) for i in range(2)]
        Sst = sb("Sst", [128, 8, 128], F32)
        halo = sb("halo", [128, 24, 3], F32)
        phalo = sb("phalo", [128, 8, 15], F32)
        wba = sb("wba", [128, 16, 16], F32R)
        cw = sb("cw_s", [128, 24, 4])
        alog = sb("alog_s", [8, 1]); dtb = sb("dtb_s", [8, 1]); nega = sb("nega", [8, 1])
        normw = sb("normw_s", [128, 128])
        pscale = sb("pscale_s", [128, 8])
        lnp = sb("lnp_s", [128, 6, 16])
        ident = sb("ident_s", [128, 128]); ones = sb("ones_s", [128, 128])
        avg = sb("avg_s", [128, 128], F32R)
        bdm = sb("bd_s", [128, 128]); tribd = sb("tribd_s", [128, 128]); msbd = sb("msbd_s", [128, 128]); minclt = sb("minclt_s", [128, 128])
        rc16 = sb("rc16_s", [128, 4, 16])
        toks = sb("toks", [128, 6, 4, 8])
        small = sb("small", [128, 16])
        ps = [st.enter_context(nc.psum_tensor(f"ps{i}", [128, 512], F32)) for i in range(8)]

        def f32(ap):
            return ap.bitcast(F32)

        def AK(pref, cs):
            return [f"{pref}{c}" for c in cs]

        def BQ(i, j, n=1):
            return big[i][:, j * 128:(j + n) * 128], [f"a{i}_{q}" for q in range(j, j + n)]

        def PQ(b, j, n=1):
            return ps[b][:, j * 128:(j + n) * 128], [f"p{b}"] * n

        CK = ["const"]
        for (dst, src) in ((cw, cw_d), (alog, alog_d), (dtb, dtb_d), (normw, normw_d), (pscale, pscale_d),
                           (lnp, lnp_d), (ident, cst_d["ident"]), (ones, cst_d["ones"]), (tribd, cst_d["tribd"]), (bdm, cst_d["bd"]),
                           (msbd, cst_d["msbd"]), (minclt, cst_d["minclt"]), (rc16, cst_d["rc16"])):
            P.dma("const", lambda e, dst=dst, src=src: e.dma_start(out=dst[:], in_=src), writes=CK)
        P.dma("const", lambda e: e.dma_start(out=avg[:], in_=cst_d["avg"]), writes=CK, E="pool")
        P.dma("const", lambda e: e.dma_start(
            out=wba[:], in_=W["w_in"][:, 4096:4112].rearrange("(kc p) c -> p kc c", p=128)), writes=CK, E="pool")
        P.op("dve", lambda e: e.memset(Sst[:], 0.0), writes=AK("S", range(8)))
        P.op("dve", lambda e: e.memset(halo[:], 0.0), writes=["halo"])
        P.op("dve", lambda e: e.memset(phalo[:], 0.0), writes=["phalo"])
        P.op("act", lambda e: e.activation(nega[:], alog[:], AF.Exp), reads=CK, writes=["nega"])
        P.op("dve", lambda e: e.tensor_scalar(nega[:], nega[:], -1.0, None, ALU.mult), reads=["nega"], writes=["nega"])

        sched = []
        ws_state = {"issued": 0, "pos": 0}

        def slab_src(name, r0, c0):
            return W[name][r0:r0 + 2048, c0:c0 + 128].rearrange("(kc p) c -> p kc c", p=128)

        def ws_issue_upto(n):
            while ws_state["issued"] < min(n, len(sched)):
                j = ws_state["issued"]
                s = j % NSLOT
                src = slab_src(*sched[j])
                P.dma(f"w{s}", lambda e, s=s, src=src: e.dma_start(out=wsl[s][:], in_=src), writes=[f"w{s}"], E="pool")
                ws_state["issued"] += 1

        def ws_use(expect):
            i = ws_state["pos"]
            assert sched[i] == expect, (i, sched[i], expect)
            ws_issue_upto(i + NSLOT)
            ws_state["pos"] += 1
            return wsl[i % NSLOT], f"w{i % NSLOT}"

        if stop_after >= 2:
            for c in range(16):
                sched.append(("xk_w", 0, c * 128))
            for c in range(16):
                sched.append(("xv_w", 0, c * 128))
        for _ in range(ntiles):
            sched.extend(tile_slabs(stop_after)[:{0: 0, 1: 0, 2: 32, 3: 40}.get(dbg, 32 if dbg < 2 else 10 ** 6)])

        pj = {"i": 0}

        def proj_bank():
            b = pj["i"] % 2
            pj["i"] += 1
            return ps[b], [f"p{b}"]

        def project(expect, act_tile, act_pref, n=TT):
            wt, wk = ws_use(expect)
            pb, pk = proj_bank()
            for kc in range(16):
                P.op("pe", lambda e, kc=kc, wt=wt, pb=pb: e.matmul(pb[:, 0:n], wt[:, kc, :], act_tile[:, kc, 0:n],
                                                                  start=(kc == 0), stop=(kc == 15)),
                     reads=[wk, f"{act_pref}{kc}"], writes=pk)
            return pb, pk

        def layer_norm(y, ypref, li):
            mean_s, mk = big[0], [f"a0_{q}" for q in range(4)]
            rstd_s, rk = big[1], [f"a1_{q}" for q in range(4)]
            pb, pk = proj_bank()
            for dc in range(16):
                P.op("pe", lambda e, dc=dc: e.matmul(pb[:], avg[:], y[:, dc, :], start=(dc == 0), stop=(dc == 15)),
                     reads=CK + [f"{ypref}{dc}"], writes=pk)
            P.op("act", lambda e: e.copy(mean_s[:], pb[:]), reads=pk, writes=mk)
            pb2, pk2 = proj_bank()
            for dc in range(16):
                P.op("dve", lambda e, dc=dc: e.tensor_tensor(y[:, dc, :], f32(y[:, dc, :]), mean_s[:], ALU.subtract),
                     reads=mk + [f"{ypref}{dc}"], writes=[f"{ypref}{dc}"])
                sq_t = lnsq[dc % 2]
                sqk = [f"lnsq{dc % 2}"]
                P.op("act", lambda e, dc=dc, sq_t=sq_t: e.activation(sq_t[:], f32(y[:, dc, :]), AF.Square),
                     reads=[f"{ypref}{dc}"], writes=sqk)
                P.op("pe", lambda e, dc=dc, sq_t=sq_t: e.matmul(pb2[:], avg[:], sq_t[:],
                                                              start=(dc == 0), stop=(dc == 15)),
                     reads=CK + sqk, writes=pk2)
            P.op("act", lambda e: e.activation(rstd_s[:], pb2[:], AF.Sqrt, bias=LN_EPS), reads=pk2, writes=rk)
            P.op("dve", lambda e: e.reciprocal(rstd_s[:], rstd_s[:]), reads=rk, writes=rk)
            for dc in range(16):
                eng = "dve" if dc % 2 == 0 else "pool"
                P.op(eng, lambda e, dc=dc: e.tensor_tensor(y[:, dc, :], f32(y[:, dc, :]), rstd_s[:], ALU.mult),
                     reads=rk + [f"{ypref}{dc}"], writes=[f"{ypref}{dc}"])
                P.op("act", lambda e, dc=dc: e.activation(hT[:, dc, :], f32(y[:, dc, :]), AF.Identity,
                                                          scale=lnp[:, 2 * li, dc:dc + 1], bias=lnp[:, 2 * li + 1, dc:dc + 1]),
                     reads=CK + [f"{ypref}{dc}"], writes=[f"H{dc}"])

        PRO = stop_after >= 2
        memtok = bufB[:].rearrange("p c t -> p (c t)")
        for mb in range(2 if PRO else 0):
            P.dma(f"xin{mb}", lambda e, mb=mb: e.dma_start(out=memtok[:, mb * D:(mb + 1) * D],
                                                      in_=mem_d[mb * 128:(mb + 1) * 128, :]),
                  writes=AK("B", range(4 * mb, 4 * mb + 4)), E="pool")
        memT = bufA[:].rearrange("p c t -> p (c t)")
        for dc in range(16 if PRO else 0):
            tb_, tk_ = PQ(3 + dc % 2, 0, 2)
            for mb in range(2):
                P.op("pe", lambda e, dc=dc, mb=mb, tb_=tb_: e.transpose(
                    tb_[:, mb * 128:(mb + 1) * 128], f32(memtok[:, mb * D + dc * 128: mb * D + (dc + 1) * 128]), ident[:]),
                    reads=CK + AK("B", [4 * mb + dc // 4]), writes=tk_)
            P.op("act" if dc % 2 else "dve", (lambda e, dc=dc, tb_=tb_: e.copy(memT[:, dc * 256:(dc + 1) * 256], tb_))
                 if dc % 2 else (lambda e, dc=dc, tb_=tb_: e.tensor_copy(memT[:, dc * 256:(dc + 1) * 256], tb_)),
                 reads=tk_, writes=AK("A", [dc // 2]))
        for c in range(16 if PRO else 0):
            wt, wk = ws_use(("xk_w", 0, c * 128))
            pb, pk = proj_bank()
            for kc in range(16):
                P.op("pe", lambda e, kc=kc, wt=wt, pb=pb: e.matmul(pb[:, 0:256], wt[:, kc, :], memT[:, kc * 256:(kc + 1) * 256],
                                                                  start=(kc == 0), stop=(kc == 15)),
                     reads=[wk] + AK("A", [kc // 2]), writes=pk)
            P.op("act", lambda e, c=c, pb=pb: e.copy(kxT[:, c, :], pb[:, 0:256]), reads=pk, writes=["kx"])
        for c in range(16 if PRO else 0):
            wt, wk = ws_use(("xv_w", 0, c * 128))
            pb, pk = proj_bank()
            for mb in range(2):
                for kc in range(16):
                    P.op("pe", lambda e, kc=kc, mb=mb, wt=wt, pb=pb: e.matmul(
                        pb[:, mb * 128:(mb + 1) * 128], memT[:, kc * 256 + mb * 128: kc * 256 + (mb + 1) * 128], wt[:, kc, :],
                        start=(kc == 0), stop=(kc == 15)),
                        reads=[wk] + AK("A", [kc // 2]), writes=pk)
            P.op("dve", lambda e, c=c, pb=pb: e.tensor_copy(
                vx[:, :, c * 128:(c + 1) * 128], pb[:, 0:256].rearrange("p (m c) -> p m c", m=2)), reads=pk, writes=["vx"])

        out_toks = []
        unit_ctr = {"u": 0}

        for ti in range(ntiles):
            t0 = ti * TT
            xtok = bufB[:].rearrange("p c t -> p (c t)")
            for tb in range(4):
                P.dma(f"xin{tb}", lambda e, tb=tb, t0=t0: e.dma_start(out=xtok[:, tb * D:(tb + 1) * D],
                                                          in_=x_d[t0 + tb * 128: t0 + (tb + 1) * 128, :]),
                      writes=AK("B", range(4 * tb, 4 * tb + 4)), E="pool")
            for dc in range(16):
                pb, pk = proj_bank()
                for tb in range(4):
                    P.op("pe", lambda e, dc=dc, tb=tb, pb=pb: e.transpose(
                        pb[:, tb * 128:(tb + 1) * 128], f32(xtok[:, tb * D + dc * 128: tb * D + (dc + 1) * 128]), ident[:]),
                        reads=CK + AK("B", [4 * tb + dc // 4]), writes=pk)
                if dc % 2:
                    P.op("act", lambda e, dc=dc, pb=pb: e.copy(hT[:, dc, :], pb[:]), reads=pk, writes=[f"H{dc}"])
                else:
                    P.op("dve", lambda e, dc=dc, pb=pb: e.tensor_copy(hT[:, dc, :], pb[:]), reads=pk, writes=[f"H{dc}"])

            bfm, bfk = big[12], [f"a12_{q}" for q in range(4)]
            gfm, gfk = big[13], [f"a13_{q}" for q in range(4)]
            for part, (dst, dk_) in enumerate(((bfm, bfk), (gfm, gfk)) if dbg >= 1 else ()):
                pb, pk = proj_bank()
                for kc in range(16):
                    P.op("pe", lambda e, kc=kc, part=part, pb=pb: e.matmul(pb[0:8, :], wba[:, kc, part * 8:(part + 1) * 8],
                                                                          hT[:, kc, :], start=(kc == 0), stop=(kc == 15)),
                         reads=CK + [f"H{kc}"], writes=pk)
                if part == 0:
                    P.op("act", lambda e, pb=pb, dst=dst: e.activation(dst[0:8, :], pb[0:8, :], AF.Sigmoid), reads=pk, writes=dk_)
                else:
                    P.op("act", lambda e, pb=pb, dst=dst: e.activation(dst[0:8, :], pb[0:8, :], AF.Exp, bias=dtb[:, 0:1]),
                         reads=pk + CK, writes=dk_)
                    P.op("act", lambda e, dst=dst: e.activation(dst[0:8, :], dst[0:8, :], AF.Ln, bias=1.0), reads=dk_, writes=dk_)
                    P.op("dve", lambda e, dst=dst: e.tensor_scalar(dst[0:8, :], dst[0:8, :], nega[:, 0:1], None, ALU.mult),
                         reads=dk_ + ["nega"], writes=dk_)
            for tb in range(4 if dbg >= 1 else 0):
                tq, tqk = PQ(3 + tb % 2, 0)
                P.op("pe", lambda e, tb=tb, tq=tq: e.transpose(tq[:, 0:8], bfm[0:8, tb * 128:(tb + 1) * 128], ident[0:8, 0:8]),
                     reads=CK + bfk, writes=tqk)
                P.op("pe", lambda e, tb=tb, tq=tq: e.transpose(tq[:, 8:16], gfm[0:8, tb * 128:(tb + 1) * 128], ident[0:8, 0:8]),
                     reads=CK + gfk, writes=tqk)
                P.op("dve", lambda e, tb=tb, tq=tq: e.tensor_copy(toks[:, 0:2, tb, :], tq[:, 0:16].rearrange("p (a h) -> p a h", a=2)),
                     reads=tqk, writes=["toks"])
                tq2, tqk2 = PQ(3 + tb % 2, 1)
                P.op("pe", lambda e, tb=tb, tq2=tq2: e.matmul(tq2[:, 0:8], tribd[:], toks[:, 1, tb, :], start=True, stop=True),
                     reads=CK + ["toks"], writes=tqk2)
                P.op("pe", lambda e, tb=tb, tq2=tq2: e.matmul(tq2[:, 8:16], bdm[:], toks[:, 1, tb, :], start=True, stop=True),
                     reads=CK + ["toks"], writes=tqk2)
                P.op("dve", lambda e, tb=tb, tq2=tq2: e.tensor_copy(toks[:, 2:4, tb, :], tq2[:, 0:16].rearrange("p (a h) -> p a h", a=2)),
                     reads=tqk2, writes=["toks"])
            P.op("act", lambda e: e.activation(toks[:, 4, :, :], toks[:, 2, :, :], AF.Exp), reads=["toks"], writes=["toks"])
            P.op("dve", lambda e: e.tensor_tensor(toks[:, 4, :, :], toks[:, 4, :, :], toks[:, 0, :, :], ALU.mult),
                 reads=["toks"], writes=["toks"])
            P.op("dve", lambda e: e.tensor_tensor(toks[:, 5, :, :], toks[:, 3, :, :], toks[:, 2, :, :], ALU.subtract),
                 reads=["toks"], writes=["toks"])
            P.op("act", lambda e: e.activation(toks[:, 5, :, :], toks[:, 5, :, :], AF.Exp), reads=["toks"], writes=["toks"])

            fin = lambda part, hh: (bufB[:, part * 2 + hh, :], f"B{part * 2 + hh}")
            for hp in range(4 if dbg >= 1.3 else 0):
                for part in range(4):
                    for hh in range(2):
                        h = 2 * hp + hh
                        pb, pk = project(("w_in", 0, part * 1024 + hp * 256 + hh * 128), hT, "H")
                        dstw, dkey = fin(part, hh)
                        dst = f32(dstw)
                        if part == 3:
                            P.op("act", lambda e, pb=pb, dstw=dstw: e.copy(dstw, pb[:]), reads=pk, writes=[dkey])
                            continue
                        ch = part * 8 + h
                        r = raw[(part * 2 + hh) % 2]
                        rk_ = f"raw{(part * 2 + hh) % 2}"
                        P.op("act", lambda e, pb=pb, r=r: e.copy(r[:, 3:515], pb[:]), reads=pk, writes=[rk_])
                        P.op("pool", lambda e, r=r, ch=ch: e.tensor_copy(r[:, 0:3], halo[:, ch, :]), reads=["halo"], writes=[rk_])
                        P.op("pool", lambda e, r=r, ch=ch: e.tensor_copy(halo[:, ch, :], r[:, 512:515]), reads=[rk_], writes=["halo"])
                        eng = "dve" if hh == 0 else "pool"
                        P.op(eng, lambda e, r=r, ch=ch, dstw=dstw: e.tensor_scalar(dstw, r[:, 0:512], cw[:, ch, 0:1], None, ALU.mult),
                             reads=CK + [rk_], writes=[dkey])
                        for j in range(1, 4):
                            P.op("dve", lambda e, r=r, ch=ch, dst=dst, dstw=dstw, j=j: e.scalar_tensor_tensor(
                                dstw, r[:, j:j + 512], cw[:, ch, j:j + 1], dst, ALU.mult, ALU.add),
                                reads=CK + [rk_, dkey], writes=[dkey])
                        P.op("act", lambda e, dst=dst, dstw=dstw: e.activation(dstw, dst, AF.Silu), reads=[dkey], writes=[dkey])
                        if part < 2:
                            sqb, sqk = big[2 + hh], [f"a{2 + hh}_{q}" for q in range(4)]
                            P.op("act", lambda e, dst=dst, sqb=sqb: e.activation(sqb[:], dst, AF.Square), reads=[dkey], writes=sqk)
                            pb2, pk2 = proj_bank()
                            P.op("pe", lambda e, pb2=pb2, sqb=sqb: e.matmul(pb2[:], ones[:], sqb[:], start=True, stop=True),
                                 reads=CK + sqk, writes=pk2)
                            P.op("act", lambda e, pb2=pb2, sqb=sqb: e.activation(sqb[:], pb2[:], AF.Sqrt, bias=NORM_EPS),
                                 reads=pk2, writes=sqk)
                            P.op("dve", lambda e, sqb=sqb: e.reciprocal(sqb[:], sqb[:]), reads=sqk, writes=sqk)
                            P.op(eng, lambda e, dst=dst, dstw=dstw, sqb=sqb: e.tensor_tensor(dstw, dst, sqb[:], ALU.mult),
                                 reads=sqk + [dkey], writes=[dkey])
                qf = [f32(fin(0, hh)[0]) for hh in range(2)]
                kf = [f32(fin(1, hh)[0]) for hh in range(2)]
                vf = [f32(fin(2, hh)[0]) for hh in range(2)]
                zf = [f32(fin(3, hh)[0]) for hh in range(2)]
                qk_ = [fin(0, hh)[1] for hh in range(2)]
                kk_ = [fin(1, hh)[1] for hh in range(2)]
                vk_ = [fin(2, hh)[1] for hh in range(2)]
                zk_ = [fin(3, hh)[1] for hh in range(2)]
                for tb in range(4 if dbg >= 1.6 else 0):
                    wp = tb % 2
                    cs = slice(tb * 128, (tb + 1) * 128)
                    Rw, Rk = BQ(4 + 2 * wp, 0, 2)
                    W1, W1k = BQ(4 + 2 * wp, 2, 2)
                    W2, W2k = BQ(5 + 2 * wp, 0, 2)
                    W3, W3k = BQ(5 + 2 * wp, 2, 2)
                    sdec = small[:, 4 * wp:4 * wp + 4]
                    sdk = [f"sdec{wp}"]
                    P.op("pool", lambda e, Rw=Rw, tb=tb, hp=hp: e.tensor_tensor(
                        Rw.rearrange("p (a t) -> p a t", a=2), tribd[:].unsqueeze(1).to_broadcast([128, 2, 128]),
                        toks[:, 1, tb, 2 * hp:2 * hp + 2].unsqueeze(2).to_broadcast([128, 2, 128]), ALU.mult),
                        reads=CK + ["toks"], writes=Rk)
                    rb, rbk = PQ(7, 2 * wp, 2)
                    P.op("pe", lambda e, rb=rb, Rw=Rw: e.matmul(rb, ones[:], Rw, start=True, stop=True), reads=CK + Rk, writes=rbk)
                    for hh in range(2):
                        h = 2 * hp + hh
                        P.op("dve", lambda e, hh=hh, h=h, tb=tb, rb=rb, W1=W1: e.tensor_scalar(
                            W1[:, hh * 128:(hh + 1) * 128], rb[:, hh * 128:(hh + 1) * 128], toks[:, 2, tb, h:h + 1], 0.0,
                            ALU.subtract, ALU.max), reads=rbk + ["toks"], writes=W1k)
                        P.op("dve", lambda e, hh=hh, h=h, tb=tb, rb=rb, W2=W2: e.tensor_scalar(
                            W2[:, hh * 128:(hh + 1) * 128], rb[:, hh * 128:(hh + 1) * 128], toks[:, 2, tb, h:h + 1], 0.0,
                            ALU.subtract, ALU.min), reads=rbk + ["toks"], writes=W2k)
                    P.op("act", lambda e, W1=W1: e.activation(W1, W1, AF.Exp, scale=-1.0), reads=W1k, writes=W1k)
                    P.op("act", lambda e, W2=W2: e.activation(W2, W2, AF.Exp), reads=W2k, writes=W2k)
                    P.op("act", lambda e, W3=W3, rb=rb: e.activation(W3, rb, AF.Exp, bias=float(-0.5 * np.log(128.0))), reads=rbk, writes=W3k)
                    P.op("act", lambda e, rb=rb, sdec=sdec: e.activation(
                        sdec.rearrange("p (a c) -> p a c", a=2), rb.rearrange("p (a c t) -> p a c t", a=2, c=2)[:, :, :, 63], AF.Exp),
                        reads=rbk, writes=sdk)
                    P.op("pool", lambda e, W1=W1: e.tensor_tensor(W1.rearrange("p (a t) -> p a t", a=2), W1.rearrange("p (a t) -> p a t", a=2),
                                                                  msbd[:].unsqueeze(1).to_broadcast([128, 2, 128]), ALU.mult),
                         reads=CK + W1k, writes=W1k)
                    P.op("pool", lambda e, W2=W2: e.tensor_tensor(W2.rearrange("p (a t) -> p a t", a=2), W2.rearrange("p (a t) -> p a t", a=2),
                                                                  minclt[:].unsqueeze(1).to_broadcast([128, 2, 128]), ALU.mult),
                         reads=CK + W2k, writes=W2k)
                    def unit_gen(hh, up):
                        h = 2 * hp + hh
                        base = 8 + 3 * up
                        names = ["kbg", "kdec", "vb", "sz", "L", "LTs", "AT", "NTa", "NTb", "Pa", "Pb", "PTa"]
                        Tl = {}
                        for n_i, nm in enumerate(names):
                            Tl[nm] = BQ(base + n_i // 4, n_i % 4)
                        extra = ["PTb", "u", "wT", "qg", "vnew", "og"]
                        for n_i, nm in enumerate(extra):
                            Tl[nm] = BQ((n_i + 6 * up) // 4, (n_i + 6 * up) % 4)
                        Dm = W1[:, hh * 128:(hh + 1) * 128]; Dmk = [W1k[hh]]
                        DTm = W2[:, hh * 128:(hh + 1) * 128]; DTmk = [W2k[hh]]
                        egr = W3[:, hh * 128:(hh + 1) * 128]; egk = [W3k[hh]]
                        bt = toks[:, 0, tb, h:h + 1]; bg2 = toks[:, 4, tb, h:h + 1]; kd = toks[:, 5, tb, h:h + 1]
                        qb_, kb_, vb_, zb_ = qf[hh][:, cs], kf[hh][:, cs], vf[hh][:, cs], zf[hh][:, cs]
                        SK = [f"S{h}"]
                        Sh = Sst[:, h, :]
                        bA, bB, bC = 3 + up, 5 + up, up
                        pt, ptk = PQ(bA, 0, 3)
                        P.op("pe", lambda e, pt=pt, kb_=kb_: e.transpose(pt[:, 0:128], kb_, ident[:]), reads=CK + [kk_[hh]], writes=[ptk[0]])
                        P.op("pe", lambda e, pt=pt, vb_=vb_: e.transpose(pt[:, 128:256], vb_, ident[:]), reads=CK + [vk_[hh]], writes=[ptk[1]])
                        P.op("pe", lambda e, pt=pt, zb_=zb_: e.transpose(pt[:, 256:384], zb_, ident[:]), reads=CK + [zk_[hh]], writes=[ptk[2]])
                        pk2_, pk2k = PQ(bB, 0, 2)
                        P.op("pe", lambda e, pk2_=pk2_, kb_=kb_: e.matmul(pk2_[:, 0:128], kb_, kb_, start=True, stop=True),
                             reads=[kk_[hh]], writes=[pk2k[0]])
                        P.op("pe", lambda e, pk2_=pk2_, kb_=kb_, qb_=qb_: e.matmul(pk2_[:, 128:256], kb_, qb_, start=True, stop=True),
                             reads=[kk_[hh], qk_[hh]], writes=[pk2k[1]])
                        yield
                        (Lt, Lk), (LTs, LTsk), (AT, ATk) = Tl["L"], Tl["LTs"], Tl["AT"]
                        P.op("dve", lambda e, Lt=Lt, pk2_=pk2_, bt=bt, Dm=Dm: e.scalar_tensor_tensor(
                            Lt, pk2_[:, 0:128], bt, Dm, ALU.mult, ALU.mult), reads=[pk2k[0], "toks"] + Dmk, writes=Lk)
                        (kbg, kbgk), (kdec, kdeck), (vb, vbk), (sz, szk) = Tl["kbg"], Tl["kdec"], Tl["vb"], Tl["sz"]
                        P.op("act", lambda e, pt=pt, kdec=kdec, kd=kd: e.activation(kdec, pt[:, 0:128], AF.Copy, scale=kd),
                             reads=[ptk[0], "toks"], writes=kdeck)
                        yield
                        plt, pltk = PQ(bC, 0)
                        P.op("pe", lambda e, plt=plt, Lt=Lt: e.transpose(plt, Lt, ident[:]), reads=CK + Lk, writes=pltk)
                        P.op("dve", lambda e, AT=AT, pk2_=pk2_, DTm=DTm: e.tensor_tensor(AT, pk2_[:, 128:256], DTm, ALU.mult),
                             reads=[pk2k[1]] + DTmk, writes=ATk)
                        P.op("act", lambda e, pt=pt, sz=sz: e.activation(sz, pt[:, 256:384], AF.Silu), reads=[ptk[2]], writes=szk)
                        yield
                        NT = [Tl["NTa"], Tl["NTb"]]
                        Pm = [Tl["Pa"], Tl["Pb"]]
                        PTm = [Tl["PTa"], Tl["PTb"]]
                        P.op("dve", lambda e, plt=plt, nt=NT[0][0]: e.scalar_tensor_tensor(nt, plt, -1.0, ident[:], ALU.mult, ALU.add),
                             reads=CK + pltk, writes=NT[0][1])
                        P.op("act", lambda e, plt=plt, LTs=LTs: e.copy(LTs, plt), reads=pltk, writes=LTsk)
                        yield
                        pq1, pq1k = PQ(bB, 3)
                        pq2, pq2k = PQ(bA, 3)
                        P.op("pe", lambda e, pq1=pq1, LTs=LTs, Lt=Lt: e.matmul(pq1, LTs, Lt, start=True, stop=True),
                             reads=LTsk + Lk, writes=pq1k)
                        P.op("pe", lambda e, pq2=pq2, LTs=LTs, Lt=Lt: e.matmul(pq2, Lt, LTs, start=True, stop=True),
                             reads=LTsk + Lk, writes=pq2k)
                        yield
                        P.op("dve", lambda e, pq1=pq1, d=Pm[0][0]: e.tensor_copy(d, pq1), reads=pq1k, writes=Pm[0][1])
                        P.op("act", lambda e, pq2=pq2, d=PTm[0][0]: e.copy(d, pq2), reads=pq2k, writes=PTm[0][1])
                        yield
                        cur = 0
                        for lev in range(5):
                            Pc, Pck = Pm[lev % 2]
                            PTc, PTck = PTm[lev % 2]
                            ntc, ntck = NT[cur]
                            ntn, ntnk = NT[1 - cur]
                            pu, puk = PQ(bC, 1)
                            P.op("pe", lambda e, pu=pu, Pc=Pc, ntc=ntc: e.matmul(pu, Pc, ntc, start=True, stop=True),
                                 reads=Pck + ntck, writes=puk)
                            if lev < 4:
                                Pn, Pnk = Pm[(lev + 1) % 2]
                                PTn, PTnk = PTm[(lev + 1) % 2]
                                P.op("pe", lambda e, pq1=pq1, PTc=PTc, Pc=Pc: e.matmul(pq1, PTc, Pc, start=True, stop=True),
                                     reads=PTck + Pck, writes=pq1k)
                                P.op("pe", lambda e, pq2=pq2, PTc=PTc, Pc=Pc: e.matmul(pq2, Pc, PTc, start=True, stop=True),
                                     reads=PTck + Pck, writes=pq2k)
                            yield
                            P.op("dve", lambda e, pu=pu, ntc=ntc, ntn=ntn: e.tensor_tensor(ntn, pu, ntc, ALU.add),
                                 reads=puk + ntck, writes=ntnk)
                            cur = 1 - cur
                            if lev < 4:
                                P.op("act", lambda e, pq2=pq2, PTn=PTn: e.copy(PTn, pq2), reads=pq2k, writes=PTnk)
                                P.op("dve", lambda e, pq1=pq1, Pn=Pn: e.tensor_copy(Pn, pq1), reads=pq1k, writes=Pnk)
                            if lev == 1:
                                P.op("dve", lambda e, pt=pt, kbg=kbg, bg2=bg2: e.tensor_scalar(kbg, pt[:, 0:128], bg2, None, ALU.mult),
                                     reads=[ptk[0], "toks"], writes=kbgk)
                            if lev == 2:
                                P.op("dve", lambda e, pt=pt, vb=vb, bt=bt: e.tensor_scalar(vb, pt[:, 128:256], bt, None, ALU.mult),
                                     reads=[ptk[1], "toks"], writes=vbk)
                            yield
                        TTm, TTk = NT[cur]
                        (u_, uk), (wT, wTk), (qg, qgk), (vnew, vnk), (og, ogk) = Tl["u"], Tl["wT"], Tl["qg"], Tl["vnew"], Tl["og"]
                        P.op("pe", lambda e, pq1=pq1, TTm=TTm, vb=vb: e.matmul(pq1, TTm, vb, start=True, stop=True),
                             reads=TTk + vbk, writes=pq1k)
                        P.op("pe", lambda e, pq2=pq2, TTm=TTm, kbg=kbg: e.matmul(pq2, kbg, TTm, start=True, stop=True),
                             reads=TTk + kbgk, writes=pq2k)
                        P.op("pool", lambda e, qg=qg, qb_=qb_, egr=egr: e.tensor_tensor(qg, qb_, egr, ALU.mult),
                             reads=[qk_[hh]] + egk, writes=qgk)
                        yield
                        P.op("act", lambda e, pq1=pq1, u_=u_: e.copy(u_, pq1), reads=pq1k, writes=uk)
                        P.op("dve", lambda e, pq2=pq2, wT=wT: e.tensor_copy(wT, pq2), reads=pq2k, writes=wTk)
                        yield
                        po, pok = PQ(2 if up == 0 else 7, 0)
                        pws, pwsk = PQ(bC, 2)
                        pds, pdsk = PQ(bB, 2)
                        for c in range(2):
                            rs = slice(64 * c, 64 * c + 64)
                            P.op("pe", lambda e, pws=pws, wT=wT, rs=rs, Sh=Sh: e.matmul(pws[rs, :], wT[:, rs], Sh, start=True, stop=True),
                                 reads=wTk + SK, writes=pwsk)
                            P.op("pe", lambda e, po=po, qg=qg, rs=rs, Sh=Sh: e.matmul(po[rs, :], qg[:, rs], Sh, start=True, stop=False),
                                 reads=qgk + SK, writes=pok)
                            yield
                            P.op("dve", lambda e, vnew=vnew, u_=u_, pws=pws, rs=rs: e.tensor_tensor(vnew[rs, :], u_[rs, :], pws[rs, :], ALU.subtract),
                                 reads=uk + pwsk, writes=vnk)
                            yield
                            P.op("pe", lambda e, po=po, AT=AT, rs=rs, vnew=vnew: e.matmul(po[rs, :], AT[rs, rs], vnew[rs, :], start=False, stop=True),
                                 reads=ATk + vnk, writes=pok)
                            P.op("pe", lambda e, pds=pds, kdec=kdec, rs=rs, vnew=vnew: e.matmul(pds, kdec[rs, :], vnew[rs, :], start=True, stop=True),
                                 reads=kdeck + vnk, writes=pdsk)
                            yield
                            P.op("dve", lambda e, Sh=Sh, sdec=sdec, hh=hh, c=c, pds=pds: e.scalar_tensor_tensor(
                                Sh, Sh, sdec[:, 2 * hh + c:2 * hh + c + 1], pds, ALU.mult, ALU.add),
                                reads=SK + sdk + pdsk, writes=SK)
                            yield
                        ss = small[:, 8 + up:9 + up]
                        ssk = [f"ss{up}"]
                        P.op("dve", lambda e, ss=ss: e.memset(ss, 0.0), writes=ssk)
                        P.op("act", lambda e, og=og, po=po, ss=ss: e.activation(og, po, AF.Square, accum_out=ss), reads=pok + ssk, writes=ogk + ssk)
                        yield
                        P.op("act", lambda e, ss=ss: e.activation(ss, ss, AF.Sqrt, scale=1.0 / 128.0, bias=NORM_EPS), reads=ssk, writes=ssk)
                        yield
                        P.op("dve", lambda e, ss=ss: e.reciprocal(ss, ss), reads=ssk, writes=ssk)
                        P.op("dve", lambda e, og=og, po=po, ss=ss: e.scalar_tensor_tensor(og, po, ss, normw[:], ALU.mult, ALU.mult),
                             reads=CK + pok + ssk, writes=ogk)
                        yield
                        P.op("pool", lambda e, og=og, sz=sz: e.tensor_tensor(og, og, sz, ALU.mult), reads=ogk + szk, writes=ogk)
                        yield
                        P.op("pe", lambda e, pws=pws, og=og: e.transpose(pws, og, ident[:]), reads=CK + ogk, writes=pwsk)
                        yield
                        P.op("act", lambda e, pws=pws, h=h, cs=cs: e.copy(bufA[:, h, cs], pws), reads=pwsk, writes=[f"A{h}"])

                    gens = [unit_gen(0, 0), unit_gen(1, 1)] if dbg >= 1.7 else []
                    while gens:
                        for g_ in list(gens):
                            try:
                                next(g_)
                            except StopIteration:
                                gens.remove(g_)

            pwv = bufB[:].rearrange("p c t -> p (c t)")[:, 0:2048].rearrange("p (g k d) -> p g k d", g=4, k=2)
            if dbg >= 3:
                P.dma("poolw", lambda e: e.dma_start(out=pwv, in_=poolw_d.rearrange("g (k p) d -> p g k d", p=128)),
                      writes=AK("B", range(4)), E="pool")
            for g in range(4 if dbg >= 3 else 0):
                win = 2 ** (g + 1)
                for cc in range(2):
                    pc = 2 * g + cc
                    pb, pk = project(("w_in", 0, 4112 + g * 256 + cc * 128), hT, "H")
                    r = raw[cc]; rk_ = f"raw{cc}"
                    P.op("act", lambda e, pb=pb, r=r: e.copy(r[:, 15:527], pb[:]), reads=pk, writes=[rk_])
                    P.op("pool", lambda e, r=r, pc=pc: e.tensor_copy(r[:, 0:15], phalo[:, pc, :]), reads=["phalo"], writes=[rk_])
                    P.op("pool", lambda e, r=r, pc=pc: e.tensor_copy(phalo[:, pc, :], r[:, 512:527]), reads=[rk_], writes=["phalo"])
                    src, srck = r, rk_
                    for lev in range(g + 1):
                        sh = 2 ** lev
                        lo = 2 * sh - 1
                        d_ = raw[2 + lev % 2]; dk_ = f"raw{2 + lev % 2}"
                        eng = "dve" if (lev + cc) % 2 == 0 else "pool"
                        P.op(eng, lambda e, d_=d_, src=src, lo=lo, sh=sh: e.tensor_tensor(d_[:, lo:527], src[:, lo:527], src[:, lo - sh:527 - sh], ALU.add),
                             reads=[srck], writes=[dk_])
                        src, srck = d_, dk_
                    dst = bufB[:, 8 + pc, :]
                    P.op("dve", lambda e, dst=dst, src=src, win=win, r=r: e.scalar_tensor_tensor(
                        dst, src[:, 15:527], 1.0 / win, r[:, 15:527], ALU.mult, ALU.subtract), reads=[srck, rk_], writes=[f"B{8 + pc}"])
                    if ti == 0:
                        P.op("dve", lambda e, dst=dst, src=src, g=g: e.tensor_tensor(dst[:, 0:16], src[:, 15:31], rc16[:, g, :], ALU.mult),
                             reads=CK + [srck, f"B{8 + pc}"], writes=[f"B{8 + pc}"])
                        P.op("dve", lambda e, dst=dst, r=r: e.tensor_tensor(dst[:, 0:16], f32(dst)[:, 0:16], r[:, 15:31], ALU.subtract),
                             reads=[rk_, f"B{8 + pc}"], writes=[f"B{8 + pc}"])
                for dch in range(2):
                    pb, pk = proj_bank()
                    for cc in range(2):
                        P.op("pe", lambda e, g=g, cc=cc, dch=dch, pb=pb: e.matmul(
                            pb[:], pwv[:, g, cc, dch * 128:(dch + 1) * 128], bufB[:, 8 + 2 * g + cc, :], start=(cc == 0), stop=(cc == 1)),
                            reads=AK("B", range(4)) + [f"B{8 + 2 * g + cc}"], writes=pk)
                    P.op("act", lambda e, g=g, dch=dch, pb=pb: e.activation(bufA[:, 8 + 2 * g + dch, :], pb[:], AF.Copy,
                                                                             scale=pscale[:, 2 * g + dch:2 * g + dch + 1]),
                         reads=CK + pk, writes=[f"A{8 + 2 * g + dch}"])

            for c in range(16 if dbg >= 4 else 0):
                pb, pk = project(("w_out", 0, c * 128), bufA, "A")
                P.op("dve", lambda e, c=c, pb=pb: e.scalar_tensor_tensor(bufB[:, c, :], f32(hT[:, c, :]), ALPHA, pb[:], ALU.mult, ALU.add),
                     reads=pk + [f"H{c}"], writes=[f"B{c}"])
            if dbg >= 4:
                layer_norm(bufB, "B", 0)

            if stop_after >= 2:
                for c in range(16):
                    pb, pk = project(("xq_w", 0, c * 128), hT, "H")
                    if c % 2:
                        P.op("act", lambda e, c=c, pb=pb: e.copy(bufA[:, c, :], pb[:]), reads=pk, writes=[f"A{c}"])
                    else:
                        P.op("dve", lambda e, c=c, pb=pb: e.tensor_copy(bufA[:, c, :], pb[:]), reads=pk, writes=[f"A{c}"])
                sc = 512.0 ** -0.5
                for hd in range(4):
                    PTh, PThk = big[4 + hd % 2], [f"a{4 + hd % 2}_{q}" for q in range(4)]
                    PTh2, PTh2k = big[6 + hd % 2], [f"a{6 + hd % 2}_{q}" for q in range(4)]
                    PTs = [(PTh, PThk), (PTh2, PTh2k)]
                    for qb in range(4):
                        u2 = (hd * 4 + qb) % 2
                        psc, psck = PQ(3 + u2, 0, 2)
                        for dc in range(4):
                            P.op("pe", lambda e, psc=psc, hd=hd, dc=dc, qb=qb: e.matmul(
                                psc, bufA[:, 4 * hd + dc, qb * 128:(qb + 1) * 128], kxT[:, 4 * hd + dc, :], start=(dc == 0), stop=(dc == 3)),
                                reads=[f"A{4 * hd + dc}", "kx"], writes=psck)
                        mx = small[:, 10 + u2:11 + u2]; mxk = [f"mx{u2}"]
                        sm = small[:, 12 + u2:13 + u2]; smk = [f"sm{u2}"]
                        pe_, pek = BQ(8 + u2, 0, 2)
                        P.op("dve", lambda e, mx=mx, psc=psc: e.reduce_max(mx, psc, AX.X), reads=psck, writes=mxk)
                        P.op("dve", lambda e, mx=mx: e.tensor_scalar(mx, mx, -sc, None, ALU.mult), reads=mxk, writes=mxk)
                        P.op("dve", lambda e, sm=sm: e.memset(sm, 0.0), writes=smk)
                        P.op("act", lambda e, pe_=pe_, psc=psc, mx=mx, sm=sm: e.activation(pe_, psc, AF.Exp, bias=mx, scale=sc, accum_out=sm),
                             reads=psck + mxk + smk, writes=pek + smk)
                        P.op("dve", lambda e, sm=sm: e.reciprocal(sm, sm), reads=smk, writes=smk)
                        P.op("pool", lambda e, pe_=pe_, sm=sm: e.tensor_scalar(pe_, pe_, sm, None, ALU.mult), reads=pek + smk, writes=pek)
                        ptt, pttk = PQ(3 + u2, 2, 2)
                        for mb in range(2):
                            P.op("pe", lambda e, ptt=ptt, pe_=pe_, mb=mb: e.transpose(ptt[:, mb * 128:(mb + 1) * 128], pe_[:, mb * 128:(mb + 1) * 128], ident[:]),
                                 reads=CK + pek, writes=pttk)
                        for mb in range(2):
                            if mb:
                                P.op("act", lambda e, ptt=ptt, mb=mb, qb=qb, d=PTs[mb][0]: e.copy(d[:, qb * 128:(qb + 1) * 128], ptt[:, mb * 128:(mb + 1) * 128]),
                                     reads=pttk, writes=[PTs[mb][1][qb]])
                            else:
                                P.op("dve", lambda e, ptt=ptt, mb=mb, qb=qb, d=PTs[mb][0]: e.tensor_copy(d[:, qb * 128:(qb + 1) * 128], ptt[:, mb * 128:(mb + 1) * 128]),
                                     reads=pttk, writes=[PTs[mb][1][qb]])
                    for dhc in range(4):
                        pb, pk = proj_bank()
                        for mb in range(2):
                            P.op("pe", lambda e, pb=pb, mb=mb, hd=hd, dhc=dhc, d=PTs[mb][0]: e.matmul(
                                pb[:], f32(vx[:, mb, hd * 512 + dhc * 128: hd * 512 + (dhc + 1) * 128]), d[:], start=(mb == 0), stop=(mb == 1)),
                                reads=["vx"] + PTs[mb][1], writes=pk)
                        oc = 4 * hd + dhc
                        if dhc % 2:
                            P.op("act", lambda e, oc=oc, pb=pb: e.copy(bufB[:, oc, :], pb[:]), reads=pk, writes=[f"B{oc}"])
                        else:
                            P.op("dve", lambda e, oc=oc, pb=pb: e.tensor_copy(bufB[:, oc, :], pb[:]), reads=pk, writes=[f"B{oc}"])
                for c in range(16):
                    pb, pk = project(("xo_w", 0, c * 128), bufB, "B")
                    P.op("dve", lambda e, c=c, pb=pb: e.scalar_tensor_tensor(bufA[:, c, :], f32(hT[:, c, :]), ALPHA, pb[:], ALU.mult, ALU.add),
                         reads=pk + [f"H{c}"], writes=[f"A{c}"])
                layer_norm(bufA, "A", 1)

            if stop_after >= 3:
                for g in range(4):
                    for j in range(16):
                        pb, pk = project(("w_up", 0, g * 2048 + j * 128), hT, "H")
                        rl, rlk = big[8 + j % 4], [f"a{8 + j % 4}_{q}" for q in range(4)]
                        P.op("act", lambda e, pb=pb, rl=rl: e.activation(rl[:], pb[:], AF.Relu), reads=pk, writes=rlk)
                        if j % 2:
                            P.op("act", lambda e, j=j, rl=rl: e.activation(bufB[:, j, :], rl[:], AF.Square), reads=rlk, writes=[f"B{j}"])
                        else:
                            P.op("dve", lambda e, j=j, rl=rl: e.tensor_tensor(bufB[:, j, :], rl[:], rl[:], ALU.mult), reads=rlk, writes=[f"B{j}"])
                    for j in range(16):
                        pb, pk = project(("w_down", g * 2048, j * 128), bufB, "B")
                        if g == 0:
                            P.op("dve", lambda e, j=j, pb=pb: e.scalar_tensor_tensor(bufA[:, j, :], f32(hT[:, j, :]), ALPHA, pb[:], ALU.mult, ALU.add),
                                 reads=pk + [f"H{j}"], writes=[f"A{j}"])
                        elif g < 3:
                            P.op("dve", lambda e, j=j, pb=pb: e.tensor_tensor(bufA[:, j, :], pb[:], f32(bufA[:, j, :]), ALU.add),
                                 reads=pk + [f"A{j}"], writes=[f"A{j}"])
                        else:
                            P.op("dve", lambda e, j=j, pb=pb: e.tensor_tensor(bufA[:, j, :], pb[:], f32(bufA[:, j, :]), ALU.add),
                                 reads=pk + [f"A{j}"], writes=[f"A{j}"])
                layer_norm(bufA, "A", 2)

            for tb in range(4):
                obig = (8, 9, 10, 11) if tb % 2 == 0 else (12, 13, 0, 1)
                for dq in range(4):
                    pb, pk = proj_bank()
                    for d4 in range(4):
                        dc = dq * 4 + d4
                        P.op("pe", lambda e, pb=pb, d4=d4, dc=dc, tb=tb: e.transpose(pb[:, d4 * 128:(d4 + 1) * 128], f32(hT[:, dc, tb * 128:(tb + 1) * 128]), ident[:]),
                             reads=CK + [f"H{dc}"], writes=pk)
                    dsto = big[obig[dq]]
                    dstk = [f"a{obig[dq]}_{q}" for q in range(4)]
                    if dq % 2:
                        P.op("act", lambda e, pb=pb, dsto=dsto: e.copy(dsto[:], pb[:]), reads=pk, writes=dstk)
                    else:
                        P.op("dve", lambda e, pb=pb, dsto=dsto: e.tensor_copy(dsto[:], pb[:]), reads=pk, writes=dstk)
                    t = P.dma(f"out{tb % 2}_{dq}", lambda e, tb=tb, dq=dq, dsto=dsto, t0=t0: e.dma_start(
                        out=out_d[t0 + tb * 128:t0 + (tb + 1) * 128, dq * 512:(dq + 1) * 512], in_=dsto[:]), reads=dstk)
                    out_toks.append(t)
        P.wait_tokens("sp", out_toks)
        P.emit()
    return nc


def host_inputs(inp, b, ntiles=4):
    T = ntiles * TT
    f = lambda a: np.ascontiguousarray(a, dtype=np.float32)
    m = {"x": f(inp["x"][b, :T]), "mem": f(inp["mem"][b])}
    for k in ("w_in", "w_out", "xq_w", "xk_w", "xv_w", "xo_w", "w_up", "w_down", "pool_w"):
        m[k] = f(inp[k][0])
    m["cw"] = f(np.asarray(inp["conv_w"][0]).T.reshape(24, 128, 4).transpose(1, 0, 2))
    m["alog"] = f(np.asarray(inp["a_log"][0]).reshape(8, 1))
    m["dtb"] = f(np.asarray(inp["dt_bias"][0]).reshape(8, 1))
    m["normw"] = f(np.broadcast_to(np.asarray(inp["gdn_norm_w"][0])[None, :], (128, 128)))
    m["pscale"] = f(np.asarray(inp["pool_scale"][0]).reshape(8, 128).T)
    lnp = np.stack([np.asarray(inp[k][0]).reshape(16, 128).T for k in ("ln1_g", "ln1_b", "ln2_g", "ln2_b", "ln3_g", "ln3_b")], axis=1)
    m["lnp"] = f(lnp)
    m.update(host_consts())
    return m


_NC_CACHE = {}


def kernel(**inputs):
    inp = {k: np.asarray(v) for k, v in inputs.items()}
    if "nc" not in _NC_CACHE:
        _NC_CACHE["nc"] = build(4, 3)
    nc = _NC_CACHE["nc"]
    in_maps = [host_inputs(inp, b) for b in range(8)]
    res = run_bass_kernel_spmd(nc, in_maps, core_ids=list(range(8)))
    out = np.stack([np.asarray(res.results[b]["out"]) for b in range(8)], axis=0)
    return out.astype(np.float32)
```

```python
import numpy as np
from contextlib import ExitStack
import concourse.bass as bass
import concourse.mybir as mybir
from concourse.bass_utils import run_bass_kernel_spmd

F32 = mybir.dt.float32
F32R = mybir.dt.float32r
AF = mybir.ActivationFunctionType
ALU = mybir.AluOpType
AX = mybir.AxisListType

EPOCH = 8000
D = 2048
SEQ = 2048
TT = 512
NDC = 16
MEM = 256
DFF = 8192
INC = 5136
ALPHA = 2.0 ** 0.25
LN_EPS = 1e-5
NORM_EPS = 1e-6
NSLOT = 3
NBIG = 14


class Prog:
    COMPUTE = ("pe", "act", "dve", "pool")

    def __init__(self, nc, stack):
        self.nc = nc
        self.stack = stack
        self.ops = {e: [] for e in ("pe", "act", "dve", "pool", "sp")}
        self.count = {e: 0 for e in self.COMPUTE}
        self.sems = {}
        self.known = {e: {} for e in self.ops}
        self.last_write = {}
        self.readers = {}
        self.dma_count = {}

    def _sem(self, name):
        if name not in self.sems:
            self.sems[name] = self.stack.enter_context(self.nc.semaphore(name))
        return self.sems[name]

    def _phys(self, cname, v):
        if cname in self.COMPUTE:
            return self._sem(f"s_{cname}_{(v - 1) // EPOCH}"), (v - 1) % EPOCH + 1
        return self._sem(cname), v

    def _collect(self, E, reads, writes):
        deps = []
        for k in reads:
            t = self.last_write.get(k)
            if t is not None:
                deps.append(t)
        for k in writes:
            t = self.last_write.get(k)
            if t is not None:
                deps.append(t)
            deps.extend(self.readers.get(k, ()))
        deps.sort(key=lambda t: -t[1])
        kn = self.known[E]
        waits = {}
        for (cname, v, snap) in deps:
            if cname == E and E == "pe":
                continue
            if kn.get(cname, 0) >= v:
                continue
            waits[cname] = max(waits.get(cname, 0), v)
            kn[cname] = v
            for kk, vv in snap.items():
                if kn.get(kk, 0) < vv:
                    kn[kk] = vv
        return [self._phys(c, v) for c, v in waits.items()]

    @staticmethod
    def _excl(reads, writes):
        pr = [k for k in reads if k[0] == "p" and k[1:].isdigit()]
        if pr:
            reads = [k for k in reads if k not in pr]
            writes = list(writes) + pr
        return reads, writes

    def op(self, E, fn, reads=(), writes=()):
        reads, writes = self._excl(reads, writes)
        waits = self._collect(E, reads, writes)
        idx = self.count[E]
        self.count[E] += 1
        snap = dict(self.known[E])
        snap[E] = idx + 1
        tok = (E, idx + 1, snap)
        sem, _ = self._phys(E, idx + 1)
        self.ops[E].append((waits, fn, sem, 1))
        self._record(tok, reads, writes)
        return tok

    def dma(self, slot, fn, reads=(), writes=(), E="sp"):
        waits = self._collect(E, reads, writes)
        cname = "dma_" + E + "_" + slot
        self.dma_count[cname] = self.dma_count.get(cname, 0) + 16
        tok = (cname, self.dma_count[cname], dict(self.known[E]))
        self.ops[E].append((waits, fn, self._sem(cname), 16))
        self._record(tok, reads, writes)
        return tok

    def _record(self, tok, reads, writes):
        for k in reads:
            self.readers.setdefault(k, []).append(tok)
        for k in writes:
            self.last_write[k] = tok
            self.readers[k] = []

    def wait_tokens(self, E, toks):
        kn = self.known[E]
        waits = []
        for (cname, v, snap) in toks:
            if kn.get(cname, 0) >= v:
                continue
            kn[cname] = v
            waits.append(self._phys(cname, v))
        self.ops[E].append((waits, None, None, 0))

    def emit(self):
        nc = self.nc
        engs = {"pe": "tensor", "act": "scalar", "dve": "vector", "pool": "gpsimd", "sp": "sync"}
        with nc.Block() as block:
            for E, attr in engs.items():
                ops = self.ops[E]

                def body(eng, ops=ops):
                    for (waits, fn, sem, n) in ops:
                        for (s, v) in waits:
                            eng.wait_ge(s, v)
                        if fn is not None:
                            ins = fn(eng)
                            ins.then_inc(sem, n)

                getattr(block, attr)(body)


def host_consts():
    i = np.arange(128)
    same = (i[:, None] // 64) == (i[None, :] // 64)
    c = {}
    c["ident"] = np.eye(128, dtype=np.float32)
    c["ones"] = np.ones((128, 128), np.float32)
    c["avg"] = np.full((128, 128), 1.0 / D, np.float32)
    c["tribd"] = (same & (i[:, None] <= i[None, :])).astype(np.float32)
    c["bd"] = same.astype(np.float32)
    c["msbd"] = (same & (i[:, None] > i[None, :])).astype(np.float32)
    c["minclt"] = (same & (i[:, None] <= i[None, :])).astype(np.float32) * np.float32(128.0 ** -0.5)
    rc = np.zeros((128, 4, 16), np.float32)
    for g, win in enumerate((2, 4, 8, 16)):
        rc[:, g, :] = 1.0 / np.minimum(np.arange(16) + 1, win).astype(np.float32)
    c["rc16"] = rc
    return c


def tile_slabs(stop_after=3):
    L = []
    for hp in range(4):
        for part in range(4):
            for hh in range(2):
                L.append(("w_in", 0, part * 1024 + hp * 256 + hh * 128))
    for g in range(4):
        for cc in range(2):
            L.append(("w_in", 0, 4112 + g * 256 + cc * 128))
    for c in range(16):
        L.append(("w_out", 0, c * 128))
    if stop_after >= 2:
        for c in range(16):
            L.append(("xq_w", 0, c * 128))
        for c in range(16):
            L.append(("xo_w", 0, c * 128))
    if stop_after >= 3:
        for g in range(4):
            for j in range(16):
                L.append(("w_up", 0, g * 2048 + j * 128))
            for j in range(16):
                L.append(("w_down", g * 2048, j * 128))
    return L


def build(ntiles=4, stop_after=3, dbg=9):
    nc = bass.Bass("TRN2", target_bir_lowering=False)
    T = ntiles * TT

    def din(name, shape):
        return nc.dram_tensor(name, list(shape), F32, kind="ExternalInput").ap()

    x_d = din("x", [T, D])
    mem_d = din("mem", [MEM, D])
    W = {"w_in": din("w_in", [D, INC]), "w_out": din("w_out", [D, D]), "xq_w": din("xq_w", [D, D]),
         "xk_w": din("xk_w", [D, D]), "xv_w": din("xv_w", [D, D]), "xo_w": din("xo_w", [D, D]),
         "w_up": din("w_up", [D, DFF]), "w_down": din("w_down", [DFF, D])}
    poolw_d = din("pool_w", [4, 256, 256])
    cw_d = din("cw", [128, 24, 4])
    alog_d = din("alog", [8, 1])
    dtb_d = din("dtb", [8, 1])
    normw_d = din("normw", [128, 128])
    pscale_d = din("pscale", [128, 8])
    lnp_d = din("lnp", [128, 6, 16])
    cst_d = {k: din(k, v.shape) for k, v in host_consts().items()}
    out_d = nc.dram_tensor("out", [T, D], F32, kind="ExternalOutput").ap()

    with ExitStack() as st:
        P = Prog(nc, st)

        def sb(name, shape, dt=F32):
            return st.enter_context(nc.sbuf_tensor(name, list(shape), dt))

        wsl = [sb(f"wsl{i}", [128, 16, 128], F32R) for i in range(NSLOT)]
        hT = sb("hT", [128, 16, TT], F32R)
        bufA = sb("bufA", [128, 16, TT], F32R)
        bufB = sb("bufB", [128, 16, TT], F32R)
        kxT = sb("kxT", [128, 16, MEM], F32R)
        vx = sb("vx", [128, 2, D], F32R)
        big = [sb(f"big{i}", [128, 512], F32) for i in range(NBIG)]
        raw = [sb(f"raw{i}", [128, 528], F32) for i in range(4)]
        lnsq = [sb(f"lnsq{i}", [128, 512], F32R) for i in range(2)]
        Sst = sb("Sst", [128, 8, 128], F32)
        halo = sb("halo", [128, 24, 3], F32)
        phalo = sb("phalo", [128, 8, 15], F32)
        wba = sb("wba", [128, 16, 16], F32R)
        cw = sb("cw_s", [128, 24, 4])
        alog = sb("alog_s", [8, 1]); dtb = sb("dtb_s", [8, 1]); nega = sb("nega", [8, 1])
        normw = sb("normw_s", [128, 128])
        pscale = sb("pscale_s", [128, 8])
        lnp = sb("lnp_s", [128, 6, 16])
        ident = sb("ident_s", [128, 128]); ones = sb("ones_s", [128, 128])
        avg = sb("avg_s", [128, 128], F32R)
        bdm = sb("bd_s", [128, 128]); tribd = sb("tribd_s", [128, 128]); msbd = sb("msbd_s", [128, 128]); minclt = sb("minclt_s", [128, 128])
        rc16 = sb("rc16_s", [128, 4, 16])
        toks = sb("toks", [128, 6, 4, 8])
        small = sb("small", [128, 16])
        ps = [st.enter_context(nc.psum_tensor(f"ps{i}", [128, 512], F32)) for i in range(8)]

        def f32(ap):
            return ap.bitcast(F32)

        def AK(pref, cs):
            return [f"{pref}{c}" for c in cs]

        def BQ(i, j, n=1):
            return big[i][:, j * 128:(j + n) * 128], [f"a{i}_{q}" for q in range(j, j + n)]

        def PQ(b, j, n=1):
            return ps[b][:, j * 128:(j + n) * 128], [f"p{b}"] * n

        CK = ["const"]
        for (dst, src) in ((cw, cw_d), (alog, alog_d), (dtb, dtb_d), (normw, normw_d), (pscale, pscale_d),
                           (lnp, lnp_d), (ident, cst_d["ident"]), (ones, cst_d["ones"]), (tribd, cst_d["tribd"]), (bdm, cst_d["bd"]),
                           (msbd, cst_d["msbd"]), (minclt, cst_d["minclt"]), (rc16, cst_d["rc16"])):
            P.dma("const", lambda e, dst=dst, src=src: e.dma_start(out=dst[:], in_=src), writes=CK)
        P.dma("const", lambda e: e.dma_start(out=avg[:], in_=cst_d["avg"]), writes=CK, E="pool")
        P.dma("const", lambda e: e.dma_start(
            out=wba[:], in_=W["w_in"][:, 4096:4112].rearrange("(kc p) c -> p kc c", p=128)), writes=CK, E="pool")
        P.op("dve", lambda e: e.memset(Sst[:], 0.0), writes=AK("S", range(8)))
        P.op("dve", lambda e: e.memset(halo[:], 0.0), writes=["halo"])
        P.op("dve", lambda e: e.memset(phalo[:], 0.0), writes=["phalo"])
        P.op("act", lambda e: e.activation(nega[:], alog[:], AF.Exp), reads=CK, writes=["nega"])
        P.op("dve", lambda e: e.tensor_scalar(nega[:], nega[:], -1.0, None, ALU.mult), reads=["nega"], writes=["nega"])

        sched = []
        ws_state = {"issued": 0, "pos": 0}

        def slab_src(name, r0, c0):
            return W[name][r0:r0 + 2048, c0:c0 + 128].rearrange("(kc p) c -> p kc c", p=128)

        def ws_issue_upto(n):
            while ws_state["issued"] < min(n, len(sched)):
                j = ws_state["issued"]
                s = j % NSLOT
                src = slab_src(*sched[j])
                P.dma(f"w{s}", lambda e, s=s, src=src: e.dma_start(out=wsl[s][:], in_=src), writes=[f"w{s}"], E="pool")
                ws_state["issued"] += 1

        def ws_use(expect):
            i = ws_state["pos"]
            assert sched[i] == expect, (i, sched[i], expect)
            ws_issue_upto(i + NSLOT)
            ws_state["pos"] += 1
            return wsl[i % NSLOT], f"w{i % NSLOT}"

        if stop_after >= 2:
            for c in range(16):
                sched.append(("xk_w", 0, c * 128))
            for c in range(16):
                sched.append(("xv_w", 0, c * 128))
        for _ in range(ntiles):
            sched.extend(tile_slabs(stop_after)[:{0: 0, 1: 0, 2: 32, 3: 40}.get(dbg, 32 if dbg < 2 else 10 ** 6)])

        pj = {"i": 0}

        def proj_bank():
            b = pj["i"] % 2
            pj["i"] += 1
            return ps[b], [f"p{b}"]

        def project(expect, act_tile, act_pref, n=TT):
            wt, wk = ws_use(expect)
            pb, pk = proj_bank()
            for kc in range(16):
                P.op("pe", lambda e, kc=kc, wt=wt, pb=pb: e.matmul(pb[:, 0:n], wt[:, kc, :], act_tile[:, kc, 0:n],
                                                                  start=(kc == 0), stop=(kc == 15)),
                     reads=[wk, f"{act_pref}{kc}"], writes=pk)
            return pb, pk

        def layer_norm(y, ypref, li):
            mean_s, mk = big[0], [f"a0_{q}" for q in range(4)]
            rstd_s, rk = big[1], [f"a1_{q}" for q in range(4)]
            pb, pk = proj_bank()
            for dc in range(16):
                P.op("pe", lambda e, dc=dc: e.matmul(pb[:], avg[:], y[:, dc, :], start=(dc == 0), stop=(dc == 15)),
                     reads=CK + [f"{ypref}{dc}"], writes=pk)
            P.op("act", lambda e: e.copy(mean_s[:], pb[:]), reads=pk, writes=mk)
            pb2, pk2 = proj_bank()
            for dc in range(16):
                P.op("dve", lambda e, dc=dc: e.tensor_tensor(y[:, dc, :], f32(y[:, dc, :]), mean_s[:], ALU.subtract),
                     reads=mk + [f"{ypref}{dc}"], writes=[f"{ypref}{dc}"])
                sq_t = lnsq[dc % 2]
                sqk = [f"lnsq{dc % 2}"]
                P.op("act", lambda e, dc=dc, sq_t=sq_t: e.activation(sq_t[:], f32(y[:, dc, :]), AF.Square),
                     reads=[f"{ypref}{dc}"], writes=sqk)
                P.op("pe", lambda e, dc=dc, sq_t=sq_t: e.matmul(pb2[:], avg[:], sq_t[:],
                                                              start=(dc == 0), stop=(dc == 15)),
                     reads=CK + sqk, writes=pk2)
            P.op("act", lambda e: e.activation(rstd_s[:], pb2[:], AF.Sqrt, bias=LN_EPS), reads=pk2, writes=rk)
            P.op("dve", lambda e: e.reciprocal(rstd_s[:], rstd_s[:]), reads=rk, writes=rk)
            for dc in range(16):
                eng = "dve" if dc % 2 == 0 else "pool"
                P.op(eng, lambda e, dc=dc: e.tensor_tensor(y[:, dc, :], f32(y[:, dc, :]), rstd_s[:], ALU.mult),
                     reads=rk + [f"{ypref}{dc}"], writes=[f"{ypref}{dc}"])
                P.op("act", lambda e, dc=dc: e.activation(hT[:, dc, :], f32(y[:, dc, :]), AF.Identity,
                                                          scale=lnp[:, 2 * li, dc:dc + 1], bias=lnp[:, 2 * li + 1, dc:dc + 1]),
                     reads=CK + [f"{ypref}{dc}"], writes=[f"H{dc}"])

        PRO = stop_after >= 2
        memtok = bufB[:].rearrange("p c t -> p (c t)")
        for mb in range(2 if PRO else 0):
            P.dma(f"xin{mb}", lambda e, mb=mb: e.dma_start(out=memtok[:, mb * D:(mb + 1) * D],
                                                      in_=mem_d[mb * 128:(mb + 1) * 128, :]),
                  writes=AK("B", range(4 * mb, 4 * mb + 4)), E="pool")
        memT = bufA[:].rearrange("p c t -> p (c t)")
        for dc in range(16 if PRO else 0):
            tb_, tk_ = PQ(3 + dc % 2, 0, 2)
            for mb in range(2):
                P.op("pe", lambda e, dc=dc, mb=mb, tb_=tb_: e.transpose(
                    tb_[:, mb * 128:(mb + 1) * 128], f32(memtok[:, mb * D + dc * 128: mb * D + (dc + 1) * 128]), ident[:]),
                    reads=CK + AK("B", [4 * mb + dc // 4]), writes=tk_)
            P.op("act" if dc % 2 else "dve", (lambda e, dc=dc, tb_=tb_: e.copy(memT[:, dc * 256:(dc + 1) * 256], tb_))
                 if dc % 2 else (lambda e, dc=dc, tb_=tb_: e.tensor_copy(memT[:, dc * 256:(dc + 1) * 256], tb_)),
                 reads=tk_, writes=AK("A", [dc // 2]))
        for c in range(16 if PRO else 0):
            wt, wk = ws_use(("xk_w", 0, c * 128))
            pb, pk = proj_bank()
            for kc in range(16):
                P.op("pe", lambda e, kc=kc, wt=wt, pb=pb: e.matmul(pb[:, 0:256], wt[:, kc, :], memT[:, kc * 256:(kc + 1) * 256],
                                                                  start=(kc == 0), stop=(kc == 15)),
                     reads=[wk] + AK("A", [kc // 2]), writes=pk)
            P.op("act", lambda e, c=c, pb=pb: e.copy(kxT[:, c, :], pb[:, 0:256]), reads=pk, writes=["kx"])
        for c in range(16 if PRO else 0):
            wt, wk = ws_use(("xv_w", 0, c * 128))
            pb, pk = proj_bank()
            for mb in range(2):
                for kc in range(16):
                    P.op("pe", lambda e, kc=kc, mb=mb, wt=wt, pb=pb: e.matmul(
                        pb[:, mb * 128:(mb + 1) * 128], memT[:, kc * 256 + mb * 128: kc * 256 + (mb + 1) * 128], wt[:, kc, :],
                        start=(kc == 0), stop=(kc == 15)),
                        reads=[wk] + AK("A", [kc // 2]), writes=pk)
            P.op("dve", lambda e, c=c, pb=pb: e.tensor_copy(
                vx[:, :, c * 128:(c + 1) * 128], pb[:, 0:256].rearrange("p (m c) -> p m c", m=2)), reads=pk, writes=["vx"])

        out_toks = []
        unit_ctr = {"u": 0}

        for ti in range(ntiles):
            t0 = ti * TT
            xtok = bufB[:].rearrange("p c t -> p (c t)")
            for tb in range(4):
                P.dma(f"xin{tb}", lambda e, tb=tb, t0=t0: e.dma_start(out=xtok[:, tb * D:(tb + 1) * D],
                                                          in_=x_d[t0 + tb * 128: t0 + (tb + 1) * 128, :]),
                      writes=AK("B", range(4 * tb, 4 * tb + 4)), E="pool")
            for dc in range(16):
                pb, pk = proj_bank()
                for tb in range(4):
                    P.op("pe", lambda e, dc=dc, tb=tb, pb=pb: e.transpose(
                        pb[:, tb * 128:(tb + 1) * 128], f32(xtok[:, tb * D + dc * 128: tb * D + (dc + 1) * 128]), ident[:]),
                        reads=CK + AK("B", [4 * tb + dc // 4]), writes=pk)
                if dc % 2:
                    P.op("act", lambda e, dc=dc, pb=pb: e.copy(hT[:, dc, :], pb[:]), reads=pk, writes=[f"H{dc}"])
                else:
                    P.op("dve", lambda e, dc=dc, pb=pb: e.tensor_copy(hT[:, dc, :], pb[:]), reads=pk, writes=[f"H{dc}"])

            bfm, bfk = big[12], [f"a12_{q}" for q in range(4)]
            gfm, gfk = big[13], [f"a13_{q}" for q in range(4)]
            for part, (dst, dk_) in enumerate(((bfm, bfk), (gfm, gfk)) if dbg >= 1 else ()):
                pb, pk = proj_bank()
                for kc in range(16):
                    P.op("pe", lambda e, kc=kc, part=part, pb=pb: e.matmul(pb[0:8, :], wba[:, kc, part * 8:(part + 1) * 8],
                                                                          hT[:, kc, :], start=(kc == 0), stop=(kc == 15)),
                         reads=CK + [f"H{kc}"], writes=pk)
                if part == 0:
                    P.op("act", lambda e, pb=pb, dst=dst: e.activation(dst[0:8, :], pb[0:8, :], AF.Sigmoid), reads=pk, writes=dk_)
                else:
                    P.op("act", lambda e, pb=pb, dst=dst: e.activation(dst[0:8, :], pb[0:8, :], AF.Exp, bias=dtb[:, 0:1]),
                         reads=pk + CK, writes=dk_)
                    P.op("act", lambda e, dst=dst: e.activation(dst[0:8, :], dst[0:8, :], AF.Ln, bias=1.0), reads=dk_, writes=dk_)
                    P.op("dve", lambda e, dst=dst: e.tensor_scalar(dst[0:8, :], dst[0:8, :], nega[:, 0:1], None, ALU.mult),
                         reads=dk_ + ["nega"], writes=dk_)
            for tb in range(4 if dbg >= 1 else 0):
                tq, tqk = PQ(3 + tb % 2, 0)
                P.op("pe", lambda e, tb=tb, tq=tq: e.transpose(tq[:, 0:8], bfm[0:8, tb * 128:(tb + 1) * 128], ident[0:8, 0:8]),
                     reads=CK + bfk, writes=tqk)
                P.op("pe", lambda e, tb=tb, tq=tq: e.transpose(tq[:, 8:16], gfm[0:8, tb * 128:(tb + 1) * 128], ident[0:8, 0:8]),
                     reads=CK + gfk, writes=tqk)
                P.op("dve", lambda e, tb=tb, tq=tq: e.tensor_copy(toks[:, 0:2, tb, :], tq[:, 0:16].rearrange("p (a h) -> p a h", a=2)),
                     reads=tqk, writes=["toks"])
                tq2, tqk2 = PQ(3 + tb % 2, 1)
                P.op("pe", lambda e, tb=tb, tq2=tq2: e.matmul(tq2[:, 0:8], tribd[:], toks[:, 1, tb, :], start=True, stop=True),
                     reads=CK + ["toks"], writes=tqk2)
                P.op("pe", lambda e, tb=tb, tq2=tq2: e.matmul(tq2[:, 8:16], bdm[:], toks[:, 1, tb, :], start=True, stop=True),
                     reads=CK + ["toks"], writes=tqk2)
                P.op("dve", lambda e, tb=tb, tq2=tq2: e.tensor_copy(toks[:, 2:4, tb, :], tq2[:, 0:16].rearrange("p (a h) -> p a h", a=2)),
                     reads=tqk2, writes=["toks"])
            P.op("act", lambda e: e.activation(toks[:, 4, :, :], toks[:, 2, :, :], AF.Exp), reads=["toks"], writes=["toks"])
            P.op("dve", lambda e: e.tensor_tensor(toks[:, 4, :, :], toks[:, 4, :, :], toks[:, 0, :, :], ALU.mult),
                 reads=["toks"], writes=["toks"])
            P.op("dve", lambda e: e.tensor_tensor(toks[:, 5, :, :], toks[:, 3, :, :], toks[:, 2, :, :], ALU.subtract),
                 reads=["toks"], writes=["toks"])
            P.op("act", lambda e: e.activation(toks[:, 5, :, :], toks[:, 5, :, :], AF.Exp), reads=["toks"], writes=["toks"])

            def fin(hp, part, hh):
                c_ = (hp % 2) * 8 + part * 2 + hh
                return bufB[:, c_, :], f"B{c_}"

            def project_gen(expect, act_tile, act_pref, n=TT):
                wt, wk = ws_use(expect)
                pb, pk = proj_bank()
                for kc in range(16):
                    P.op("pe", lambda e, kc=kc, wt=wt, pb=pb: e.matmul(pb[:, 0:n], wt[:, kc, :], act_tile[:, kc, 0:n],
                                                                      start=(kc == 0), stop=(kc == 15)),
                         reads=[wk, f"{act_pref}{kc}"], writes=pk)
                    if kc % 4 == 3:
                        yield
                return pb, pk

            def proj_stage(hp):
                for part in range(4):
                    for hh in range(2):
                        h = 2 * hp + hh
                        pb, pk = yield from project_gen(("w_in", 0, part * 1024 + hp * 256 + hh * 128), hT, "H")
                        dstw, dkey = fin(hp, part, hh)
                        dst = f32(dstw)
                        if part == 3:
                            P.op("act", lambda e, pb=pb, dstw=dstw: e.copy(dstw, pb[:]), reads=pk, writes=[dkey])
                            yield
                            continue
                        ch = part * 8 + h
                        r = raw[(part * 2 + hh) % 2]
                        rk_ = f"raw{(part * 2 + hh) % 2}"
                        P.op("act", lambda e, pb=pb, r=r: e.copy(r[:, 3:515], pb[:]), reads=pk, writes=[rk_])
                        P.op("pool", lambda e, r=r, ch=ch: e.tensor_copy(r[:, 0:3], halo[:, ch, :]), reads=["halo"], writes=[rk_])
                        P.op("pool", lambda e, r=r, ch=ch: e.tensor_copy(halo[:, ch, :], r[:, 512:515]), reads=[rk_], writes=["halo"])
                        yield
                        eng = "dve" if hh == 0 else "pool"
                        P.op(eng, lambda e, r=r, ch=ch, dstw=dstw: e.tensor_scalar(dstw, r[:, 0:512], cw[:, ch, 0:1], None, ALU.mult),
                             reads=CK + [rk_], writes=[dkey])
                        for j in range(1, 4):
                            P.op("dve", lambda e, r=r, ch=ch, dst=dst, dstw=dstw, j=j: e.scalar_tensor_tensor(
                                dstw, r[:, j:j + 512], cw[:, ch, j:j + 1], dst, ALU.mult, ALU.add),
                                reads=CK + [rk_, dkey], writes=[dkey])
                            yield
                        P.op("act", lambda e, dst=dst, dstw=dstw: e.activation(dstw, dst, AF.Silu), reads=[dkey], writes=[dkey])
                        if part < 2:
                            sqb, sqk = raw[2 + hh][:, 0:512], [f"raw{2 + hh}"]
                            P.op("act", lambda e, dst=dst, sqb=sqb: e.activation(sqb, dst, AF.Square), reads=[dkey], writes=sqk)
                            yield
                            pb2, pk2 = proj_bank()
                            P.op("pe", lambda e, pb2=pb2, sqb=sqb: e.matmul(pb2[:], ones[:], sqb, start=True, stop=True),
                                 reads=CK + sqk, writes=pk2)
                            yield
                            P.op("act", lambda e, pb2=pb2, sqb=sqb: e.activation(sqb, pb2[:], AF.Sqrt, bias=NORM_EPS),
                                 reads=pk2, writes=sqk)
                            yield
                            P.op("dve", lambda e, sqb=sqb: e.reciprocal(sqb, sqb), reads=sqk, writes=sqk)
                            yield
                            P.op(eng, lambda e, dst=dst, dstw=dstw, sqb=sqb: e.tensor_tensor(dstw, dst, sqb, ALU.mult),
                                 reads=sqk + [dkey], writes=[dkey])

            def units_stream(hp):
                qf = [f32(fin(hp, 0, hh)[0]) for hh in range(2)]
                kf = [f32(fin(hp, 1, hh)[0]) for hh in range(2)]
                vf = [f32(fin(hp, 2, hh)[0]) for hh in range(2)]
                zf = [f32(fin(hp, 3, hh)[0]) for hh in range(2)]
                qk_ = [fin(hp, 0, hh)[1] for hh in range(2)]
                kk_ = [fin(hp, 1, hh)[1] for hh in range(2)]
                vk_ = [fin(hp, 2, hh)[1] for hh in range(2)]
                zk_ = [fin(hp, 3, hh)[1] for hh in range(2)]
                for tb in range(4 if dbg >= 1.6 else 0):
                    wp = tb % 2
                    cs = slice(tb * 128, (tb + 1) * 128)
                    Rw, Rk = BQ(4 + 2 * wp, 0, 2)
                    W1, W1k = BQ(4 + 2 * wp, 2, 2)
                    W2, W2k = BQ(5 + 2 * wp, 0, 2)
                    W3, W3k = BQ(5 + 2 * wp, 2, 2)
                    sdec = small[:, 4 * wp:4 * wp + 4]
                    sdk = [f"sdec{wp}"]
                    P.op("pool", lambda e, Rw=Rw, tb=tb, hp=hp: e.tensor_tensor(
                        Rw.rearrange("p (a t) -> p a t", a=2), tribd[:].unsqueeze(1).to_broadcast([128, 2, 128]),
                        toks[:, 1, tb, 2 * hp:2 * hp + 2].unsqueeze(2).to_broadcast([128, 2, 128]), ALU.mult),
                        reads=CK + ["toks"], writes=Rk)
                    rb, rbk = PQ(7, 2 * wp, 2)
                    P.op("pe", lambda e, rb=rb, Rw=Rw: e.matmul(rb, ones[:], Rw, start=True, stop=True), reads=CK + Rk, writes=rbk)
                    yield
                    for hh in range(2):
                        h = 2 * hp + hh
                        P.op("dve", lambda e, hh=hh, h=h, tb=tb, rb=rb, W1=W1: e.tensor_scalar(
                            W1[:, hh * 128:(hh + 1) * 128], rb[:, hh * 128:(hh + 1) * 128], toks[:, 2, tb, h:h + 1], 0.0,
                            ALU.subtract, ALU.max), reads=rbk + ["toks"], writes=W1k)
                        P.op("dve", lambda e, hh=hh, h=h, tb=tb, rb=rb, W2=W2: e.tensor_scalar(
                            W2[:, hh * 128:(hh + 1) * 128], rb[:, hh * 128:(hh + 1) * 128], toks[:, 2, tb, h:h + 1], 0.0,
                            ALU.subtract, ALU.min), reads=rbk + ["toks"], writes=W2k)
                    yield
                    P.op("act", lambda e, W1=W1: e.activation(W1, W1, AF.Exp, scale=-1.0), reads=W1k, writes=W1k)
                    P.op("act", lambda e, W2=W2: e.activation(W2, W2, AF.Exp), reads=W2k, writes=W2k)
                    P.op("act", lambda e, W3=W3, rb=rb: e.activation(W3, rb, AF.Exp, bias=float(-0.5 * np.log(128.0))), reads=rbk, writes=W3k)
                    P.op("act", lambda e, rb=rb, sdec=sdec: e.activation(
                        sdec.rearrange("p (a c) -> p a c", a=2), rb.rearrange("p (a c t) -> p a c t", a=2, c=2)[:, :, :, 63], AF.Exp),
                        reads=rbk, writes=sdk)
                    P.op("pool", lambda e, W1=W1: e.tensor_tensor(W1.rearrange("p (a t) -> p a t", a=2), W1.rearrange("p (a t) -> p a t", a=2),
                                                                  msbd[:].unsqueeze(1).to_broadcast([128, 2, 128]), ALU.mult),
                         reads=CK + W1k, writes=W1k)
                    P.op("pool", lambda e, W2=W2: e.tensor_tensor(W2.rearrange("p (a t) -> p a t", a=2), W2.rearrange("p (a t) -> p a t", a=2),
                                                                  minclt[:].unsqueeze(1).to_broadcast([128, 2, 128]), ALU.mult),
                         reads=CK + W2k, writes=W2k)
                    def unit_gen(hh, up):
                        h = 2 * hp + hh
                        base = 8 + 3 * up
                        names = ["kbg", "kdec", "vb", "sz", "L", "LTs", "AT", "NTa", "NTb", "Pa", "Pb", "PTa"]
                        Tl = {}
                        for n_i, nm in enumerate(names):
                            Tl[nm] = BQ(base + n_i // 4, n_i % 4)
                        extra = ["PTb", "u", "wT", "qg", "vnew", "og"]
                        for n_i, nm in enumerate(extra):
                            Tl[nm] = BQ((n_i + 6 * up) // 4, (n_i + 6 * up) % 4)
                        Dm = W1[:, hh * 128:(hh + 1) * 128]; Dmk = [W1k[hh]]
                        DTm = W2[:, hh * 128:(hh + 1) * 128]; DTmk = [W2k[hh]]
                        egr = W3[:, hh * 128:(hh + 1) * 128]; egk = [W3k[hh]]
                        bt = toks[:, 0, tb, h:h + 1]; bg2 = toks[:, 4, tb, h:h + 1]; kd = toks[:, 5, tb, h:h + 1]
                        qb_, kb_, vb_, zb_ = qf[hh][:, cs], kf[hh][:, cs], vf[hh][:, cs], zf[hh][:, cs]
                        SK = [f"S{h}"]
                        Sh = Sst[:, h, :]
                        bA, bB = 3 + up, 5 + up
                        pt, ptk = PQ(bA, 0, 3)
                        P.op("pe", lambda e, pt=pt, kb_=kb_: e.transpose(pt[:, 0:128], kb_, ident[:]), reads=CK + [kk_[hh]], writes=[ptk[0]])
                        P.op("pe", lambda e, pt=pt, vb_=vb_: e.transpose(pt[:, 128:256], vb_, ident[:]), reads=CK + [vk_[hh]], writes=[ptk[1]])
                        P.op("pe", lambda e, pt=pt, zb_=zb_: e.transpose(pt[:, 256:384], zb_, ident[:]), reads=CK + [zk_[hh]], writes=[ptk[2]])
                        pk2_, pk2k = PQ(bB, 0, 2)
                        P.op("pe", lambda e, pk2_=pk2_, kb_=kb_: e.matmul(pk2_[:, 0:128], kb_, kb_, start=True, stop=True),
                             reads=[kk_[hh]], writes=[pk2k[0]])
                        P.op("pe", lambda e, pk2_=pk2_, kb_=kb_, qb_=qb_: e.matmul(pk2_[:, 128:256], kb_, qb_, start=True, stop=True),
                             reads=[kk_[hh], qk_[hh]], writes=[pk2k[1]])
                        yield
                        (Lt, Lk), (LTs, LTsk), (AT, ATk) = Tl["L"], Tl["LTs"], Tl["AT"]
                        P.op("dve", lambda e, Lt=Lt, pk2_=pk2_, bt=bt, Dm=Dm: e.scalar_tensor_tensor(
                            Lt, pk2_[:, 0:128], bt, Dm, ALU.mult, ALU.mult), reads=[pk2k[0], "toks"] + Dmk, writes=Lk)
                        (kbg, kbgk), (kdec, kdeck), (vb, vbk), (sz, szk) = Tl["kbg"], Tl["kdec"], Tl["vb"], Tl["sz"]
                        P.op("act", lambda e, pt=pt, kdec=kdec, kd=kd: e.activation(kdec, pt[:, 0:128], AF.Copy, scale=kd),
                             reads=[ptk[0], "toks"], writes=kdeck)
                        yield
                        plt, pltk = PQ(bB, 0)
                        P.op("pe", lambda e, plt=plt, Lt=Lt: e.transpose(plt, Lt, ident[:]), reads=CK + Lk, writes=pltk)
                        P.op("dve", lambda e, AT=AT, pk2_=pk2_, DTm=DTm: e.tensor_tensor(AT, pk2_[:, 128:256], DTm, ALU.mult),
                             reads=[pk2k[1]] + DTmk, writes=ATk)
                        P.op("act", lambda e, pt=pt, sz=sz: e.activation(sz, pt[:, 256:384], AF.Silu), reads=[ptk[2]], writes=szk)
                        yield
                        NT = [Tl["NTa"], Tl["NTb"]]
                        Pm = [Tl["Pa"], Tl["Pb"]]
                        PTm = [Tl["PTa"], Tl["PTb"]]
                        P.op("dve", lambda e, plt=plt, nt=NT[0][0]: e.scalar_tensor_tensor(nt, plt, -1.0, ident[:], ALU.mult, ALU.add),
                             reads=CK + pltk, writes=NT[0][1])
                        P.op("act", lambda e, plt=plt, LTs=LTs: e.copy(LTs, plt), reads=pltk, writes=LTsk)
                        yield
                        pq1, pq1k = PQ(bB, 3)
                        pq2, pq2k = PQ(bA, 3)
                        P.op("pe", lambda e, pq1=pq1, LTs=LTs, Lt=Lt: e.matmul(pq1, LTs, Lt, start=True, stop=True),
                             reads=LTsk + Lk, writes=pq1k)
                        P.op("pe", lambda e, pq2=pq2, LTs=LTs, Lt=Lt: e.matmul(pq2, Lt, LTs, start=True, stop=True),
                             reads=LTsk + Lk, writes=pq2k)
                        yield
                        P.op("dve", lambda e, pq1=pq1, d=Pm[0][0]: e.tensor_copy(d, pq1), reads=pq1k, writes=Pm[0][1])
                        P.op("act", lambda e, pq2=pq2, d=PTm[0][0]: e.copy(d, pq2), reads=pq2k, writes=PTm[0][1])
                        yield
                        cur = 0
                        for lev in range(5):
                            Pc, Pck = Pm[lev % 2]
                            PTc, PTck = PTm[lev % 2]
                            ntc, ntck = NT[cur]
                            ntn, ntnk = NT[1 - cur]
                            pu, puk = PQ(bB, 0)
                            P.op("pe", lambda e, pu=pu, Pc=Pc, ntc=ntc: e.matmul(pu, Pc, ntc, start=True, stop=True),
                                 reads=Pck + ntck, writes=puk)
                            if lev < 4:
                                Pn, Pnk = Pm[(lev + 1) % 2]
                                PTn, PTnk = PTm[(lev + 1) % 2]
                                P.op("pe", lambda e, pq1=pq1, PTc=PTc, Pc=Pc: e.matmul(pq1, PTc, Pc, start=True, stop=True),
                                     reads=PTck + Pck, writes=pq1k)
                                P.op("pe", lambda e, pq2=pq2, PTc=PTc, Pc=Pc: e.matmul(pq2, Pc, PTc, start=True, stop=True),
                                     reads=PTck + Pck, writes=pq2k)
                            yield
                            P.op("dve", lambda e, pu=pu, ntc=ntc, ntn=ntn: e.tensor_tensor(ntn, pu, ntc, ALU.add),
                                 reads=puk + ntck, writes=ntnk)
                            cur = 1 - cur
                            if lev < 4:
                                P.op("act", lambda e, pq2=pq2, PTn=PTn: e.copy(PTn, pq2), reads=pq2k, writes=PTnk)
                                P.op("dve", lambda e, pq1=pq1, Pn=Pn: e.tensor_copy(Pn, pq1), reads=pq1k, writes=Pnk)
                            if lev == 1:
                                P.op("dve", lambda e, pt=pt, kbg=kbg, bg2=bg2: e.tensor_scalar(kbg, pt[:, 0:128], bg2, None, ALU.mult),
                                     reads=[ptk[0], "toks"], writes=kbgk)
                            if lev == 2:
                                P.op("dve", lambda e, pt=pt, vb=vb, bt=bt: e.tensor_scalar(vb, pt[:, 128:256], bt, None, ALU.mult),
                                     reads=[ptk[1], "toks"], writes=vbk)
                            yield
                        TTm, TTk = NT[cur]
                        (u_, uk), (wT, wTk), (qg, qgk), (vnew, vnk), (og, ogk) = Tl["u"], Tl["wT"], Tl["qg"], Tl["vnew"], Tl["og"]
                        P.op("pe", lambda e, pq1=pq1, TTm=TTm, vb=vb: e.matmul(pq1, TTm, vb, start=True, stop=True),
                             reads=TTk + vbk, writes=pq1k)
                        P.op("pe", lambda e, pq2=pq2, TTm=TTm, kbg=kbg: e.matmul(pq2, kbg, TTm, start=True, stop=True),
                             reads=TTk + kbgk, writes=pq2k)
                        P.op("pool", lambda e, qg=qg, qb_=qb_, egr=egr: e.tensor_tensor(qg, qb_, egr, ALU.mult),
                             reads=[qk_[hh]] + egk, writes=qgk)
                        yield
                        P.op("act", lambda e, pq1=pq1, u_=u_: e.copy(u_, pq1), reads=pq1k, writes=uk)
                        P.op("dve", lambda e, pq2=pq2, wT=wT: e.tensor_copy(wT, pq2), reads=pq2k, writes=wTk)
                        yield
                        po, pok = PQ(2 if up == 0 else 7, 0)
                        pws, pwsk = PQ(bB, 1)
                        pds, pdsk = PQ(bB, 2)
                        for c in range(2):
                            rs = slice(64 * c, 64 * c + 64)
                            P.op("pe", lambda e, pws=pws, wT=wT, rs=rs, Sh=Sh: e.matmul(pws[rs, :], wT[:, rs], Sh, start=True, stop=True),
                                 reads=wTk + SK, writes=pwsk)
                            P.op("pe", lambda e, po=po, qg=qg, rs=rs, Sh=Sh: e.matmul(po[rs, :], qg[:, rs], Sh, start=True, stop=False),
                                 reads=qgk + SK, writes=pok)
                            yield
                            P.op("dve", lambda e, vnew=vnew, u_=u_, pws=pws, rs=rs: e.tensor_tensor(vnew[rs, :], u_[rs, :], pws[rs, :], ALU.subtract),
                                 reads=uk + pwsk, writes=vnk)
                            yield
                            P.op("pe", lambda e, po=po, AT=AT, rs=rs, vnew=vnew: e.matmul(po[rs, :], AT[rs, rs], vnew[rs, :], start=False, stop=True),
                                 reads=ATk + vnk, writes=pok)
                            P.op("pe", lambda e, pds=pds, kdec=kdec, rs=rs, vnew=vnew: e.matmul(pds, kdec[rs, :], vnew[rs, :], start=True, stop=True),
                                 reads=kdeck + vnk, writes=pdsk)
                            yield
                            P.op("dve", lambda e, Sh=Sh, sdec=sdec, hh=hh, c=c, pds=pds: e.scalar_tensor_tensor(
                                Sh, Sh, sdec[:, 2 * hh + c:2 * hh + c + 1], pds, ALU.mult, ALU.add),
                                reads=SK + sdk + pdsk, writes=SK)
                            yield
                        ss = small[:, 8 + up:9 + up]
                        ssk = [f"ss{up}"]
                        P.op("dve", lambda e, ss=ss: e.memset(ss, 0.0), writes=ssk)
                        P.op("act", lambda e, og=og, po=po, ss=ss: e.activation(og, po, AF.Square, accum_out=ss), reads=pok + ssk, writes=ogk + ssk)
                        yield
                        P.op("act", lambda e, ss=ss: e.activation(ss, ss, AF.Sqrt, scale=1.0 / 128.0, bias=NORM_EPS), reads=ssk, writes=ssk)
                        yield
                        P.op("dve", lambda e, ss=ss: e.reciprocal(ss, ss), reads=ssk, writes=ssk)
                        P.op("dve", lambda e, og=og, po=po, ss=ss: e.scalar_tensor_tensor(og, po, ss, normw[:], ALU.mult, ALU.mult),
                             reads=CK + pok + ssk, writes=ogk)
                        yield
                        P.op("pool", lambda e, og=og, sz=sz: e.tensor_tensor(og, og, sz, ALU.mult), reads=ogk + szk, writes=ogk)
                        yield
                        P.op("pe", lambda e, pws=pws, og=og: e.transpose(pws, og, ident[:]), reads=CK + ogk, writes=pwsk)
                        yield
                        P.op("act", lambda e, pws=pws, h=h, cs=cs: e.copy(bufA[:, h, cs], pws), reads=pwsk, writes=[f"A{h}"])

                    gens = [unit_gen(0, 0), unit_gen(1, 1)] if dbg >= 1.7 else []
                    while gens:
                        for g_ in list(gens):
                            try:
                                next(g_)
                            except StopIteration:
                                gens.remove(g_)
                        yield


            def lockstep(gs):
                gs = list(gs)
                while gs:
                    for g_ in list(gs):
                        try:
                            next(g_)
                        except StopIteration:
                            gs.remove(g_)

            if dbg >= 1.3:
                lockstep([proj_stage(0)])
                for hp in range(4):
                    lockstep([units_stream(hp)] + ([proj_stage(hp + 1)] if hp < 3 else []))

            pwv = bufB[:].rearrange("p c t -> p (c t)")[:, 0:2048].rearrange("p (g k d) -> p g k d", g=4, k=2)
            if dbg >= 3:
                P.dma("poolw", lambda e: e.dma_start(out=pwv, in_=poolw_d.rearrange("g (k p) d -> p g k d", p=128)),
                      writes=AK("B", range(4)), E="pool")
            for g in range(4 if dbg >= 3 else 0):
                win = 2 ** (g + 1)
                for cc in range(2):
                    pc = 2 * g + cc
                    pb, pk = project(("w_in", 0, 4112 + g * 256 + cc * 128), hT, "H")
                    r = raw[cc]; rk_ = f"raw{cc}"
                    P.op("act", lambda e, pb=pb, r=r: e.copy(r[:, 15:527], pb[:]), reads=pk, writes=[rk_])
                    P.op("pool", lambda e, r=r, pc=pc: e.tensor_copy(r[:, 0:15], phalo[:, pc, :]), reads=["phalo"], writes=[rk_])
                    P.op("pool", lambda e, r=r, pc=pc: e.tensor_copy(phalo[:, pc, :], r[:, 512:527]), reads=[rk_], writes=["phalo"])
                    src, srck = r, rk_
                    for lev in range(g + 1):
                        sh = 2 ** lev
                        lo = 2 * sh - 1
                        d_ = raw[2 + lev % 2]; dk_ = f"raw{2 + lev % 2}"
                        eng = "dve" if (lev + cc) % 2 == 0 else "pool"
                        P.op(eng, lambda e, d_=d_, src=src, lo=lo, sh=sh: e.tensor_tensor(d_[:, lo:527], src[:, lo:527], src[:, lo - sh:527 - sh], ALU.add),
                             reads=[srck], writes=[dk_])
                        src, srck = d_, dk_
                    dst = bufB[:, 8 + pc, :]
                    P.op("dve", lambda e, dst=dst, src=src, win=win, r=r: e.scalar_tensor_tensor(
                        dst, src[:, 15:527], 1.0 / win, r[:, 15:527], ALU.mult, ALU.subtract), reads=[srck, rk_], writes=[f"B{8 + pc}"])
                    if ti == 0:
                        P.op("dve", lambda e, dst=dst, src=src, g=g: e.tensor_tensor(dst[:, 0:16], src[:, 15:31], rc16[:, g, :], ALU.mult),
                             reads=CK + [srck, f"B{8 + pc}"], writes=[f"B{8 + pc}"])
                        P.op("dve", lambda e, dst=dst, r=r: e.tensor_tensor(dst[:, 0:16], f32(dst)[:, 0:16], r[:, 15:31], ALU.subtract),
                             reads=[rk_, f"B{8 + pc}"], writes=[f"B{8 + pc}"])
                for dch in range(2):
                    pb, pk = proj_bank()
                    for cc in range(2):
                        P.op("pe", lambda e, g=g, cc=cc, dch=dch, pb=pb: e.matmul(
                            pb[:], pwv[:, g, cc, dch * 128:(dch + 1) * 128], bufB[:, 8 + 2 * g + cc, :], start=(cc == 0), stop=(cc == 1)),
                            reads=AK("B", range(4)) + [f"B{8 + 2 * g + cc}"], writes=pk)
                    P.op("act", lambda e, g=g, dch=dch, pb=pb: e.activation(bufA[:, 8 + 2 * g + dch, :], pb[:], AF.Copy,
                                                                             scale=pscale[:, 2 * g + dch:2 * g + dch + 1]),
                         reads=CK + pk, writes=[f"A{8 + 2 * g + dch}"])

            for c in range(16 if dbg >= 4 else 0):
                pb, pk = project(("w_out", 0, c * 128), bufA, "A")
                P.op("dve", lambda e, c=c, pb=pb: e.scalar_tensor_tensor(bufB[:, c, :], f32(hT[:, c, :]), ALPHA, pb[:], ALU.mult, ALU.add),
                     reads=pk + [f"H{c}"], writes=[f"B{c}"])
            if dbg >= 4:
                layer_norm(bufB, "B", 0)

            if stop_after >= 2:
                for c in range(16):
                    pb, pk = project(("xq_w", 0, c * 128), hT, "H")
                    if c % 2:
                        P.op("act", lambda e, c=c, pb=pb: e.copy(bufA[:, c, :], pb[:]), reads=pk, writes=[f"A{c}"])
                    else:
                        P.op("dve", lambda e, c=c, pb=pb: e.tensor_copy(bufA[:, c, :], pb[:]), reads=pk, writes=[f"A{c}"])
                sc = 512.0 ** -0.5
                for hd in range(4):
                    PTh, PThk = big[4 + hd % 2], [f"a{4 + hd % 2}_{q}" for q in range(4)]
                    PTh2, PTh2k = big[6 + hd % 2], [f"a{6 + hd % 2}_{q}" for q in range(4)]
                    PTs = [(PTh, PThk), (PTh2, PTh2k)]
                    for qb in range(4):
                        u2 = (hd * 4 + qb) % 2
                        psc, psck = PQ(3 + u2, 0, 2)
                        for dc in range(4):
                            P.op("pe", lambda e, psc=psc, hd=hd, dc=dc, qb=qb: e.matmul(
                                psc, bufA[:, 4 * hd + dc, qb * 128:(qb + 1) * 128], kxT[:, 4 * hd + dc, :], start=(dc == 0), stop=(dc == 3)),
                                reads=[f"A{4 * hd + dc}", "kx"], writes=psck)
                        mx = small[:, 10 + u2:11 + u2]; mxk = [f"mx{u2}"]
                        sm = small[:, 12 + u2:13 + u2]; smk = [f"sm{u2}"]
                        pe_, pek = BQ(8 + u2, 0, 2)
                        P.op("dve", lambda e, mx=mx, psc=psc: e.reduce_max(mx, psc, AX.X), reads=psck, writes=mxk)
                        P.op("dve", lambda e, mx=mx: e.tensor_scalar(mx, mx, -sc, None, ALU.mult), reads=mxk, writes=mxk)
                        P.op("dve", lambda e, sm=sm: e.memset(sm, 0.0), writes=smk)
                        P.op("act", lambda e, pe_=pe_, psc=psc, mx=mx, sm=sm: e.activation(pe_, psc, AF.Exp, bias=mx, scale=sc, accum_out=sm),
                             reads=psck + mxk + smk, writes=pek + smk)
                        P.op("dve", lambda e, sm=sm: e.reciprocal(sm, sm), reads=smk, writes=smk)
                        P.op("pool", lambda e, pe_=pe_, sm=sm: e.tensor_scalar(pe_, pe_, sm, None, ALU.mult), reads=pek + smk, writes=pek)
                        ptt, pttk = PQ(3 + u2, 2, 2)
                        for mb in range(2):
                            P.op("pe", lambda e, ptt=ptt, pe_=pe_, mb=mb: e.transpose(ptt[:, mb * 128:(mb + 1) * 128], pe_[:, mb * 128:(mb + 1) * 128], ident[:]),
                                 reads=CK + pek, writes=pttk)
                        for mb in range(2):
                            if mb:
                                P.op("act", lambda e, ptt=ptt, mb=mb, qb=qb, d=PTs[mb][0]: e.copy(d[:, qb * 128:(qb + 1) * 128], ptt[:, mb * 128:(mb + 1) * 128]),
                                     reads=pttk, writes=[PTs[mb][1][qb]])
                            else:
                                P.op("dve", lambda e, ptt=ptt, mb=mb, qb=qb, d=PTs[mb][0]: e.tensor_copy(d[:, qb * 128:(qb + 1) * 128], ptt[:, mb * 128:(mb + 1) * 128]),
                                     reads=pttk, writes=[PTs[mb][1][qb]])
                    for dhc in range(4):
                        pb, pk = proj_bank()
                        for mb in range(2):
                            P.op("pe", lambda e, pb=pb, mb=mb, hd=hd, dhc=dhc, d=PTs[mb][0]: e.matmul(
                                pb[:], f32(vx[:, mb, hd * 512 + dhc * 128: hd * 512 + (dhc + 1) * 128]), d[:], start=(mb == 0), stop=(mb == 1)),
                                reads=["vx"] + PTs[mb][1], writes=pk)
                        oc = 4 * hd + dhc
                        if dhc % 2:
                            P.op("act", lambda e, oc=oc, pb=pb: e.copy(bufB[:, oc, :], pb[:]), reads=pk, writes=[f"B{oc}"])
                        else:
                            P.op("dve", lambda e, oc=oc, pb=pb: e.tensor_copy(bufB[:, oc, :], pb[:]), reads=pk, writes=[f"B{oc}"])
                for c in range(16):
                    pb, pk = project(("xo_w", 0, c * 128), bufB, "B")
                    P.op("dve", lambda e, c=c, pb=pb: e.scalar_tensor_tensor(bufA[:, c, :], f32(hT[:, c, :]), ALPHA, pb[:], ALU.mult, ALU.add),
                         reads=pk + [f"H{c}"], writes=[f"A{c}"])
                layer_norm(bufA, "A", 1)

            if stop_after >= 3:
                for g in range(4):
                    for j in range(16):
                        pb, pk = project(("w_up", 0, g * 2048 + j * 128), hT, "H")
                        rl, rlk = big[8 + j % 4], [f"a{8 + j % 4}_{q}" for q in range(4)]
                        P.op("act", lambda e, pb=pb, rl=rl: e.activation(rl[:], pb[:], AF.Relu), reads=pk, writes=rlk)
                        if j % 2:
                            P.op("act", lambda e, j=j, rl=rl: e.activation(bufB[:, j, :], rl[:], AF.Square), reads=rlk, writes=[f"B{j}"])
                        else:
                            P.op("dve", lambda e, j=j, rl=rl: e.tensor_tensor(bufB[:, j, :], rl[:], rl[:], ALU.mult), reads=rlk, writes=[f"B{j}"])
                    for j in range(16):
                        pb, pk = project(("w_down", g * 2048, j * 128), bufB, "B")
                        if g == 0:
                            P.op("dve", lambda e, j=j, pb=pb: e.scalar_tensor_tensor(bufA[:, j, :], f32(hT[:, j, :]), ALPHA, pb[:], ALU.mult, ALU.add),
                                 reads=pk + [f"H{j}"], writes=[f"A{j}"])
                        elif g < 3:
                            P.op("dve", lambda e, j=j, pb=pb: e.tensor_tensor(bufA[:, j, :], pb[:], f32(bufA[:, j, :]), ALU.add),
                                 reads=pk + [f"A{j}"], writes=[f"A{j}"])
                        else:
                            P.op("dve", lambda e, j=j, pb=pb: e.tensor_tensor(bufA[:, j, :], pb[:], f32(bufA[:, j, :]), ALU.add),
                                 reads=pk + [f"A{j}"], writes=[f"A{j}"])
                layer_norm(bufA, "A", 2)

            for tb in range(4):
                obig = (8, 9, 10, 11) if tb % 2 == 0 else (12, 13, 0, 1)
                for dq in range(4):
                    pb, pk = proj_bank()
                    for d4 in range(4):
                        dc = dq * 4 + d4
                        P.op("pe", lambda e, pb=pb, d4=d4, dc=dc, tb=tb: e.transpose(pb[:, d4 * 128:(d4 + 1) * 128], f32(hT[:, dc, tb * 128:(tb + 1) * 128]), ident[:]),
                             reads=CK + [f"H{dc}"], writes=pk)
                    dsto = big[obig[dq]]
                    dstk = [f"a{obig[dq]}_{q}" for q in range(4)]
                    if dq % 2:
                        P.op("act", lambda e, pb=pb, dsto=dsto: e.copy(dsto[:], pb[:]), reads=pk, writes=dstk)
                    else:
                        P.op("dve", lambda e, pb=pb, dsto=dsto: e.tensor_copy(dsto[:], pb[:]), reads=pk, writes=dstk)
                    t = P.dma(f"out{tb % 2}_{dq}", lambda e, tb=tb, dq=dq, dsto=dsto, t0=t0: e.dma_start(
                        out=out_d[t0 + tb * 128:t0 + (tb + 1) * 128, dq * 512:(dq + 1) * 512], in_=dsto[:]), reads=dstk)
                    out_toks.append(t)
        P.wait_tokens("sp", out_toks)
        P.emit()
    return nc


def host_inputs(inp, b, ntiles=4):
    T = ntiles * TT
    f = lambda a: np.ascontiguousarray(a, dtype=np.float32)
    m = {"x": f(inp["x"][b, :T]), "mem": f(inp["mem"][b])}
    for k in ("w_in", "w_out", "xq_w", "xk_w", "xv_w", "xo_w", "w_up", "w_down", "pool_w"):
        m[k] = f(inp[k][0])
    m["cw"] = f(np.asarray(inp["conv_w"][0]).T.reshape(24, 128, 4).transpose(1, 0, 2))
    m["alog"] = f(np.asarray(inp["a_log"][0]).reshape(8, 1))
    m["dtb"] = f(np.asarray(inp["dt_bias"][0]).reshape(8, 1))
    m["normw"] = f(np.broadcast_to(np.asarray(inp["gdn_norm_w"][0])[None, :], (128, 128)))
    m["pscale"] = f(np.asarray(inp["pool_scale"][0]).reshape(8, 128).T)
    lnp = np.stack([np.asarray(inp[k][0]).reshape(16, 128).T for k in ("ln1_g", "ln1_b", "ln2_g", "ln2_b", "ln3_g", "ln3_b")], axis=1)
    m["lnp"] = f(lnp)
    m.update(host_consts())
    return m


_NC_CACHE = {}


def kernel(**inputs):
    inp = {k: np.asarray(v) for k, v in inputs.items()}
    if "nc" not in _NC_CACHE:
        _NC_CACHE["nc"] = build(4, 3)
    nc = _NC_CACHE["nc"]
    in_maps = [host_inputs(inp, b) for b in range(8)]
    res = run_bass_kernel_spmd(nc, in_maps, core_ids=list(range(8)))
    out = np.stack([np.asarray(res.results[b]["out"]) for b in range(8)], axis=0)
    return out.astype(np.float32)
```

```python
import numpy as np
from contextlib import ExitStack
import concourse.bass as bass
import concourse.mybir as mybir
from concourse.bass_utils import run_bass_kernel_spmd

F32 = mybir.dt.float32
F32R = mybir.dt.float32r
AF = mybir.ActivationFunctionType
ALU = mybir.AluOpType
AX = mybir.AxisListType

EPOCH = 8000
D = 2048
SEQ = 2048
TT = 512
NDC = 16
MEM = 256
DFF = 8192
INC = 5136
ALPHA = 2.0 ** 0.25
LN_EPS = 1e-5
NORM_EPS = 1e-6
NSLOT = 3
NBIG = 14


class Prog:
    COMPUTE = ("pe", "act", "dve", "pool")

    def __init__(self, nc, stack):
        self.nc = nc
        self.stack = stack
        self.ops = {e: [] for e in ("pe", "act", "dve", "pool", "sp")}
        self.count = {e: 0 for e in self.COMPUTE}
        self.sems = {}
        self.known = {e: {} for e in self.ops}
        self.last_write = {}
        self.readers = {}
        self.dma_count = {}

    def _sem(self, name):
        if name not in self.sems:
            self.sems[name] = self.stack.enter_context(self.nc.semaphore(name))
        return self.sems[name]

    def _phys(self, cname, v):
        if cname in self.COMPUTE:
            return self._sem(f"s_{cname}_{(v - 1) // EPOCH}"), (v - 1) % EPOCH + 1
        return self._sem(cname), v

    def _collect(self, E, reads, writes):
        deps = []
        for k in reads:
            t = self.last_write.get(k)
            if t is not None:
                deps.append(t)
        for k in writes:
            t = self.last_write.get(k)
            if t is not None:
                deps.append(t)
            deps.extend(self.readers.get(k, ()))
        deps.sort(key=lambda t: -t[1])
        kn = self.known[E]
        waits = {}
        for (cname, v, snap) in deps:
            if cname == E and E == "pe":
                continue
            if kn.get(cname, 0) >= v:
                continue
            waits[cname] = max(waits.get(cname, 0), v)
            kn[cname] = v
            for kk, vv in snap.items():
                if kn.get(kk, 0) < vv:
                    kn[kk] = vv
        return [self._phys(c, v) for c, v in waits.items()]

    @staticmethod
    def _excl(reads, writes):
        pr = [k for k in reads if k[0] == "p" and k[1:].isdigit()]
        if pr:
            reads = [k for k in reads if k not in pr]
            writes = list(writes) + pr
        return reads, writes

    def op(self, E, fn, reads=(), writes=()):
        reads, writes = self._excl(reads, writes)
        waits = self._collect(E, reads, writes)
        idx = self.count[E]
        self.count[E] += 1
        snap = dict(self.known[E])
        snap[E] = idx + 1
        tok = (E, idx + 1, snap)
        sem, _ = self._phys(E, idx + 1)
        self.ops[E].append((waits, fn, sem, 1))
        self._record(tok, reads, writes)
        return tok

    def dma(self, slot, fn, reads=(), writes=(), E="sp"):
        waits = self._collect(E, reads, writes)
        cname = "dma_" + E + "_" + slot
        self.dma_count[cname] = self.dma_count.get(cname, 0) + 16
        tok = (cname, self.dma_count[cname], dict(self.known[E]))
        self.ops[E].append((waits, fn, self._sem(cname), 16))
        self._record(tok, reads, writes)
        return tok

    def _record(self, tok, reads, writes):
        for k in reads:
            self.readers.setdefault(k, []).append(tok)
        for k in writes:
            self.last_write[k] = tok
            self.readers[k] = []

    def wait_tokens(self, E, toks):
        kn = self.known[E]
        waits = []
        for (cname, v, snap) in toks:
            if kn.get(cname, 0) >= v:
                continue
            kn[cname] = v
            waits.append(self._phys(cname, v))
        self.ops[E].append((waits, None, None, 0))

    def emit(self):
        nc = self.nc
        engs = {"pe": "tensor", "act": "scalar", "dve": "vector", "pool": "gpsimd", "sp": "sync"}
        with nc.Block() as block:
            for E, attr in engs.items():
                ops = self.ops[E]

                def body(eng, ops=ops):
                    for (waits, fn, sem, n) in ops:
                        for (s, v) in waits:
                            eng.wait_ge(s, v)
                        if fn is not None:
                            ins = fn(eng)
                            ins.then_inc(sem, n)

                getattr(block, attr)(body)


def host_consts():
    i = np.arange(128)
    same = (i[:, None] // 64) == (i[None, :] // 64)
    c = {}
    c["ident"] = np.eye(128, dtype=np.float32)
    c["ones"] = np.ones((128, 128), np.float32)
    c["avg"] = np.full((128, 128), 1.0 / D, np.float32)
    c["tribd"] = (same & (i[:, None] <= i[None, :])).astype(np.float32)
    c["bd"] = same.astype(np.float32)
    c["msbd"] = (same & (i[:, None] > i[None, :])).astype(np.float32)
    c["minclt"] = (same & (i[:, None] <= i[None, :])).astype(np.float32) * np.float32(128.0 ** -0.5)
    rc = np.zeros((128, 4, 16), np.float32)
    for g, win in enumerate((2, 4, 8, 16)):
        rc[:, g, :] = 1.0 / np.minimum(np.arange(16) + 1, win).astype(np.float32)
    c["rc16"] = rc
    return c


def tile_slabs(stop_after=3):
    L = []
    for hp in range(4):
        for part in range(4):
            for hh in range(2):
                L.append(("w_in", 0, part * 1024 + hp * 256 + hh * 128))
    for g in range(4):
        for cc in range(2):
            L.append(("w_in", 0, 4112 + g * 256 + cc * 128))
    for c in range(16):
        L.append(("w_out", 0, c * 128))
    if stop_after >= 2:
        for c in range(16):
            L.append(("xq_w", 0, c * 128))
        for c in range(16):
            L.append(("xo_w", 0, c * 128))
    if stop_after >= 3:
        for g in range(4):
            for j in range(16):
                L.append(("w_up", 0, g * 2048 + j * 128))
            for j in range(16):
                L.append(("w_down", g * 2048, j * 128))
    return L


def build(ntiles=4, stop_after=3, dbg=9):
    nc = bass.Bass("TRN2", target_bir_lowering=False)
    T = ntiles * TT

    def din(name, shape):
        return nc.dram_tensor(name, list(shape), F32, kind="ExternalInput").ap()

    x_d = din("x", [T, D])
    mem_d = din("mem", [MEM, D])
    W = {"w_in": din("w_in", [D, INC]), "w_out": din("w_out", [D, D]), "xq_w": din("xq_w", [D, D]),
         "xk_w": din("xk_w", [D, D]), "xv_w": din("xv_w", [D, D]), "xo_w": din("xo_w", [D, D]),
         "w_up": din("w_up", [D, DFF]), "w_down": din("w_down", [DFF, D])}
    poolw_d = din("pool_w", [4, 256, 256])
    cw_d = din("cw", [128, 24, 4])
    alog_d = din("alog", [8, 1])
    dtb_d = din("dtb", [8, 1])
    normw_d = din("normw", [128, 128])
    pscale_d = din("pscale", [128, 8])
    lnp_d = din("lnp", [128, 6, 16])
    cst_d = {k: din(k, v.shape) for k, v in host_consts().items()}
    out_d = nc.dram_tensor("out", [T, D], F32, kind="ExternalOutput").ap()

    with ExitStack() as st:
        P = Prog(nc, st)

        def sb(name, shape, dt=F32):
            return st.enter_context(nc.sbuf_tensor(name, list(shape), dt))

        wsl = [sb(f"wsl{i}", [128, 16, 128], F32R) for i in range(NSLOT)]
        hT = sb("hT", [128, 16, TT], F32R)
        bufA = sb("bufA", [128, 16, TT], F32R)
        bufB = sb("bufB", [128, 16, TT], F32R)
        kxT = sb("kxT", [128, 16, MEM], F32R)
        vx = sb("vx", [128, 2, D], F32R)
        big = [sb(f"big{i}", [128, 512], F32) for i in range(NBIG)]
        raw = [sb(f"raw{i}", [128, 528], F32) for i in range(4)]
        lnsq = [sb(f"lnsq{i}", [128, 512], F32R) for i in range(2)]
        Sst = sb("Sst", [128, 8, 128], F32)
        halo = sb("halo", [128, 24, 3], F32)
        phalo = sb("phalo", [128, 8, 15], F32)
        wba = sb("wba", [128, 16, 16], F32R)
        cw = sb("cw_s", [128, 24, 4])
        alog = sb("alog_s", [8, 1]); dtb = sb("dtb_s", [8, 1]); nega = sb("nega", [8, 1])
        normw = sb("normw_s", [128, 128])
        pscale = sb("pscale_s", [128, 8])
        lnp = sb("lnp_s", [128, 6, 16])
        ident = sb("ident_s", [128, 128]); ones = sb("ones_s", [128, 128])
        avg = sb("avg_s", [128, 128], F32R)
        bdm = sb("bd_s", [128, 128]); tribd = sb("tribd_s", [128, 128]); msbd = sb("msbd_s", [128, 128]); minclt = sb("minclt_s", [128, 128])
        rc16 = sb("rc16_s", [128, 4, 16])
        toks = sb("toks", [128, 6, 4, 8])
        small = sb("small", [128, 16])
        small2 = sb("small2", [128, 8])
        ps = [st.enter_context(nc.psum_tensor(f"ps{i}", [128, 512], F32)) for i in range(8)]

        def f32(ap):
            return ap.bitcast(F32)

        def AK(pref, cs):
            return [f"{pref}{c}" for c in cs]

        def BQ(i, j, n=1):
            return big[i][:, j * 128:(j + n) * 128], [f"a{i}_{q}" for q in range(j, j + n)]

        def PQ(b, j, n=1):
            return ps[b][:, j * 128:(j + n) * 128], [f"p{b}"] * n

        CK = ["const"]
        for (dst, src) in ((cw, cw_d), (alog, alog_d), (dtb, dtb_d), (normw, normw_d), (pscale, pscale_d),
                           (lnp, lnp_d), (ident, cst_d["ident"]), (ones, cst_d["ones"]), (tribd, cst_d["tribd"]), (bdm, cst_d["bd"]),
                           (msbd, cst_d["msbd"]), (minclt, cst_d["minclt"]), (rc16, cst_d["rc16"])):
            P.dma("const", lambda e, dst=dst, src=src: e.dma_start(out=dst[:], in_=src), writes=CK)
        P.dma("const", lambda e: e.dma_start(out=avg[:], in_=cst_d["avg"]), writes=CK, E="pool")
        P.dma("const", lambda e: e.dma_start(
            out=wba[:], in_=W["w_in"][:, 4096:4112].rearrange("(kc p) c -> p kc c", p=128)), writes=CK, E="pool")
        P.op("dve", lambda e: e.memset(Sst[:], 0.0), writes=AK("S", range(8)))
        P.op("dve", lambda e: e.memset(halo[:], 0.0), writes=["halo"])
        P.op("dve", lambda e: e.memset(phalo[:], 0.0), writes=["phalo"])
        P.op("act", lambda e: e.activation(nega[:], alog[:], AF.Exp), reads=CK, writes=["nega"])
        P.op("dve", lambda e: e.tensor_scalar(nega[:], nega[:], -1.0, None, ALU.mult), reads=["nega"], writes=["nega"])

        sched = []
        ws_state = {"issued": 0, "pos": 0}

        def slab_src(name, r0, c0):
            return W[name][r0:r0 + 2048, c0:c0 + 128].rearrange("(kc p) c -> p kc c", p=128)

        def ws_issue_upto(n):
            while ws_state["issued"] < min(n, len(sched)):
                j = ws_state["issued"]
                s = j % NSLOT
                src = slab_src(*sched[j])
                P.dma(f"w{s}", lambda e, s=s, src=src: e.dma_start(out=wsl[s][:], in_=src), writes=[f"w{s}"], E="pool")
                ws_state["issued"] += 1

        def ws_use(expect):
            i = ws_state["pos"]
            assert sched[i] == expect, (i, sched[i], expect)
            ws_issue_upto(i + NSLOT)
            ws_state["pos"] += 1
            return wsl[i % NSLOT], f"w{i % NSLOT}"

        if stop_after >= 2:
            for c in range(16):
                sched.append(("xk_w", 0, c * 128))
            for c in range(16):
                sched.append(("xv_w", 0, c * 128))
        for _ in range(ntiles):
            sched.extend(tile_slabs(stop_after)[:{0: 0, 1: 0, 2: 32, 3: 40}.get(dbg, 32 if dbg < 2 else 10 ** 6)])

        pj = {"i": 0}

        def proj_bank():
            b = pj["i"] % 2
            pj["i"] += 1
            return ps[b], [f"p{b}"]

        def project(expect, act_tile, act_pref, n=TT):
            wt, wk = ws_use(expect)
            pb, pk = proj_bank()
            for kc in range(16):
                P.op("pe", lambda e, kc=kc, wt=wt, pb=pb: e.matmul(pb[:, 0:n], wt[:, kc, :], act_tile[:, kc, 0:n],
                                                                  start=(kc == 0), stop=(kc == 15)),
                     reads=[wk, f"{act_pref}{kc}"], writes=pk)
            return pb, pk

        def layer_norm(y, ypref, li):
            mean_s, mk = big[0], [f"a0_{q}" for q in range(4)]
            rstd_s, rk = big[1], [f"a1_{q}" for q in range(4)]
            pb, pk = proj_bank()
            for dc in range(16):
                P.op("pe", lambda e, dc=dc: e.matmul(pb[:], avg[:], y[:, dc, :], start=(dc == 0), stop=(dc == 15)),
                     reads=CK + [f"{ypref}{dc}"], writes=pk)
            P.op("act", lambda e: e.copy(mean_s[:], pb[:]), reads=pk, writes=mk)
            pb2, pk2 = proj_bank()
            for dc in range(16):
                P.op("dve", lambda e, dc=dc: e.tensor_tensor(y[:, dc, :], f32(y[:, dc, :]), mean_s[:], ALU.subtract),
                     reads=mk + [f"{ypref}{dc}"], writes=[f"{ypref}{dc}"])
                sq_t = lnsq[dc % 2]
                sqk = [f"lnsq{dc % 2}"]
                P.op("act", lambda e, dc=dc, sq_t=sq_t: e.activation(sq_t[:], f32(y[:, dc, :]), AF.Square),
                     reads=[f"{ypref}{dc}"], writes=sqk)
                P.op("pe", lambda e, dc=dc, sq_t=sq_t: e.matmul(pb2[:], avg[:], sq_t[:],
                                                              start=(dc == 0), stop=(dc == 15)),
                     reads=CK + sqk, writes=pk2)
            P.op("act", lambda e: e.activation(rstd_s[:], pb2[:], AF.Sqrt, bias=LN_EPS), reads=pk2, writes=rk)
            P.op("dve", lambda e: e.reciprocal(rstd_s[:], rstd_s[:]), reads=rk, writes=rk)
            for dc in range(16):
                eng = "dve" if dc % 2 == 0 else "pool"
                P.op(eng, lambda e, dc=dc: e.tensor_tensor(y[:, dc, :], f32(y[:, dc, :]), rstd_s[:], ALU.mult),
                     reads=rk + [f"{ypref}{dc}"], writes=[f"{ypref}{dc}"])
                P.op("act", lambda e, dc=dc: e.activation(hT[:, dc, :], f32(y[:, dc, :]), AF.Identity,
                                                          scale=lnp[:, 2 * li, dc:dc + 1], bias=lnp[:, 2 * li + 1, dc:dc + 1]),
                     reads=CK + [f"{ypref}{dc}"], writes=[f"H{dc}"])

        PRO = stop_after >= 2
        memtok = bufB[:].rearrange("p c t -> p (c t)")
        for mb in range(2 if PRO else 0):
            P.dma(f"xin{mb}", lambda e, mb=mb: e.dma_start(out=memtok[:, mb * D:(mb + 1) * D],
                                                      in_=mem_d[mb * 128:(mb + 1) * 128, :]),
                  writes=AK("B", range(4 * mb, 4 * mb + 4)), E="pool")
        memT = bufA[:].rearrange("p c t -> p (c t)")
        for dc in range(16 if PRO else 0):
            tb_, tk_ = PQ(3 + dc % 2, 0, 2)
            for mb in range(2):
                P.op("pe", lambda e, dc=dc, mb=mb, tb_=tb_: e.transpose(
                    tb_[:, mb * 128:(mb + 1) * 128], f32(memtok[:, mb * D + dc * 128: mb * D + (dc + 1) * 128]), ident[:]),
                    reads=CK + AK("B", [4 * mb + dc // 4]), writes=tk_)
            P.op("act" if dc % 2 else "dve", (lambda e, dc=dc, tb_=tb_: e.copy(memT[:, dc * 256:(dc + 1) * 256], tb_))
                 if dc % 2 else (lambda e, dc=dc, tb_=tb_: e.tensor_copy(memT[:, dc * 256:(dc + 1) * 256], tb_)),
                 reads=tk_, writes=AK("A", [dc // 2]))
        for c in range(16 if PRO else 0):
            wt, wk = ws_use(("xk_w", 0, c * 128))
            pb, pk = proj_bank()
            for kc in range(16):
                P.op("pe", lambda e, kc=kc, wt=wt, pb=pb: e.matmul(pb[:, 0:256], wt[:, kc, :], memT[:, kc * 256:(kc + 1) * 256],
                                                                  start=(kc == 0), stop=(kc == 15)),
                     reads=[wk] + AK("A", [kc // 2]), writes=pk)
            P.op("act", lambda e, c=c, pb=pb: e.copy(kxT[:, c, :], pb[:, 0:256]), reads=pk, writes=["kx"])
        for c in range(16 if PRO else 0):
            wt, wk = ws_use(("xv_w", 0, c * 128))
            pb, pk = proj_bank()
            for mb in range(2):
                for kc in range(16):
                    P.op("pe", lambda e, kc=kc, mb=mb, wt=wt, pb=pb: e.matmul(
                        pb[:, mb * 128:(mb + 1) * 128], memT[:, kc * 256 + mb * 128: kc * 256 + (mb + 1) * 128], wt[:, kc, :],
                        start=(kc == 0), stop=(kc == 15)),
                        reads=[wk] + AK("A", [kc // 2]), writes=pk)
            P.op("dve", lambda e, c=c, pb=pb: e.tensor_copy(
                vx[:, :, c * 128:(c + 1) * 128], pb[:, 0:256].rearrange("p (m c) -> p m c", m=2)), reads=pk, writes=["vx"])

        out_toks = []
        unit_ctr = {"u": 0}

        for ti in range(ntiles):
            t0 = ti * TT
            xtok = bufB[:].rearrange("p c t -> p (c t)")
            for tb in range(4):
                P.dma(f"xin{tb}", lambda e, tb=tb, t0=t0: e.dma_start(out=xtok[:, tb * D:(tb + 1) * D],
                                                          in_=x_d[t0 + tb * 128: t0 + (tb + 1) * 128, :]),
                      writes=AK("B", range(4 * tb, 4 * tb + 4)), E="pool")
            for dc in range(16):
                pb, pk = proj_bank()
                for tb in range(4):
                    P.op("pe", lambda e, dc=dc, tb=tb, pb=pb: e.transpose(
                        pb[:, tb * 128:(tb + 1) * 128], f32(xtok[:, tb * D + dc * 128: tb * D + (dc + 1) * 128]), ident[:]),
                        reads=CK + AK("B", [4 * tb + dc // 4]), writes=pk)
                if dc % 2:
                    P.op("act", lambda e, dc=dc, pb=pb: e.copy(hT[:, dc, :], pb[:]), reads=pk, writes=[f"H{dc}"])
                else:
                    P.op("dve", lambda e, dc=dc, pb=pb: e.tensor_copy(hT[:, dc, :], pb[:]), reads=pk, writes=[f"H{dc}"])

            bfm, bfk = big[12], [f"a12_{q}" for q in range(4)]
            gfm, gfk = big[13], [f"a13_{q}" for q in range(4)]
            for part, (dst, dk_) in enumerate(((bfm, bfk), (gfm, gfk)) if dbg >= 1 else ()):
                pb, pk = proj_bank()
                for kc in range(16):
                    P.op("pe", lambda e, kc=kc, part=part, pb=pb: e.matmul(pb[0:8, :], wba[:, kc, part * 8:(part + 1) * 8],
                                                                          hT[:, kc, :], start=(kc == 0), stop=(kc == 15)),
                         reads=CK + [f"H{kc}"], writes=pk)
                if part == 0:
                    P.op("act", lambda e, pb=pb, dst=dst: e.activation(dst[0:8, :], pb[0:8, :], AF.Sigmoid), reads=pk, writes=dk_)
                else:
                    P.op("act", lambda e, pb=pb, dst=dst: e.activation(dst[0:8, :], pb[0:8, :], AF.Exp, bias=dtb[:, 0:1]),
                         reads=pk + CK, writes=dk_)
                    P.op("act", lambda e, dst=dst: e.activation(dst[0:8, :], dst[0:8, :], AF.Ln, bias=1.0), reads=dk_, writes=dk_)
                    P.op("dve", lambda e, dst=dst: e.tensor_scalar(dst[0:8, :], dst[0:8, :], nega[:, 0:1], None, ALU.mult),
                         reads=dk_ + ["nega"], writes=dk_)
            for tb in range(4 if dbg >= 1 else 0):
                tq, tqk = PQ(3 + tb % 2, 0)
                P.op("pe", lambda e, tb=tb, tq=tq: e.transpose(tq[:, 0:8], bfm[0:8, tb * 128:(tb + 1) * 128], ident[0:8, 0:8]),
                     reads=CK + bfk, writes=tqk)
                P.op("pe", lambda e, tb=tb, tq=tq: e.transpose(tq[:, 8:16], gfm[0:8, tb * 128:(tb + 1) * 128], ident[0:8, 0:8]),
                     reads=CK + gfk, writes=tqk)
                P.op("dve", lambda e, tb=tb, tq=tq: e.tensor_copy(toks[:, 0:2, tb, :], tq[:, 0:16].rearrange("p (a h) -> p a h", a=2)),
                     reads=tqk, writes=["toks"])
                tq2, tqk2 = PQ(3 + tb % 2, 1)
                P.op("pe", lambda e, tb=tb, tq2=tq2: e.matmul(tq2[:, 0:8], tribd[:], toks[:, 1, tb, :], start=True, stop=True),
                     reads=CK + ["toks"], writes=tqk2)
                P.op("pe", lambda e, tb=tb, tq2=tq2: e.matmul(tq2[:, 8:16], bdm[:], toks[:, 1, tb, :], start=True, stop=True),
                     reads=CK + ["toks"], writes=tqk2)
                P.op("dve", lambda e, tb=tb, tq2=tq2: e.tensor_copy(toks[:, 2:4, tb, :], tq2[:, 0:16].rearrange("p (a h) -> p a h", a=2)),
                     reads=tqk2, writes=["toks"])
            P.op("act", lambda e: e.activation(toks[:, 4, :, :], toks[:, 2, :, :], AF.Exp), reads=["toks"], writes=["toks"])
            P.op("dve", lambda e: e.tensor_tensor(toks[:, 4, :, :], toks[:, 4, :, :], toks[:, 0, :, :], ALU.mult),
                 reads=["toks"], writes=["toks"])
            P.op("dve", lambda e: e.tensor_tensor(toks[:, 5, :, :], toks[:, 3, :, :], toks[:, 2, :, :], ALU.subtract),
                 reads=["toks"], writes=["toks"])
            P.op("act", lambda e: e.activation(toks[:, 5, :, :], toks[:, 5, :, :], AF.Exp), reads=["toks"], writes=["toks"])

            def fin(hp, part, hh):
                c_ = (hp % 2) * 8 + part * 2 + hh
                return bufB[:, c_, :], f"B{c_}"

            def project_gen(expect, act_tile, act_pref, n=TT):
                wt, wk = ws_use(expect)
                pb, pk = proj_bank()
                for kc in range(16):
                    P.op("pe", lambda e, kc=kc, wt=wt, pb=pb: e.matmul(pb[:, 0:n], wt[:, kc, :], act_tile[:, kc, 0:n],
                                                                      start=(kc == 0), stop=(kc == 15)),
                         reads=[wk, f"{act_pref}{kc}"], writes=pk)
                    if kc % 4 == 3:
                        yield
                return pb, pk

            def proj_stage(hp):
                for part in range(4):
                    for hh in range(2):
                        h = 2 * hp + hh
                        pb, pk = yield from project_gen(("w_in", 0, part * 1024 + hp * 256 + hh * 128), hT, "H")
                        dstw, dkey = fin(hp, part, hh)
                        dst = f32(dstw)
                        if part == 3:
                            P.op("act", lambda e, pb=pb, dstw=dstw: e.copy(dstw, pb[:]), reads=pk, writes=[dkey])
                            yield
                            continue
                        ch = part * 8 + h
                        r = raw[(part * 2 + hh) % 2]
                        rk_ = f"raw{(part * 2 + hh) % 2}"
                        P.op("act", lambda e, pb=pb, r=r: e.copy(r[:, 3:515], pb[:]), reads=pk, writes=[rk_])
                        P.op("pool", lambda e, r=r, ch=ch: e.tensor_copy(r[:, 0:3], halo[:, ch, :]), reads=["halo"], writes=[rk_])
                        P.op("pool", lambda e, r=r, ch=ch: e.tensor_copy(halo[:, ch, :], r[:, 512:515]), reads=[rk_], writes=["halo"])
                        yield
                        eng = "dve" if hh == 0 else "pool"
                        P.op(eng, lambda e, r=r, ch=ch, dstw=dstw: e.tensor_scalar(dstw, r[:, 0:512], cw[:, ch, 0:1], None, ALU.mult),
                             reads=CK + [rk_], writes=[dkey])
                        for j in range(1, 4):
                            P.op("dve", lambda e, r=r, ch=ch, dst=dst, dstw=dstw, j=j: e.scalar_tensor_tensor(
                                dstw, r[:, j:j + 512], cw[:, ch, j:j + 1], dst, ALU.mult, ALU.add),
                                reads=CK + [rk_, dkey], writes=[dkey])
                            yield
                        P.op("act", lambda e, dst=dst, dstw=dstw: e.activation(dstw, dst, AF.Silu), reads=[dkey], writes=[dkey])
                        if part < 2:
                            sqb, sqk = raw[2 + hh][:, 0:512], [f"raw{2 + hh}"]
                            P.op("act", lambda e, dst=dst, sqb=sqb: e.activation(sqb, dst, AF.Square), reads=[dkey], writes=sqk)
                            yield
                            pb2, pk2 = proj_bank()
                            P.op("pe", lambda e, pb2=pb2, sqb=sqb: e.matmul(pb2[:], ones[:], sqb, start=True, stop=True),
                                 reads=CK + sqk, writes=pk2)
                            yield
                            P.op("act", lambda e, pb2=pb2, sqb=sqb: e.activation(sqb, pb2[:], AF.Sqrt, bias=NORM_EPS),
                                 reads=pk2, writes=sqk)
                            yield
                            P.op("dve", lambda e, sqb=sqb: e.reciprocal(sqb, sqb), reads=sqk, writes=sqk)
                            yield
                            P.op(eng, lambda e, dst=dst, dstw=dstw, sqb=sqb: e.tensor_tensor(dstw, dst, sqb, ALU.mult),
                                 reads=sqk + [dkey], writes=[dkey])

            def units_stream(hp):
                qf = [f32(fin(hp, 0, hh)[0]) for hh in range(2)]
                kf = [f32(fin(hp, 1, hh)[0]) for hh in range(2)]
                vf = [f32(fin(hp, 2, hh)[0]) for hh in range(2)]
                zf = [f32(fin(hp, 3, hh)[0]) for hh in range(2)]
                qk_ = [fin(hp, 0, hh)[1] for hh in range(2)]
                kk_ = [fin(hp, 1, hh)[1] for hh in range(2)]
                vk_ = [fin(hp, 2, hh)[1] for hh in range(2)]
                zk_ = [fin(hp, 3, hh)[1] for hh in range(2)]
                for tb in range(4 if dbg >= 1.6 else 0):
                    wp = tb % 2
                    cs = slice(tb * 128, (tb + 1) * 128)
                    Rw, Rk = BQ(4 + 2 * wp, 0, 2)
                    W1, W1k = BQ(4 + 2 * wp, 2, 2)
                    W2, W2k = BQ(5 + 2 * wp, 0, 2)
                    W3, W3k = BQ(5 + 2 * wp, 2, 2)
                    sdec = small[:, 4 * wp:4 * wp + 4]
                    sdk = [f"sdec{wp}"]
                    P.op("pool", lambda e, Rw=Rw, tb=tb, hp=hp: e.tensor_tensor(
                        Rw.rearrange("p (a t) -> p a t", a=2), tribd[:].unsqueeze(1).to_broadcast([128, 2, 128]),
                        toks[:, 1, tb, 2 * hp:2 * hp + 2].unsqueeze(2).to_broadcast([128, 2, 128]), ALU.mult),
                        reads=CK + ["toks"], writes=Rk)
                    rb, rbk = PQ(7, 2 * wp, 2)
                    P.op("pe", lambda e, rb=rb, Rw=Rw: e.matmul(rb, ones[:], Rw, start=True, stop=True), reads=CK + Rk, writes=rbk)
                    yield
                    for hh in range(2):
                        h = 2 * hp + hh
                        P.op("dve", lambda e, hh=hh, h=h, tb=tb, rb=rb, W1=W1: e.tensor_scalar(
                            W1[:, hh * 128:(hh + 1) * 128], rb[:, hh * 128:(hh + 1) * 128], toks[:, 2, tb, h:h + 1], 0.0,
                            ALU.subtract, ALU.max), reads=rbk + ["toks"], writes=W1k)
                        P.op("dve", lambda e, hh=hh, h=h, tb=tb, rb=rb, W2=W2: e.tensor_scalar(
                            W2[:, hh * 128:(hh + 1) * 128], rb[:, hh * 128:(hh + 1) * 128], toks[:, 2, tb, h:h + 1], 0.0,
                            ALU.subtract, ALU.min), reads=rbk + ["toks"], writes=W2k)
                    yield
                    P.op("act", lambda e, W1=W1: e.activation(W1, W1, AF.Exp, scale=-1.0), reads=W1k, writes=W1k)
                    P.op("act", lambda e, W2=W2: e.activation(W2, W2, AF.Exp), reads=W2k, writes=W2k)
                    P.op("act", lambda e, W3=W3, rb=rb: e.activation(W3, rb, AF.Exp, bias=float(-0.5 * np.log(128.0))), reads=rbk, writes=W3k)
                    P.op("act", lambda e, rb=rb, sdec=sdec: e.activation(
                        sdec.rearrange("p (a c) -> p a c", a=2), rb.rearrange("p (a c t) -> p a c t", a=2, c=2)[:, :, :, 63], AF.Exp),
                        reads=rbk, writes=sdk)
                    P.op("pool", lambda e, W1=W1: e.tensor_tensor(W1.rearrange("p (a t) -> p a t", a=2), W1.rearrange("p (a t) -> p a t", a=2),
                                                                  msbd[:].unsqueeze(1).to_broadcast([128, 2, 128]), ALU.mult),
                         reads=CK + W1k, writes=W1k)
                    P.op("pool", lambda e, W2=W2: e.tensor_tensor(W2.rearrange("p (a t) -> p a t", a=2), W2.rearrange("p (a t) -> p a t", a=2),
                                                                  minclt[:].unsqueeze(1).to_broadcast([128, 2, 128]), ALU.mult),
                         reads=CK + W2k, writes=W2k)
                    def unit_gen(hh, up):
                        h = 2 * hp + hh
                        base = 8 + 3 * up
                        names = ["kbg", "kdec", "vb", "sz", "L", "LTs", "AT", "NTa", "NTb", "Pa", "Pb", "PTa"]
                        Tl = {}
                        for n_i, nm in enumerate(names):
                            Tl[nm] = BQ(base + n_i // 4, n_i % 4)
                        extra = ["PTb", "u", "wT", "qg", "vnew", "og"]
                        for n_i, nm in enumerate(extra):
                            Tl[nm] = BQ((n_i + 6 * up) // 4, (n_i + 6 * up) % 4)
                        Dm = W1[:, hh * 128:(hh + 1) * 128]; Dmk = [W1k[hh]]
                        DTm = W2[:, hh * 128:(hh + 1) * 128]; DTmk = [W2k[hh]]
                        egr = W3[:, hh * 128:(hh + 1) * 128]; egk = [W3k[hh]]
                        bt = toks[:, 0, tb, h:h + 1]; bg2 = toks[:, 4, tb, h:h + 1]; kd = toks[:, 5, tb, h:h + 1]
                        qb_, kb_, vb_, zb_ = qf[hh][:, cs], kf[hh][:, cs], vf[hh][:, cs], zf[hh][:, cs]
                        SK = [f"S{h}"]
                        Sh = Sst[:, h, :]
                        bA, bB = 3 + up, 5 + up
                        pt, ptk = PQ(bA, 0, 3)
                        P.op("pe", lambda e, pt=pt, kb_=kb_: e.transpose(pt[:, 0:128], kb_, ident[:]), reads=CK + [kk_[hh]], writes=[ptk[0]])
                        P.op("pe", lambda e, pt=pt, vb_=vb_: e.transpose(pt[:, 128:256], vb_, ident[:]), reads=CK + [vk_[hh]], writes=[ptk[1]])
                        P.op("pe", lambda e, pt=pt, zb_=zb_: e.transpose(pt[:, 256:384], zb_, ident[:]), reads=CK + [zk_[hh]], writes=[ptk[2]])
                        pk2_, pk2k = PQ(bB, 0, 2)
                        P.op("pe", lambda e, pk2_=pk2_, kb_=kb_: e.matmul(pk2_[:, 0:128], kb_, kb_, start=True, stop=True),
                             reads=[kk_[hh]], writes=[pk2k[0]])
                        P.op("pe", lambda e, pk2_=pk2_, kb_=kb_, qb_=qb_: e.matmul(pk2_[:, 128:256], kb_, qb_, start=True, stop=True),
                             reads=[kk_[hh], qk_[hh]], writes=[pk2k[1]])
                        yield
                        (Lt, Lk), (LTs, LTsk), (AT, ATk) = Tl["L"], Tl["LTs"], Tl["AT"]
                        P.op("dve", lambda e, Lt=Lt, pk2_=pk2_, bt=bt, Dm=Dm: e.scalar_tensor_tensor(
                            Lt, pk2_[:, 0:128], bt, Dm, ALU.mult, ALU.mult), reads=[pk2k[0], "toks"] + Dmk, writes=Lk)
                        (kbg, kbgk), (kdec, kdeck), (vb, vbk), (sz, szk) = Tl["kbg"], Tl["kdec"], Tl["vb"], Tl["sz"]
                        P.op("act", lambda e, pt=pt, kdec=kdec, kd=kd: e.activation(kdec, pt[:, 0:128], AF.Copy, scale=kd),
                             reads=[ptk[0], "toks"], writes=kdeck)
                        yield
                        plt, pltk = PQ(bB, 0)
                        P.op("pe", lambda e, plt=plt, Lt=Lt: e.transpose(plt, Lt, ident[:]), reads=CK + Lk, writes=pltk)
                        P.op("dve", lambda e, AT=AT, pk2_=pk2_, DTm=DTm: e.tensor_tensor(AT, pk2_[:, 128:256], DTm, ALU.mult),
                             reads=[pk2k[1]] + DTmk, writes=ATk)
                        P.op("act", lambda e, pt=pt, sz=sz: e.activation(sz, pt[:, 256:384], AF.Silu), reads=[ptk[2]], writes=szk)
                        yield
                        NT = [Tl["NTa"], Tl["NTb"]]
                        Pm = [Tl["Pa"], Tl["Pb"]]
                        PTm = [Tl["PTa"], Tl["PTb"]]
                        P.op("dve", lambda e, plt=plt, nt=NT[0][0]: e.scalar_tensor_tensor(nt, plt, -1.0, ident[:], ALU.mult, ALU.add),
                             reads=CK + pltk, writes=NT[0][1])
                        P.op("act", lambda e, plt=plt, LTs=LTs: e.copy(LTs, plt), reads=pltk, writes=LTsk)
                        yield
                        pq1, pq1k = PQ(bB, 3)
                        pq2, pq2k = PQ(bA, 3)
                        P.op("pe", lambda e, pq1=pq1, LTs=LTs, Lt=Lt: e.matmul(pq1, LTs, Lt, start=True, stop=True),
                             reads=LTsk + Lk, writes=pq1k)
                        P.op("pe", lambda e, pq2=pq2, LTs=LTs, Lt=Lt: e.matmul(pq2, Lt, LTs, start=True, stop=True),
                             reads=LTsk + Lk, writes=pq2k)
                        yield
                        P.op("dve", lambda e, pq1=pq1, d=Pm[0][0]: e.tensor_copy(d, pq1), reads=pq1k, writes=Pm[0][1])
                        P.op("act", lambda e, pq2=pq2, d=PTm[0][0]: e.copy(d, pq2), reads=pq2k, writes=PTm[0][1])
                        yield
                        cur = 0
                        for lev in range(5):
                            Pc, Pck = Pm[lev % 2]
                            PTc, PTck = PTm[lev % 2]
                            ntc, ntck = NT[cur]
                            ntn, ntnk = NT[1 - cur]
                            pu, puk = PQ(bB, 0)
                            P.op("pe", lambda e, pu=pu, Pc=Pc, ntc=ntc: e.matmul(pu, Pc, ntc, start=True, stop=True),
                                 reads=Pck + ntck, writes=puk)
                            if lev < 4:
                                Pn, Pnk = Pm[(lev + 1) % 2]
                                PTn, PTnk = PTm[(lev + 1) % 2]
                                P.op("pe", lambda e, pq1=pq1, PTc=PTc, Pc=Pc: e.matmul(pq1, PTc, Pc, start=True, stop=True),
                                     reads=PTck + Pck, writes=pq1k)
                                P.op("pe", lambda e, pq2=pq2, PTc=PTc, Pc=Pc: e.matmul(pq2, Pc, PTc, start=True, stop=True),
                                     reads=PTck + Pck, writes=pq2k)
                            yield
                            P.op("dve", lambda e, pu=pu, ntc=ntc, ntn=ntn: e.tensor_tensor(ntn, pu, ntc, ALU.add),
                                 reads=puk + ntck, writes=ntnk)
                            cur = 1 - cur
                            if lev < 4:
                                P.op("act", lambda e, pq2=pq2, PTn=PTn: e.copy(PTn, pq2), reads=pq2k, writes=PTnk)
                                P.op("dve", lambda e, pq1=pq1, Pn=Pn: e.tensor_copy(Pn, pq1), reads=pq1k, writes=Pnk)
                            if lev == 1:
                                P.op("dve", lambda e, pt=pt, kbg=kbg, bg2=bg2: e.tensor_scalar(kbg, pt[:, 0:128], bg2, None, ALU.mult),
                                     reads=[ptk[0], "toks"], writes=kbgk)
                            if lev == 2:
                                P.op("dve", lambda e, pt=pt, vb=vb, bt=bt: e.tensor_scalar(vb, pt[:, 128:256], bt, None, ALU.mult),
                                     reads=[ptk[1], "toks"], writes=vbk)
                            yield
                        TTm, TTk = NT[cur]
                        (u_, uk), (wT, wTk), (qg, qgk), (vnew, vnk), (og, ogk) = Tl["u"], Tl["wT"], Tl["qg"], Tl["vnew"], Tl["og"]
                        P.op("pe", lambda e, pq1=pq1, TTm=TTm, vb=vb: e.matmul(pq1, TTm, vb, start=True, stop=True),
                             reads=TTk + vbk, writes=pq1k)
                        P.op("pe", lambda e, pq2=pq2, TTm=TTm, kbg=kbg: e.matmul(pq2, kbg, TTm, start=True, stop=True),
                             reads=TTk + kbgk, writes=pq2k)
                        P.op("pool", lambda e, qg=qg, qb_=qb_, egr=egr: e.tensor_tensor(qg, qb_, egr, ALU.mult),
                             reads=[qk_[hh]] + egk, writes=qgk)
                        yield
                        P.op("act", lambda e, pq1=pq1, u_=u_: e.copy(u_, pq1), reads=pq1k, writes=uk)
                        P.op("dve", lambda e, pq2=pq2, wT=wT: e.tensor_copy(wT, pq2), reads=pq2k, writes=wTk)
                        yield
                        po, pok = PQ(2 if up == 0 else 7, 0)
                        pws, pwsk = PQ(bB, 1)
                        pds, pdsk = PQ(bB, 2)
                        for c in range(2):
                            rs = slice(64 * c, 64 * c + 64)
                            P.op("pe", lambda e, pws=pws, wT=wT, rs=rs, Sh=Sh: e.matmul(pws[rs, :], wT[:, rs], Sh, start=True, stop=True),
                                 reads=wTk + SK, writes=pwsk)
                            P.op("pe", lambda e, po=po, qg=qg, rs=rs, Sh=Sh: e.matmul(po[rs, :], qg[:, rs], Sh, start=True, stop=False),
                                 reads=qgk + SK, writes=pok)
                            yield
                            P.op("dve", lambda e, vnew=vnew, u_=u_, pws=pws, rs=rs: e.tensor_tensor(vnew[rs, :], u_[rs, :], pws[rs, :], ALU.subtract),
                                 reads=uk + pwsk, writes=vnk)
                            yield
                            P.op("pe", lambda e, po=po, AT=AT, rs=rs, vnew=vnew: e.matmul(po[rs, :], AT[rs, rs], vnew[rs, :], start=False, stop=True),
                                 reads=ATk + vnk, writes=pok)
                            P.op("pe", lambda e, pds=pds, kdec=kdec, rs=rs, vnew=vnew: e.matmul(pds, kdec[rs, :], vnew[rs, :], start=True, stop=True),
                                 reads=kdeck + vnk, writes=pdsk)
                            yield
                            P.op("dve", lambda e, Sh=Sh, sdec=sdec, hh=hh, c=c, pds=pds: e.scalar_tensor_tensor(
                                Sh, Sh, sdec[:, 2 * hh + c:2 * hh + c + 1], pds, ALU.mult, ALU.add),
                                reads=SK + sdk + pdsk, writes=SK)
                            yield
                        ss = small[:, 8 + up:9 + up]
                        ssk = [f"ss{up}"]
                        P.op("dve", lambda e, ss=ss: e.memset(ss, 0.0), writes=ssk)
                        P.op("act", lambda e, og=og, po=po, ss=ss: e.activation(og, po, AF.Square, accum_out=ss), reads=pok + ssk, writes=ogk + ssk)
                        yield
                        P.op("act", lambda e, ss=ss: e.activation(ss, ss, AF.Sqrt, scale=1.0 / 128.0, bias=NORM_EPS), reads=ssk, writes=ssk)
                        yield
                        P.op("dve", lambda e, ss=ss: e.reciprocal(ss, ss), reads=ssk, writes=ssk)
                        P.op("dve", lambda e, og=og, po=po, ss=ss: e.scalar_tensor_tensor(og, po, ss, normw[:], ALU.mult, ALU.mult),
                             reads=CK + pok + ssk, writes=ogk)
                        yield
                        P.op("pool", lambda e, og=og, sz=sz: e.tensor_tensor(og, og, sz, ALU.mult), reads=ogk + szk, writes=ogk)
                        yield
                        P.op("pe", lambda e, pws=pws, og=og: e.transpose(pws, og, ident[:]), reads=CK + ogk, writes=pwsk)
                        yield
                        P.op("act", lambda e, pws=pws, h=h, cs=cs: e.copy(bufA[:, h, cs], pws), reads=pwsk, writes=[f"A{h}"])

                    gens = [unit_gen(0, 0), unit_gen(1, 1)] if dbg >= 1.7 else []
                    while gens:
                        for g_ in list(gens):
                            try:
                                next(g_)
                            except StopIteration:
                                gens.remove(g_)
                        yield


            def lockstep(gs):
                gs = list(gs)
                while gs:
                    for g_ in list(gs):
                        try:
                            next(g_)
                        except StopIteration:
                            gs.remove(g_)

            if dbg >= 1.3:
                lockstep([proj_stage(0)])
                for hp in range(4):
                    lockstep([units_stream(hp)] + ([proj_stage(hp + 1)] if hp < 3 else []))

            pwv = bufB[:].rearrange("p c t -> p (c t)")[:, 0:2048].rearrange("p (g k d) -> p g k d", g=4, k=2)
            if dbg >= 3:
                P.dma("poolw", lambda e: e.dma_start(out=pwv, in_=poolw_d.rearrange("g (k p) d -> p g k d", p=128)),
                      writes=AK("B", range(4)), E="pool")
            for g in range(4 if dbg >= 3 else 0):
                win = 2 ** (g + 1)
                for cc in range(2):
                    pc = 2 * g + cc
                    pb, pk = project(("w_in", 0, 4112 + g * 256 + cc * 128), hT, "H")
                    r = raw[cc]; rk_ = f"raw{cc}"
                    P.op("act", lambda e, pb=pb, r=r: e.copy(r[:, 15:527], pb[:]), reads=pk, writes=[rk_])
                    P.op("pool", lambda e, r=r, pc=pc: e.tensor_copy(r[:, 0:15], phalo[:, pc, :]), reads=["phalo"], writes=[rk_])
                    P.op("pool", lambda e, r=r, pc=pc: e.tensor_copy(phalo[:, pc, :], r[:, 512:527]), reads=[rk_], writes=["phalo"])
                    src, srck = r, rk_
                    for lev in range(g + 1):
                        sh = 2 ** lev
                        lo = 2 * sh - 1
                        d_ = raw[2 + lev % 2]; dk_ = f"raw{2 + lev % 2}"
                        eng = "dve" if (lev + cc) % 2 == 0 else "pool"
                        P.op(eng, lambda e, d_=d_, src=src, lo=lo, sh=sh: e.tensor_tensor(d_[:, lo:527], src[:, lo:527], src[:, lo - sh:527 - sh], ALU.add),
                             reads=[srck], writes=[dk_])
                        src, srck = d_, dk_
                    dst = bufB[:, 8 + pc, :]
                    P.op("dve", lambda e, dst=dst, src=src, win=win, r=r: e.scalar_tensor_tensor(
                        dst, src[:, 15:527], 1.0 / win, r[:, 15:527], ALU.mult, ALU.subtract), reads=[srck, rk_], writes=[f"B{8 + pc}"])
                    if ti == 0:
                        P.op("dve", lambda e, dst=dst, src=src, g=g: e.tensor_tensor(dst[:, 0:16], src[:, 15:31], rc16[:, g, :], ALU.mult),
                             reads=CK + [srck, f"B{8 + pc}"], writes=[f"B{8 + pc}"])
                        P.op("dve", lambda e, dst=dst, r=r: e.tensor_tensor(dst[:, 0:16], f32(dst)[:, 0:16], r[:, 15:31], ALU.subtract),
                             reads=[rk_, f"B{8 + pc}"], writes=[f"B{8 + pc}"])
                for dch in range(2):
                    pb, pk = proj_bank()
                    for cc in range(2):
                        P.op("pe", lambda e, g=g, cc=cc, dch=dch, pb=pb: e.matmul(
                            pb[:], pwv[:, g, cc, dch * 128:(dch + 1) * 128], bufB[:, 8 + 2 * g + cc, :], start=(cc == 0), stop=(cc == 1)),
                            reads=AK("B", range(4)) + [f"B{8 + 2 * g + cc}"], writes=pk)
                    P.op("act", lambda e, g=g, dch=dch, pb=pb: e.activation(bufA[:, 8 + 2 * g + dch, :], pb[:], AF.Copy,
                                                                             scale=pscale[:, 2 * g + dch:2 * g + dch + 1]),
                         reads=CK + pk, writes=[f"A{8 + 2 * g + dch}"])

            for c in range(16 if dbg >= 4 else 0):
                pb, pk = project(("w_out", 0, c * 128), bufA, "A")
                P.op("dve", lambda e, c=c, pb=pb: e.scalar_tensor_tensor(bufB[:, c, :], f32(hT[:, c, :]), ALPHA, pb[:], ALU.mult, ALU.add),
                     reads=pk + [f"H{c}"], writes=[f"B{c}"])
            if dbg >= 4:
                layer_norm(bufB, "B", 0)

            if stop_after >= 2:
                sc = 512.0 ** -0.5

                def qproj_gen(hd):
                    for dc in range(4):
                        c = 4 * hd + dc
                        pb, pk = yield from project_gen(("xq_w", 0, c * 128), hT, "H")
                        if c % 2:
                            P.op("act", lambda e, c=c, pb=pb: e.copy(bufA[:, c, :], pb[:]), reads=pk, writes=[f"A{c}"])
                        else:
                            P.op("dve", lambda e, c=c, pb=pb: e.tensor_copy(bufA[:, c, :], pb[:]), reads=pk, writes=[f"A{c}"])
                        yield

                def attn_chain(hd, qb, PTs):
                    bank = 3 + qb
                    psc, psck = PQ(bank, 0, 2)
                    for dc in range(4):
                        P.op("pe", lambda e, psc=psc, hd=hd, dc=dc, qb=qb: e.matmul(
                            psc, bufA[:, 4 * hd + dc, qb * 128:(qb + 1) * 128], kxT[:, 4 * hd + dc, :], start=(dc == 0), stop=(dc == 3)),
                            reads=[f"A{4 * hd + dc}", "kx"], writes=psck)
                    yield
                    mx = small2[:, qb:qb + 1]; mxk = [f"mx{qb}"]
                    sm = small2[:, 4 + qb:5 + qb]; smk = [f"sm{qb}"]
                    pe_, pek = BQ(8 + qb, 0, 2)
                    P.op("dve", lambda e, mx=mx, psc=psc: e.reduce_max(mx, psc, AX.X), reads=psck, writes=mxk)
                    P.op("dve", lambda e, sm=sm: e.memset(sm, 0.0), writes=smk)
                    yield
                    P.op("dve", lambda e, mx=mx: e.tensor_scalar(mx, mx, -sc, None, ALU.mult), reads=mxk, writes=mxk)
                    yield
                    P.op("act", lambda e, pe_=pe_, psc=psc, mx=mx, sm=sm: e.activation(pe_, psc, AF.Exp, bias=mx, scale=sc, accum_out=sm),
                         reads=psck + mxk + smk, writes=pek + smk)
                    yield
                    P.op("dve", lambda e, sm=sm: e.reciprocal(sm, sm), reads=smk, writes=smk)
                    yield
                    P.op("pool", lambda e, pe_=pe_, sm=sm: e.tensor_scalar(pe_, pe_, sm, None, ALU.mult), reads=pek + smk, writes=pek)
                    yield
                    ptt, pttk = PQ(bank, 2, 2)
                    for mb in range(2):
                        P.op("pe", lambda e, ptt=ptt, pe_=pe_, mb=mb: e.transpose(ptt[:, mb * 128:(mb + 1) * 128], pe_[:, mb * 128:(mb + 1) * 128], ident[:]),
                             reads=CK + pek, writes=pttk)
                    yield
                    P.op("dve", lambda e, ptt=ptt, qb=qb, d=PTs[0][0]: e.tensor_copy(d[:, qb * 128:(qb + 1) * 128], ptt[:, 0:128]),
                         reads=pttk, writes=[PTs[0][1][qb]])
                    P.op("act", lambda e, ptt=ptt, qb=qb, d=PTs[1][0]: e.copy(d[:, qb * 128:(qb + 1) * 128], ptt[:, 128:256]),
                         reads=pttk, writes=[PTs[1][1][qb]])

                def attn_head(hd):
                    PTs = [(big[4 + hd % 2], [f"a{4 + hd % 2}_{q}" for q in range(4)]),
                           (big[6 + hd % 2], [f"a{6 + hd % 2}_{q}" for q in range(4)])]
                    chains = [attn_chain(hd, qb, PTs) for qb in range(4)]
                    while chains:
                        for g_ in list(chains):
                            try:
                                next(g_)
                            except StopIteration:
                                chains.remove(g_)
                        yield
                    for dhc in range(4):
                        pb, pk = (ps[2], ["p2"]) if dhc % 2 == 0 else (ps[7], ["p7"])
                        for mb in range(2):
                            P.op("pe", lambda e, pb=pb, mb=mb, hd=hd, dhc=dhc, d=PTs[mb][0]: e.matmul(
                                pb[:], f32(vx[:, mb, hd * 512 + dhc * 128: hd * 512 + (dhc + 1) * 128]), d[:], start=(mb == 0), stop=(mb == 1)),
                                reads=["vx"] + PTs[mb][1], writes=pk)
                        yield
                        oc = 4 * hd + dhc
                        if dhc % 2:
                            P.op("act", lambda e, oc=oc, pb=pb: e.copy(bufB[:, oc, :], pb[:]), reads=pk, writes=[f"B{oc}"])
                        else:
                            P.op("dve", lambda e, oc=oc, pb=pb: e.tensor_copy(bufB[:, oc, :], pb[:]), reads=pk, writes=[f"B{oc}"])
                        yield

                lockstep([qproj_gen(0)])
                for hd in range(4):
                    lockstep([attn_head(hd)] + ([qproj_gen(hd + 1)] if hd < 3 else []))
                for c in range(16):
                    pb, pk = project(("xo_w", 0, c * 128), bufB, "B")
                    P.op("dve", lambda e, c=c, pb=pb: e.scalar_tensor_tensor(bufA[:, c, :], f32(hT[:, c, :]), ALPHA, pb[:], ALU.mult, ALU.add),
                         reads=pk + [f"H{c}"], writes=[f"A{c}"])
                layer_norm(bufA, "A", 1)

            if stop_after >= 3:
                for g in range(4):
                    for j in range(16):
                        pb, pk = project(("w_up", 0, g * 2048 + j * 128), hT, "H")
                        rl, rlk = big[8 + j % 4], [f"a{8 + j % 4}_{q}" for q in range(4)]
                        P.op("act", lambda e, pb=pb, rl=rl: e.activation(rl[:], pb[:], AF.Relu), reads=pk, writes=rlk)
                        if j % 2:
                            P.op("act", lambda e, j=j, rl=rl: e.activation(bufB[:, j, :], rl[:], AF.Square), reads=rlk, writes=[f"B{j}"])
                        else:
                            P.op("dve", lambda e, j=j, rl=rl: e.tensor_tensor(bufB[:, j, :], rl[:], rl[:], ALU.mult), reads=rlk, writes=[f"B{j}"])
                    for j in range(16):
                        pb, pk = project(("w_down", g * 2048, j * 128), bufB, "B")
                        if g == 0:
                            P.op("dve", lambda e, j=j, pb=pb: e.scalar_tensor_tensor(bufA[:, j, :], f32(hT[:, j, :]), ALPHA, pb[:], ALU.mult, ALU.add),
                                 reads=pk + [f"H{j}"], writes=[f"A{j}"])
                        elif g < 3:
                            P.op("dve", lambda e, j=j, pb=pb: e.tensor_tensor(bufA[:, j, :], pb[:], f32(bufA[:, j, :]), ALU.add),
                                 reads=pk + [f"A{j}"], writes=[f"A{j}"])
                        else:
                            P.op("dve", lambda e, j=j, pb=pb: e.tensor_tensor(bufA[:, j, :], pb[:], f32(bufA[:, j, :]), ALU.add),
                                 reads=pk + [f"A{j}"], writes=[f"A{j}"])
                layer_norm(bufA, "A", 2)

            for tb in range(4):
                obig = (8, 9, 10, 11) if tb % 2 == 0 else (12, 13, 0, 1)
                for dq in range(4):
                    pb, pk = proj_bank()
                    for d4 in range(4):
                        dc = dq * 4 + d4
                        P.op("pe", lambda e, pb=pb, d4=d4, dc=dc, tb=tb: e.transpose(pb[:, d4 * 128:(d4 + 1) * 128], f32(hT[:, dc, tb * 128:(tb + 1) * 128]), ident[:]),
                             reads=CK + [f"H{dc}"], writes=pk)
                    dsto = big[obig[dq]]
                    dstk = [f"a{obig[dq]}_{q}" for q in range(4)]
                    if dq % 2:
                        P.op("act", lambda e, pb=pb, dsto=dsto: e.copy(dsto[:], pb[:]), reads=pk, writes=dstk)
                    else:
                        P.op("dve", lambda e, pb=pb, dsto=dsto: e.tensor_copy(dsto[:], pb[:]), reads=pk, writes=dstk)
                    t = P.dma(f"out{tb % 2}_{dq}", lambda e, tb=tb, dq=dq, dsto=dsto, t0=t0: e.dma_start(
                        out=out_d[t0 + tb * 128:t0 + (tb + 1) * 128, dq * 512:(dq + 1) * 512], in_=dsto[:]), reads=dstk)
                    out_toks.append(t)
        P.wait_tokens("sp", out_toks)
        P.emit()
    return nc


def host_inputs(inp, b, ntiles=4):
    T = ntiles * TT
    f = lambda a: np.ascontiguousarray(a, dtype=np.float32)
    m = {"x": f(inp["x"][b, :T]), "mem": f(inp["mem"][b])}
    for k in ("w_in", "w_out", "xq_w", "xk_w", "xv_w", "xo_w", "w_up", "w_down", "pool_w"):
        m[k] = f(inp[k][0])
    m["cw"] = f(np.asarray(inp["conv_w"][0]).T.reshape(24, 128, 4).transpose(1, 0, 2))
    m["alog"] = f(np.asarray(inp["a_log"][0]).reshape(8, 1))
    m["dtb"] = f(np.asarray(inp["dt_bias"][0]).reshape(8, 1))
    m["normw"] = f(np.broadcast_to(np.asarray(inp["gdn_norm_w"][0])[None, :], (128, 128)))
    m["pscale"] = f(np.asarray(inp["pool_scale"][0]).reshape(8, 128).T)
    lnp = np.stack([np.asarray(inp[k][0]).reshape(16, 128).T for k in ("ln1_g", "ln1_b", "ln2_g", "ln2_b", "ln3_g", "ln3_b")], axis=1)
    m["lnp"] = f(lnp)
    m.update(host_consts())
    return m


_NC_CACHE = {}


def kernel(**inputs):
    inp = {k: np.asarray(v) for k, v in inputs.items()}
    if "nc" not in _NC_CACHE:
        _NC_CACHE["nc"] = build(4, 3)
    nc = _NC_CACHE["nc"]
    in_maps = [host_inputs(inp, b) for b in range(8)]
    res = run_bass_kernel_spmd(nc, in_maps, core_ids=list(range(8)))
    out = np.stack([np.asarray(res.results[b]["out"]) for b in range(8)], axis=0)
    return out.astype(np.float32)
```

```python
import numpy as np
from contextlib import ExitStack
import concourse.bass as bass
import concourse.mybir as mybir
from concourse.bass_utils import run_bass_kernel_spmd

F32 = mybir.dt.float32
F32R = mybir.dt.float32r
AF = mybir.ActivationFunctionType
ALU = mybir.AluOpType
AX = mybir.AxisListType

EPOCH = 8000
D = 2048
SEQ = 2048
TT = 512
NDC = 16
MEM = 256
DFF = 8192
INC = 5136
ALPHA = 2.0 ** 0.25
LN_EPS = 1e-5
NORM_EPS = 1e-6
NSLOT = 3
NBIG = 14
PGRAN = 16
URATIO = 2


class Prog:
    COMPUTE = ("pe", "act", "dve", "pool")

    def __init__(self, nc, stack):
        self.nc = nc
        self.stack = stack
        self.ops = {e: [] for e in ("pe", "act", "dve", "pool", "sp")}
        self.count = {e: 0 for e in self.COMPUTE}
        self.sems = {}
        self.known = {e: {} for e in self.ops}
        self.last_write = {}
        self.readers = {}
        self.dma_count = {}

    def _sem(self, name):
        if name not in self.sems:
            self.sems[name] = self.stack.enter_context(self.nc.semaphore(name))
        return self.sems[name]

    def _phys(self, cname, v):
        if cname in self.COMPUTE:
            return self._sem(f"s_{cname}_{(v - 1) // EPOCH}"), (v - 1) % EPOCH + 1
        return self._sem(cname), v

    def _collect(self, E, reads, writes):
        deps = []
        for k in reads:
            t = self.last_write.get(k)
            if t is not None:
                deps.append(t)
        for k in writes:
            t = self.last_write.get(k)
            if t is not None:
                deps.append(t)
            deps.extend(self.readers.get(k, ()))
        deps.sort(key=lambda t: -t[1])
        kn = self.known[E]
        waits = {}
        for (cname, v, snap) in deps:
            if cname == E and E == "pe":
                continue
            if kn.get(cname, 0) >= v:
                continue
            waits[cname] = max(waits.get(cname, 0), v)
            kn[cname] = v
            for kk, vv in snap.items():
                if kn.get(kk, 0) < vv:
                    kn[kk] = vv
        return [self._phys(c, v) for c, v in waits.items()]

    @staticmethod
    def _excl(reads, writes):
        pr = [k for k in reads if k[0] == "p" and k[1:].isdigit()]
        if pr:
            reads = [k for k in reads if k not in pr]
            writes = list(writes) + pr
        return reads, writes

    def op(self, E, fn, reads=(), writes=()):
        reads, writes = self._excl(reads, writes)
        waits = self._collect(E, reads, writes)
        idx = self.count[E]
        self.count[E] += 1
        snap = dict(self.known[E])
        snap[E] = idx + 1
        tok = (E, idx + 1, snap)
        sem, _ = self._phys(E, idx + 1)
        self.ops[E].append((waits, fn, sem, 1))
        self._record(tok, reads, writes)
        return tok

    def dma(self, slot, fn, reads=(), writes=(), E="sp"):
        waits = self._collect(E, reads, writes)
        cname = "dma_" + E + "_" + slot
        self.dma_count[cname] = self.dma_count.get(cname, 0) + 16
        tok = (cname, self.dma_count[cname], dict(self.known[E]))
        self.ops[E].append((waits, fn, self._sem(cname), 16))
        self._record(tok, reads, writes)
        return tok

    def _record(self, tok, reads, writes):
        for k in reads:
            self.readers.setdefault(k, []).append(tok)
        for k in writes:
            self.last_write[k] = tok
            self.readers[k] = []

    def wait_tokens(self, E, toks):
        kn = self.known[E]
        waits = []
        for (cname, v, snap) in toks:
            if kn.get(cname, 0) >= v:
                continue
            kn[cname] = v
            waits.append(self._phys(cname, v))
        self.ops[E].append((waits, None, None, 0))

    def emit(self):
        nc = self.nc
        engs = {"pe": "tensor", "act": "scalar", "dve": "vector", "pool": "gpsimd", "sp": "sync"}
        with nc.Block() as block:
            for E, attr in engs.items():
                ops = self.ops[E]

                def body(eng, ops=ops):
                    for (waits, fn, sem, n) in ops:
                        for (s, v) in waits:
                            eng.wait_ge(s, v)
                        if fn is not None:
                            ins = fn(eng)
                            ins.then_inc(sem, n)

                getattr(block, attr)(body)


def host_consts():
    i = np.arange(128)
    same = (i[:, None] // 64) == (i[None, :] // 64)
    c = {}
    c["ident"] = np.eye(128, dtype=np.float32)
    c["ones"] = np.ones((128, 128), np.float32)
    c["avg"] = np.full((128, 128), 1.0 / D, np.float32)
    c["tribd"] = (same & (i[:, None] <= i[None, :])).astype(np.float32)
    c["bd"] = same.astype(np.float32)
    c["msbd"] = (same & (i[:, None] > i[None, :])).astype(np.float32)
    c["minclt"] = (same & (i[:, None] <= i[None, :])).astype(np.float32) * np.float32(128.0 ** -0.5)
    rc = np.zeros((128, 4, 16), np.float32)
    for g, win in enumerate((2, 4, 8, 16)):
        rc[:, g, :] = 1.0 / np.minimum(np.arange(16) + 1, win).astype(np.float32)
    c["rc16"] = rc
    return c


def tile_slabs(stop_after=3):
    L = []
    for hp in range(4):
        for part in range(4):
            for hh in range(2):
                L.append(("w_in", 0, part * 1024 + hp * 256 + hh * 128))
    for g in range(4):
        for cc in range(2):
            L.append(("w_in", 0, 4112 + g * 256 + cc * 128))
    for c in range(16):
        L.append(("w_out", 0, c * 128))
    if stop_after >= 2:
        for c in range(16):
            L.append(("xq_w", 0, c * 128))
        for c in range(16):
            L.append(("xo_w", 0, c * 128))
    if stop_after >= 3:
        for g in range(4):
            for j in range(16):
                L.append(("w_up", 0, g * 2048 + j * 128))
            for j in range(16):
                L.append(("w_down", g * 2048, j * 128))
    return L


def build(ntiles=4, stop_after=3, dbg=9):
    nc = bass.Bass("TRN2", target_bir_lowering=False)
    T = ntiles * TT

    def din(name, shape):
        return nc.dram_tensor(name, list(shape), F32, kind="ExternalInput").ap()

    x_d = din("x", [T, D])
    mem_d = din("mem", [MEM, D])
    W = {"w_in": din("w_in", [D, INC]), "w_out": din("w_out", [D, D]), "xq_w": din("xq_w", [D, D]),
         "xk_w": din("xk_w", [D, D]), "xv_w": din("xv_w", [D, D]), "xo_w": din("xo_w", [D, D]),
         "w_up": din("w_up", [D, DFF]), "w_down": din("w_down", [DFF, D])}
    poolw_d = din("pool_w", [4, 256, 256])
    cw_d = din("cw", [128, 24, 4])
    alog_d = din("alog", [8, 1])
    dtb_d = din("dtb", [8, 1])
    normw_d = din("normw", [128, 128])
    pscale_d = din("pscale", [128, 8])
    lnp_d = din("lnp", [128, 6, 16])
    cst_d = {k: din(k, v.shape) for k, v in host_consts().items()}
    out_d = nc.dram_tensor("out", [T, D], F32, kind="ExternalOutput").ap()

    with ExitStack() as st:
        P = Prog(nc, st)

        def sb(name, shape, dt=F32):
            return st.enter_context(nc.sbuf_tensor(name, list(shape), dt))

        wsl = [sb(f"wsl{i}", [128, 16, 128], F32R) for i in range(NSLOT)]
        hT = sb("hT", [128, 16, TT], F32R)
        bufA = sb("bufA", [128, 16, TT], F32R)
        bufB = sb("bufB", [128, 16, TT], F32R)
        kxT = sb("kxT", [128, 16, MEM], F32R)
        vx = sb("vx", [128, 2, D], F32R)
        big = [sb(f"big{i}", [128, 512], F32) for i in range(NBIG)]
        raw = [sb(f"raw{i}", [128, 528], F32) for i in range(4)]
        lnsq = [sb(f"lnsq{i}", [128, 512], F32R) for i in range(2)]
        Sst = sb("Sst", [128, 8, 128], F32)
        halo = sb("halo", [128, 24, 3], F32)
        phalo = sb("phalo", [128, 8, 15], F32)
        wba = sb("wba", [128, 16, 16], F32R)
        cw = sb("cw_s", [128, 24, 4])
        alog = sb("alog_s", [8, 1]); dtb = sb("dtb_s", [8, 1]); nega = sb("nega", [8, 1])
        normw = sb("normw_s", [128, 128])
        pscale = sb("pscale_s", [128, 8])
        lnp = sb("lnp_s", [128, 6, 16])
        ident = sb("ident_s", [128, 128]); ones = sb("ones_s", [128, 128])
        avg = sb("avg_s", [128, 128], F32R)
        bdm = sb("bd_s", [128, 128]); tribd = sb("tribd_s", [128, 128]); msbd = sb("msbd_s", [128, 128]); minclt = sb("minclt_s", [128, 128])
        rc16 = sb("rc16_s", [128, 4, 16])
        toks = sb("toks", [128, 6, 4, 8])
        small = sb("small", [128, 16])
        small2 = sb("small2", [128, 8])
        ps = [st.enter_context(nc.psum_tensor(f"ps{i}", [128, 512], F32)) for i in range(8)]

        def f32(ap):
            return ap.bitcast(F32)

        def AK(pref, cs):
            return [f"{pref}{c}" for c in cs]

        def BQ(i, j, n=1):
            return big[i][:, j * 128:(j + n) * 128], [f"a{i}_{q}" for q in range(j, j + n)]

        def PQ(b, j, n=1):
            return ps[b][:, j * 128:(j + n) * 128], [f"p{b}"] * n

        CK = ["const"]
        for (dst, src) in ((cw, cw_d), (alog, alog_d), (dtb, dtb_d), (normw, normw_d), (pscale, pscale_d),
                           (lnp, lnp_d), (ident, cst_d["ident"]), (ones, cst_d["ones"]), (tribd, cst_d["tribd"]), (bdm, cst_d["bd"]),
                           (msbd, cst_d["msbd"]), (minclt, cst_d["minclt"]), (rc16, cst_d["rc16"])):
            P.dma("const", lambda e, dst=dst, src=src: e.dma_start(out=dst[:], in_=src), writes=CK)
        P.dma("const", lambda e: e.dma_start(out=avg[:], in_=cst_d["avg"]), writes=CK, E="pool")
        P.dma("const", lambda e: e.dma_start(
            out=wba[:], in_=W["w_in"][:, 4096:4112].rearrange("(kc p) c -> p kc c", p=128)), writes=CK, E="pool")
        P.op("dve", lambda e: e.memset(Sst[:], 0.0), writes=AK("S", range(8)))
        P.op("dve", lambda e: e.memset(halo[:], 0.0), writes=["halo"])
        P.op("dve", lambda e: e.memset(phalo[:], 0.0), writes=["phalo"])
        P.op("act", lambda e: e.activation(nega[:], alog[:], AF.Exp), reads=CK, writes=["nega"])
        P.op("dve", lambda e: e.tensor_scalar(nega[:], nega[:], -1.0, None, ALU.mult), reads=["nega"], writes=["nega"])

        sched = []
        ws_state = {"issued": 0, "pos": 0}

        def slab_src(name, r0, c0):
            return W[name][r0:r0 + 2048, c0:c0 + 128].rearrange("(kc p) c -> p kc c", p=128)

        def ws_issue_upto(n):
            while ws_state["issued"] < min(n, len(sched)):
                j = ws_state["issued"]
                s = j % NSLOT
                src = slab_src(*sched[j])
                P.dma(f"w{s}", lambda e, s=s, src=src: e.dma_start(out=wsl[s][:], in_=src), writes=[f"w{s}"], E="pool")
                ws_state["issued"] += 1

        def ws_use(expect):
            i = ws_state["pos"]
            assert sched[i] == expect, (i, sched[i], expect)
            ws_issue_upto(i + NSLOT)
            ws_state["pos"] += 1
            return wsl[i % NSLOT], f"w{i % NSLOT}"

        if stop_after >= 2:
            for c in range(16):
                sched.append(("xk_w", 0, c * 128))
            for c in range(16):
                sched.append(("xv_w", 0, c * 128))
        for _ in range(ntiles):
            sched.extend(tile_slabs(stop_after)[:{0: 0, 1: 0, 2: 32, 3: 40}.get(dbg, 32 if dbg < 2 else 10 ** 6)])

        pj = {"i": 0}

        def proj_bank():
            b = pj["i"] % 2
            pj["i"] += 1
            return ps[b], [f"p{b}"]

        def project(expect, act_tile, act_pref, n=TT):
            wt, wk = ws_use(expect)
            pb, pk = proj_bank()
            for kc in range(16):
                P.op("pe", lambda e, kc=kc, wt=wt, pb=pb: e.matmul(pb[:, 0:n], wt[:, kc, :], act_tile[:, kc, 0:n],
                                                                  start=(kc == 0), stop=(kc == 15)),
                     reads=[wk, f"{act_pref}{kc}"], writes=pk)
            return pb, pk

        def layer_norm(y, ypref, li):
            mean_s, mk = big[0], [f"a0_{q}" for q in range(4)]
            rstd_s, rk = big[1], [f"a1_{q}" for q in range(4)]
            pb, pk = proj_bank()
            for dc in range(16):
                P.op("pe", lambda e, dc=dc: e.matmul(pb[:], avg[:], y[:, dc, :], start=(dc == 0), stop=(dc == 15)),
                     reads=CK + [f"{ypref}{dc}"], writes=pk)
            P.op("act", lambda e: e.copy(mean_s[:], pb[:]), reads=pk, writes=mk)
            pb2, pk2 = proj_bank()
            for dc in range(16):
                P.op("dve", lambda e, dc=dc: e.tensor_tensor(y[:, dc, :], f32(y[:, dc, :]), mean_s[:], ALU.subtract),
                     reads=mk + [f"{ypref}{dc}"], writes=[f"{ypref}{dc}"])
                sq_t = lnsq[dc % 2]
                sqk = [f"lnsq{dc % 2}"]
                P.op("act", lambda e, dc=dc, sq_t=sq_t: e.activation(sq_t[:], f32(y[:, dc, :]), AF.Square),
                     reads=[f"{ypref}{dc}"], writes=sqk)
                P.op("pe", lambda e, dc=dc, sq_t=sq_t: e.matmul(pb2[:], avg[:], sq_t[:],
                                                              start=(dc == 0), stop=(dc == 15)),
                     reads=CK + sqk, writes=pk2)
            P.op("act", lambda e: e.activation(rstd_s[:], pb2[:], AF.Sqrt, bias=LN_EPS), reads=pk2, writes=rk)
            P.op("dve", lambda e: e.reciprocal(rstd_s[:], rstd_s[:]), reads=rk, writes=rk)
            for dc in range(16):
                eng = "dve" if dc % 2 == 0 else "pool"
                P.op(eng, lambda e, dc=dc: e.tensor_tensor(y[:, dc, :], f32(y[:, dc, :]), rstd_s[:], ALU.mult),
                     reads=rk + [f"{ypref}{dc}"], writes=[f"{ypref}{dc}"])
                P.op("act", lambda e, dc=dc: e.activation(hT[:, dc, :], f32(y[:, dc, :]), AF.Identity,
                                                          scale=lnp[:, 2 * li, dc:dc + 1], bias=lnp[:, 2 * li + 1, dc:dc + 1]),
                     reads=CK + [f"{ypref}{dc}"], writes=[f"H{dc}"])

        PRO = stop_after >= 2
        memtok = bufB[:].rearrange("p c t -> p (c t)")
        for mb in range(2 if PRO else 0):
            P.dma(f"xin{mb}", lambda e, mb=mb: e.dma_start(out=memtok[:, mb * D:(mb + 1) * D],
                                                      in_=mem_d[mb * 128:(mb + 1) * 128, :]),
                  writes=AK("B", range(4 * mb, 4 * mb + 4)), E="pool")
        memT = bufA[:].rearrange("p c t -> p (c t)")
        for dc in range(16 if PRO else 0):
            tb_, tk_ = PQ(3 + dc % 2, 0, 2)
            for mb in range(2):
                P.op("pe", lambda e, dc=dc, mb=mb, tb_=tb_: e.transpose(
                    tb_[:, mb * 128:(mb + 1) * 128], f32(memtok[:, mb * D + dc * 128: mb * D + (dc + 1) * 128]), ident[:]),
                    reads=CK + AK("B", [4 * mb + dc // 4]), writes=tk_)
            P.op("act" if dc % 2 else "dve", (lambda e, dc=dc, tb_=tb_: e.copy(memT[:, dc * 256:(dc + 1) * 256], tb_))
                 if dc % 2 else (lambda e, dc=dc, tb_=tb_: e.tensor_copy(memT[:, dc * 256:(dc + 1) * 256], tb_)),
                 reads=tk_, writes=AK("A", [dc // 2]))
        for c in range(16 if PRO else 0):
            wt, wk = ws_use(("xk_w", 0, c * 128))
            pb, pk = proj_bank()
            for kc in range(16):
                P.op("pe", lambda e, kc=kc, wt=wt, pb=pb: e.matmul(pb[:, 0:256], wt[:, kc, :], memT[:, kc * 256:(kc + 1) * 256],
                                                                  start=(kc == 0), stop=(kc == 15)),
                     reads=[wk] + AK("A", [kc // 2]), writes=pk)
            P.op("act", lambda e, c=c, pb=pb: e.copy(kxT[:, c, :], pb[:, 0:256]), reads=pk, writes=["kx"])
        for c in range(16 if PRO else 0):
            wt, wk = ws_use(("xv_w", 0, c * 128))
            pb, pk = proj_bank()
            for mb in range(2):
                for kc in range(16):
                    P.op("pe", lambda e, kc=kc, mb=mb, wt=wt, pb=pb: e.matmul(
                        pb[:, mb * 128:(mb + 1) * 128], memT[:, kc * 256 + mb * 128: kc * 256 + (mb + 1) * 128], wt[:, kc, :],
                        start=(kc == 0), stop=(kc == 15)),
                        reads=[wk] + AK("A", [kc // 2]), writes=pk)
            P.op("dve", lambda e, c=c, pb=pb: e.tensor_copy(
                vx[:, :, c * 128:(c + 1) * 128], pb[:, 0:256].rearrange("p (m c) -> p m c", m=2)), reads=pk, writes=["vx"])

        out_toks = []
        unit_ctr = {"u": 0}

        for ti in range(ntiles):
            t0 = ti * TT
            xtok = bufB[:].rearrange("p c t -> p (c t)")
            for tb in range(4):
                P.dma(f"xin{tb}", lambda e, tb=tb, t0=t0: e.dma_start(out=xtok[:, tb * D:(tb + 1) * D],
                                                          in_=x_d[t0 + tb * 128: t0 + (tb + 1) * 128, :]),
                      writes=AK("B", range(4 * tb, 4 * tb + 4)), E="pool")
            for dc in range(16):
                pb, pk = proj_bank()
                for tb in range(4):
                    P.op("pe", lambda e, dc=dc, tb=tb, pb=pb: e.transpose(
                        pb[:, tb * 128:(tb + 1) * 128], f32(xtok[:, tb * D + dc * 128: tb * D + (dc + 1) * 128]), ident[:]),
                        reads=CK + AK("B", [4 * tb + dc // 4]), writes=pk)
                if dc % 2:
                    P.op("act", lambda e, dc=dc, pb=pb: e.copy(hT[:, dc, :], pb[:]), reads=pk, writes=[f"H{dc}"])
                else:
                    P.op("dve", lambda e, dc=dc, pb=pb: e.tensor_copy(hT[:, dc, :], pb[:]), reads=pk, writes=[f"H{dc}"])

            bfm, bfk = big[12], [f"a12_{q}" for q in range(4)]
            gfm, gfk = big[13], [f"a13_{q}" for q in range(4)]
            for part, (dst, dk_) in enumerate(((bfm, bfk), (gfm, gfk)) if dbg >= 1 else ()):
                pb, pk = proj_bank()
                for kc in range(16):
                    P.op("pe", lambda e, kc=kc, part=part, pb=pb: e.matmul(pb[0:8, :], wba[:, kc, part * 8:(part + 1) * 8],
                                                                          hT[:, kc, :], start=(kc == 0), stop=(kc == 15)),
                         reads=CK + [f"H{kc}"], writes=pk)
                if part == 0:
                    P.op("act", lambda e, pb=pb, dst=dst: e.activation(dst[0:8, :], pb[0:8, :], AF.Sigmoid), reads=pk, writes=dk_)
                else:
                    P.op("act", lambda e, pb=pb, dst=dst: e.activation(dst[0:8, :], pb[0:8, :], AF.Exp, bias=dtb[:, 0:1]),
                         reads=pk + CK, writes=dk_)
                    P.op("act", lambda e, dst=dst: e.activation(dst[0:8, :], dst[0:8, :], AF.Ln, bias=1.0), reads=dk_, writes=dk_)
                    P.op("dve", lambda e, dst=dst: e.tensor_scalar(dst[0:8, :], dst[0:8, :], nega[:, 0:1], None, ALU.mult),
                         reads=dk_ + ["nega"], writes=dk_)
            for tb in range(4 if dbg >= 1 else 0):
                tq, tqk = PQ(3 + tb % 2, 0)
                P.op("pe", lambda e, tb=tb, tq=tq: e.transpose(tq[:, 0:8], bfm[0:8, tb * 128:(tb + 1) * 128], ident[0:8, 0:8]),
                     reads=CK + bfk, writes=tqk)
                P.op("pe", lambda e, tb=tb, tq=tq: e.transpose(tq[:, 8:16], gfm[0:8, tb * 128:(tb + 1) * 128], ident[0:8, 0:8]),
                     reads=CK + gfk, writes=tqk)
                P.op("dve", lambda e, tb=tb, tq=tq: e.tensor_copy(toks[:, 0:2, tb, :], tq[:, 0:16].rearrange("p (a h) -> p a h", a=2)),
                     reads=tqk, writes=["toks"])
                tq2, tqk2 = PQ(3 + tb % 2, 1)
                P.op("pe", lambda e, tb=tb, tq2=tq2: e.matmul(tq2[:, 0:8], tribd[:], toks[:, 1, tb, :], start=True, stop=True),
                     reads=CK + ["toks"], writes=tqk2)
                P.op("pe", lambda e, tb=tb, tq2=tq2: e.matmul(tq2[:, 8:16], bdm[:], toks[:, 1, tb, :], start=True, stop=True),
                     reads=CK + ["toks"], writes=tqk2)
                P.op("dve", lambda e, tb=tb, tq2=tq2: e.tensor_copy(toks[:, 2:4, tb, :], tq2[:, 0:16].rearrange("p (a h) -> p a h", a=2)),
                     reads=tqk2, writes=["toks"])
            P.op("act", lambda e: e.activation(toks[:, 4, :, :], toks[:, 2, :, :], AF.Exp), reads=["toks"], writes=["toks"])
            P.op("dve", lambda e: e.tensor_tensor(toks[:, 4, :, :], toks[:, 4, :, :], toks[:, 0, :, :], ALU.mult),
                 reads=["toks"], writes=["toks"])
            P.op("dve", lambda e: e.tensor_tensor(toks[:, 5, :, :], toks[:, 3, :, :], toks[:, 2, :, :], ALU.subtract),
                 reads=["toks"], writes=["toks"])
            P.op("act", lambda e: e.activation(toks[:, 5, :, :], toks[:, 5, :, :], AF.Exp), reads=["toks"], writes=["toks"])

            def fin(hp, part, hh):
                c_ = (hp % 2) * 8 + part * 2 + hh
                return bufB[:, c_, :], f"B{c_}"

            def project_gen(expect, act_tile, act_pref, n=TT):
                wt, wk = ws_use(expect)
                pb, pk = proj_bank()
                for kc in range(16):
                    P.op("pe", lambda e, kc=kc, wt=wt, pb=pb: e.matmul(pb[:, 0:n], wt[:, kc, :], act_tile[:, kc, 0:n],
                                                                      start=(kc == 0), stop=(kc == 15)),
                         reads=[wk, f"{act_pref}{kc}"], writes=pk)
                    if kc % PGRAN == PGRAN - 1:
                        yield
                return pb, pk

            def proj_stage(hp):
                for part in range(4):
                    for hh in range(2):
                        h = 2 * hp + hh
                        pb, pk = yield from project_gen(("w_in", 0, part * 1024 + hp * 256 + hh * 128), hT, "H")
                        dstw, dkey = fin(hp, part, hh)
                        dst = f32(dstw)
                        if part == 3:
                            P.op("act", lambda e, pb=pb, dstw=dstw: e.copy(dstw, pb[:]), reads=pk, writes=[dkey])
                            yield
                            continue
                        ch = part * 8 + h
                        r = raw[(part * 2 + hh) % 2]
                        rk_ = f"raw{(part * 2 + hh) % 2}"
                        P.op("act", lambda e, pb=pb, r=r: e.copy(r[:, 3:515], pb[:]), reads=pk, writes=[rk_])
                        P.op("pool", lambda e, r=r, ch=ch: e.tensor_copy(r[:, 0:3], halo[:, ch, :]), reads=["halo"], writes=[rk_])
                        P.op("pool", lambda e, r=r, ch=ch: e.tensor_copy(halo[:, ch, :], r[:, 512:515]), reads=[rk_], writes=["halo"])
                        yield
                        eng = "dve" if hh == 0 else "pool"
                        P.op(eng, lambda e, r=r, ch=ch, dstw=dstw: e.tensor_scalar(dstw, r[:, 0:512], cw[:, ch, 0:1], None, ALU.mult),
                             reads=CK + [rk_], writes=[dkey])
                        for j in range(1, 4):
                            P.op("dve", lambda e, r=r, ch=ch, dst=dst, dstw=dstw, j=j: e.scalar_tensor_tensor(
                                dstw, r[:, j:j + 512], cw[:, ch, j:j + 1], dst, ALU.mult, ALU.add),
                                reads=CK + [rk_, dkey], writes=[dkey])
                            yield
                        P.op("act", lambda e, dst=dst, dstw=dstw: e.activation(dstw, dst, AF.Silu), reads=[dkey], writes=[dkey])
                        if part < 2:
                            sqb, sqk = raw[2 + hh][:, 0:512], [f"raw{2 + hh}"]
                            P.op("act", lambda e, dst=dst, sqb=sqb: e.activation(sqb, dst, AF.Square), reads=[dkey], writes=sqk)
                            yield
                            pb2, pk2 = proj_bank()
                            P.op("pe", lambda e, pb2=pb2, sqb=sqb: e.matmul(pb2[:], ones[:], sqb, start=True, stop=True),
                                 reads=CK + sqk, writes=pk2)
                            yield
                            P.op("act", lambda e, pb2=pb2, sqb=sqb: e.activation(sqb, pb2[:], AF.Sqrt, bias=NORM_EPS),
                                 reads=pk2, writes=sqk)
                            yield
                            P.op("dve", lambda e, sqb=sqb: e.reciprocal(sqb, sqb), reads=sqk, writes=sqk)
                            yield
                            P.op(eng, lambda e, dst=dst, dstw=dstw, sqb=sqb: e.tensor_tensor(dstw, dst, sqb, ALU.mult),
                                 reads=sqk + [dkey], writes=[dkey])

            def units_stream(hp):
                qf = [f32(fin(hp, 0, hh)[0]) for hh in range(2)]
                kf = [f32(fin(hp, 1, hh)[0]) for hh in range(2)]
                vf = [f32(fin(hp, 2, hh)[0]) for hh in range(2)]
                zf = [f32(fin(hp, 3, hh)[0]) for hh in range(2)]
                qk_ = [fin(hp, 0, hh)[1] for hh in range(2)]
                kk_ = [fin(hp, 1, hh)[1] for hh in range(2)]
                vk_ = [fin(hp, 2, hh)[1] for hh in range(2)]
                zk_ = [fin(hp, 3, hh)[1] for hh in range(2)]
                for tb in range(4 if dbg >= 1.6 else 0):
                    wp = tb % 2
                    cs = slice(tb * 128, (tb + 1) * 128)
                    Rw, Rk = BQ(4 + 2 * wp, 0, 2)
                    W1, W1k = BQ(4 + 2 * wp, 2, 2)
                    W2, W2k = BQ(5 + 2 * wp, 0, 2)
                    W3, W3k = BQ(5 + 2 * wp, 2, 2)
                    sdec = small[:, 4 * wp:4 * wp + 4]
                    sdk = [f"sdec{wp}"]
                    P.op("pool", lambda e, Rw=Rw, tb=tb, hp=hp: e.tensor_tensor(
                        Rw.rearrange("p (a t) -> p a t", a=2), tribd[:].unsqueeze(1).to_broadcast([128, 2, 128]),
                        toks[:, 1, tb, 2 * hp:2 * hp + 2].unsqueeze(2).to_broadcast([128, 2, 128]), ALU.mult),
                        reads=CK + ["toks"], writes=Rk)
                    rb, rbk = PQ(7, 2 * wp, 2)
                    P.op("pe", lambda e, rb=rb, Rw=Rw: e.matmul(rb, ones[:], Rw, start=True, stop=True), reads=CK + Rk, writes=rbk)
                    yield
                    for hh in range(2):
                        h = 2 * hp + hh
                        P.op("dve", lambda e, hh=hh, h=h, tb=tb, rb=rb, W1=W1: e.tensor_scalar(
                            W1[:, hh * 128:(hh + 1) * 128], rb[:, hh * 128:(hh + 1) * 128], toks[:, 2, tb, h:h + 1], 0.0,
                            ALU.subtract, ALU.max), reads=rbk + ["toks"], writes=W1k)
                        P.op("dve", lambda e, hh=hh, h=h, tb=tb, rb=rb, W2=W2: e.tensor_scalar(
                            W2[:, hh * 128:(hh + 1) * 128], rb[:, hh * 128:(hh + 1) * 128], toks[:, 2, tb, h:h + 1], 0.0,
                            ALU.subtract, ALU.min), reads=rbk + ["toks"], writes=W2k)
                    yield
                    P.op("act", lambda e, W1=W1: e.activation(W1, W1, AF.Exp, scale=-1.0), reads=W1k, writes=W1k)
                    P.op("act", lambda e, W2=W2: e.activation(W2, W2, AF.Exp), reads=W2k, writes=W2k)
                    P.op("act", lambda e, W3=W3, rb=rb: e.activation(W3, rb, AF.Exp, bias=float(-0.5 * np.log(128.0))), reads=rbk, writes=W3k)
                    P.op("act", lambda e, rb=rb, sdec=sdec: e.activation(
                        sdec.rearrange("p (a c) -> p a c", a=2), rb.rearrange("p (a c t) -> p a c t", a=2, c=2)[:, :, :, 63], AF.Exp),
                        reads=rbk, writes=sdk)
                    P.op("pool", lambda e, W1=W1: e.tensor_tensor(W1.rearrange("p (a t) -> p a t", a=2), W1.rearrange("p (a t) -> p a t", a=2),
                                                                  msbd[:].unsqueeze(1).to_broadcast([128, 2, 128]), ALU.mult),
                         reads=CK + W1k, writes=W1k)
                    P.op("pool", lambda e, W2=W2: e.tensor_tensor(W2.rearrange("p (a t) -> p a t", a=2), W2.rearrange("p (a t) -> p a t", a=2),
                                                                  minclt[:].unsqueeze(1).to_broadcast([128, 2, 128]), ALU.mult),
                         reads=CK + W2k, writes=W2k)
                    def unit_gen(hh, up):
                        h = 2 * hp + hh
                        base = 8 + 3 * up
                        names = ["kbg", "kdec", "vb", "sz", "L", "LTs", "AT", "NTa", "NTb", "Pa", "Pb", "PTa"]
                        Tl = {}
                        for n_i, nm in enumerate(names):
                            Tl[nm] = BQ(base + n_i // 4, n_i % 4)
                        extra = ["PTb", "u", "wT", "qg", "vnew", "og"]
                        for n_i, nm in enumerate(extra):
                            Tl[nm] = BQ((n_i + 6 * up) // 4, (n_i + 6 * up) % 4)
                        Dm = W1[:, hh * 128:(hh + 1) * 128]; Dmk = [W1k[hh]]
                        DTm = W2[:, hh * 128:(hh + 1) * 128]; DTmk = [W2k[hh]]
                        egr = W3[:, hh * 128:(hh + 1) * 128]; egk = [W3k[hh]]
                        bt = toks[:, 0, tb, h:h + 1]; bg2 = toks[:, 4, tb, h:h + 1]; kd = toks[:, 5, tb, h:h + 1]
                        qb_, kb_, vb_, zb_ = qf[hh][:, cs], kf[hh][:, cs], vf[hh][:, cs], zf[hh][:, cs]
                        SK = [f"S{h}"]
                        Sh = Sst[:, h, :]
                        bA, bB = 3 + up, 5 + up
                        pt, ptk = PQ(bA, 0, 3)
                        P.op("pe", lambda e, pt=pt, kb_=kb_: e.transpose(pt[:, 0:128], kb_, ident[:]), reads=CK + [kk_[hh]], writes=[ptk[0]])
                        P.op("pe", lambda e, pt=pt, vb_=vb_: e.transpose(pt[:, 128:256], vb_, ident[:]), reads=CK + [vk_[hh]], writes=[ptk[1]])
                        P.op("pe", lambda e, pt=pt, zb_=zb_: e.transpose(pt[:, 256:384], zb_, ident[:]), reads=CK + [zk_[hh]], writes=[ptk[2]])
                        pk2_, pk2k = PQ(bB, 0, 2)
                        P.op("pe", lambda e, pk2_=pk2_, kb_=kb_: e.matmul(pk2_[:, 0:128], kb_, kb_, start=True, stop=True),
                             reads=[kk_[hh]], writes=[pk2k[0]])
                        P.op("pe", lambda e, pk2_=pk2_, kb_=kb_, qb_=qb_: e.matmul(pk2_[:, 128:256], kb_, qb_, start=True, stop=True),
                             reads=[kk_[hh], qk_[hh]], writes=[pk2k[1]])
                        yield
                        (Lt, Lk), (LTs, LTsk), (AT, ATk) = Tl["L"], Tl["LTs"], Tl["AT"]
                        P.op("dve", lambda e, Lt=Lt, pk2_=pk2_, bt=bt, Dm=Dm: e.scalar_tensor_tensor(
                            Lt, pk2_[:, 0:128], bt, Dm, ALU.mult, ALU.mult), reads=[pk2k[0], "toks"] + Dmk, writes=Lk)
                        (kbg, kbgk), (kdec, kdeck), (vb, vbk), (sz, szk) = Tl["kbg"], Tl["kdec"], Tl["vb"], Tl["sz"]
                        P.op("act", lambda e, pt=pt, kdec=kdec, kd=kd: e.activation(kdec, pt[:, 0:128], AF.Copy, scale=kd),
                             reads=[ptk[0], "toks"], writes=kdeck)
                        yield
                        plt, pltk = PQ(bB, 0)
                        P.op("pe", lambda e, plt=plt, Lt=Lt: e.transpose(plt, Lt, ident[:]), reads=CK + Lk, writes=pltk)
                        P.op("dve", lambda e, AT=AT, pk2_=pk2_, DTm=DTm: e.tensor_tensor(AT, pk2_[:, 128:256], DTm, ALU.mult),
                             reads=[pk2k[1]] + DTmk, writes=ATk)
                        P.op("act", lambda e, pt=pt, sz=sz: e.activation(sz, pt[:, 256:384], AF.Silu), reads=[ptk[2]], writes=szk)
                        yield
                        NT = [Tl["NTa"], Tl["NTb"]]
                        Pm = [Tl["Pa"], Tl["Pb"]]
                        PTm = [Tl["PTa"], Tl["PTb"]]
                        P.op("dve", lambda e, plt=plt, nt=NT[0][0]: e.scalar_tensor_tensor(nt, plt, -1.0, ident[:], ALU.mult, ALU.add),
                             reads=CK + pltk, writes=NT[0][1])
                        P.op("act", lambda e, plt=plt, LTs=LTs: e.copy(LTs, plt), reads=pltk, writes=LTsk)
                        yield
                        pq1, pq1k = PQ(bB, 3)
                        pq2, pq2k = PQ(bA, 3)
                        P.op("pe", lambda e, pq1=pq1, LTs=LTs, Lt=Lt: e.matmul(pq1, LTs, Lt, start=True, stop=True),
                             reads=LTsk + Lk, writes=pq1k)
                        P.op("pe", lambda e, pq2=pq2, LTs=LTs, Lt=Lt: e.matmul(pq2, Lt, LTs, start=True, stop=True),
                             reads=LTsk + Lk, writes=pq2k)
                        yield
                        P.op("dve", lambda e, pq1=pq1, d=Pm[0][0]: e.tensor_copy(d, pq1), reads=pq1k, writes=Pm[0][1])
                        P.op("act", lambda e, pq2=pq2, d=PTm[0][0]: e.copy(d, pq2), reads=pq2k, writes=PTm[0][1])
                        yield
                        cur = 0
                        for lev in range(5):
                            Pc, Pck = Pm[lev % 2]
                            PTc, PTck = PTm[lev % 2]
                            ntc, ntck = NT[cur]
                            ntn, ntnk = NT[1 - cur]
                            pu, puk = PQ(bB, 0)
                            P.op("pe", lambda e, pu=pu, Pc=Pc, ntc=ntc: e.matmul(pu, Pc, ntc, start=True, stop=True),
                                 reads=Pck + ntck, writes=puk)
                            if lev < 4:
                                Pn, Pnk = Pm[(lev + 1) % 2]
                                PTn, PTnk = PTm[(lev + 1) % 2]
                                P.op("pe", lambda e, pq1=pq1, PTc=PTc, Pc=Pc: e.matmul(pq1, PTc, Pc, start=True, stop=True),
                                     reads=PTck + Pck, writes=pq1k)
                                P.op("pe", lambda e, pq2=pq2, PTc=PTc, Pc=Pc: e.matmul(pq2, Pc, PTc, start=True, stop=True),
                                     reads=PTck + Pck, writes=pq2k)
                            yield
                            P.op("dve", lambda e, pu=pu, ntc=ntc, ntn=ntn: e.tensor_tensor(ntn, pu, ntc, ALU.add),
                                 reads=puk + ntck, writes=ntnk)
                            cur = 1 - cur
                            if lev < 4:
                                P.op("act", lambda e, pq2=pq2, PTn=PTn: e.copy(PTn, pq2), reads=pq2k, writes=PTnk)
                                P.op("dve", lambda e, pq1=pq1, Pn=Pn: e.tensor_copy(Pn, pq1), reads=pq1k, writes=Pnk)
                            if lev == 1:
                                P.op("dve", lambda e, pt=pt, kbg=kbg, bg2=bg2: e.tensor_scalar(kbg, pt[:, 0:128], bg2, None, ALU.mult),
                                     reads=[ptk[0], "toks"], writes=kbgk)
                            if lev == 2:
                                P.op("dve", lambda e, pt=pt, vb=vb, bt=bt: e.tensor_scalar(vb, pt[:, 128:256], bt, None, ALU.mult),
                                     reads=[ptk[1], "toks"], writes=vbk)
                            yield
                        TTm, TTk = NT[cur]
                        (u_, uk), (wT, wTk), (qg, qgk), (vnew, vnk), (og, ogk) = Tl["u"], Tl["wT"], Tl["qg"], Tl["vnew"], Tl["og"]
                        P.op("pe", lambda e, pq1=pq1, TTm=TTm, vb=vb: e.matmul(pq1, TTm, vb, start=True, stop=True),
                             reads=TTk + vbk, writes=pq1k)
                        P.op("pe", lambda e, pq2=pq2, TTm=TTm, kbg=kbg: e.matmul(pq2, kbg, TTm, start=True, stop=True),
                             reads=TTk + kbgk, writes=pq2k)
                        P.op("pool", lambda e, qg=qg, qb_=qb_, egr=egr: e.tensor_tensor(qg, qb_, egr, ALU.mult),
                             reads=[qk_[hh]] + egk, writes=qgk)
                        yield
                        P.op("act", lambda e, pq1=pq1, u_=u_: e.copy(u_, pq1), reads=pq1k, writes=uk)
                        P.op("dve", lambda e, pq2=pq2, wT=wT: e.tensor_copy(wT, pq2), reads=pq2k, writes=wTk)
                        yield
                        po, pok = PQ(2 if up == 0 else 7, 0)
                        pws, pwsk = PQ(bB, 1)
                        pds, pdsk = PQ(bB, 2)
                        for c in range(2):
                            rs = slice(64 * c, 64 * c + 64)
                            P.op("pe", lambda e, pws=pws, wT=wT, rs=rs, Sh=Sh: e.matmul(pws[rs, :], wT[:, rs], Sh, start=True, stop=True),
                                 reads=wTk + SK, writes=pwsk)
                            P.op("pe", lambda e, po=po, qg=qg, rs=rs, Sh=Sh: e.matmul(po[rs, :], qg[:, rs], Sh, start=True, stop=False),
                                 reads=qgk + SK, writes=pok)
                            yield
                            P.op("dve", lambda e, vnew=vnew, u_=u_, pws=pws, rs=rs: e.tensor_tensor(vnew[rs, :], u_[rs, :], pws[rs, :], ALU.subtract),
                                 reads=uk + pwsk, writes=vnk)
                            yield
                            P.op("pe", lambda e, po=po, AT=AT, rs=rs, vnew=vnew: e.matmul(po[rs, :], AT[rs, rs], vnew[rs, :], start=False, stop=True),
                                 reads=ATk + vnk, writes=pok)
                            P.op("pe", lambda e, pds=pds, kdec=kdec, rs=rs, vnew=vnew: e.matmul(pds, kdec[rs, :], vnew[rs, :], start=True, stop=True),
                                 reads=kdeck + vnk, writes=pdsk)
                            yield
                            P.op("dve", lambda e, Sh=Sh, sdec=sdec, hh=hh, c=c, pds=pds: e.scalar_tensor_tensor(
                                Sh, Sh, sdec[:, 2 * hh + c:2 * hh + c + 1], pds, ALU.mult, ALU.add),
                                reads=SK + sdk + pdsk, writes=SK)
                            yield
                        ss = small[:, 8 + up:9 + up]
                        ssk = [f"ss{up}"]
                        P.op("dve", lambda e, ss=ss: e.memset(ss, 0.0), writes=ssk)
                        P.op("act", lambda e, og=og, po=po, ss=ss: e.activation(og, po, AF.Square, accum_out=ss), reads=pok + ssk, writes=ogk + ssk)
                        yield
                        P.op("act", lambda e, ss=ss: e.activation(ss, ss, AF.Sqrt, scale=1.0 / 128.0, bias=NORM_EPS), reads=ssk, writes=ssk)
                        yield
                        P.op("dve", lambda e, ss=ss: e.reciprocal(ss, ss), reads=ssk, writes=ssk)
                        P.op("dve", lambda e, og=og, po=po, ss=ss: e.scalar_tensor_tensor(og, po, ss, normw[:], ALU.mult, ALU.mult),
                             reads=CK + pok + ssk, writes=ogk)
                        yield
                        P.op("pool", lambda e, og=og, sz=sz: e.tensor_tensor(og, og, sz, ALU.mult), reads=ogk + szk, writes=ogk)
                        yield
                        P.op("pe", lambda e, pws=pws, og=og: e.transpose(pws, og, ident[:]), reads=CK + ogk, writes=pwsk)
                        yield
                        P.op("act", lambda e, pws=pws, h=h, cs=cs: e.copy(bufA[:, h, cs], pws), reads=pwsk, writes=[f"A{h}"])

                    gens = [unit_gen(0, 0), unit_gen(1, 1)] if dbg >= 1.7 else []
                    while gens:
                        for g_ in list(gens):
                            try:
                                next(g_)
                            except StopIteration:
                                gens.remove(g_)
                        yield


            def lockstep(gs):
                gs = list(gs)
                while gs:
                    for gi_, g_ in enumerate(list(gs)):
                        for _rep in range(URATIO if gi_ == 0 else 1):
                            try:
                                next(g_)
                            except StopIteration:
                                if g_ in gs:
                                    gs.remove(g_)
                                break

            if dbg >= 1.3:
                lockstep([proj_stage(0)])
                for hp in range(4):
                    lockstep([units_stream(hp)] + ([proj_stage(hp + 1)] if hp < 3 else []))

            pwv = bufB[:].rearrange("p c t -> p (c t)")[:, 0:2048].rearrange("p (g k d) -> p g k d", g=4, k=2)
            if dbg >= 3:
                P.dma("poolw", lambda e: e.dma_start(out=pwv, in_=poolw_d.rearrange("g (k p) d -> p g k d", p=128)),
                      writes=AK("B", range(4)), E="pool")
            for g in range(4 if dbg >= 3 else 0):
                win = 2 ** (g + 1)
                for cc in range(2):
                    pc = 2 * g + cc
                    pb, pk = project(("w_in", 0, 4112 + g * 256 + cc * 128), hT, "H")
                    r = raw[cc]; rk_ = f"raw{cc}"
                    P.op("act", lambda e, pb=pb, r=r: e.copy(r[:, 15:527], pb[:]), reads=pk, writes=[rk_])
                    P.op("pool", lambda e, r=r, pc=pc: e.tensor_copy(r[:, 0:15], phalo[:, pc, :]), reads=["phalo"], writes=[rk_])
                    P.op("pool", lambda e, r=r, pc=pc: e.tensor_copy(phalo[:, pc, :], r[:, 512:527]), reads=[rk_], writes=["phalo"])
                    src, srck = r, rk_
                    for lev in range(g + 1):
                        sh = 2 ** lev
                        lo = 2 * sh - 1
                        d_ = raw[2 + lev % 2]; dk_ = f"raw{2 + lev % 2}"
                        eng = "dve" if (lev + cc) % 2 == 0 else "pool"
                        P.op(eng, lambda e, d_=d_, src=src, lo=lo, sh=sh: e.tensor_tensor(d_[:, lo:527], src[:, lo:527], src[:, lo - sh:527 - sh], ALU.add),
                             reads=[srck], writes=[dk_])
                        src, srck = d_, dk_
                    dst = bufB[:, 8 + pc, :]
                    P.op("dve", lambda e, dst=dst, src=src, win=win, r=r: e.scalar_tensor_tensor(
                        dst, src[:, 15:527], 1.0 / win, r[:, 15:527], ALU.mult, ALU.subtract), reads=[srck, rk_], writes=[f"B{8 + pc}"])
                    if ti == 0:
                        P.op("dve", lambda e, dst=dst, src=src, g=g: e.tensor_tensor(dst[:, 0:16], src[:, 15:31], rc16[:, g, :], ALU.mult),
                             reads=CK + [srck, f"B{8 + pc}"], writes=[f"B{8 + pc}"])
                        P.op("dve", lambda e, dst=dst, r=r: e.tensor_tensor(dst[:, 0:16], f32(dst)[:, 0:16], r[:, 15:31], ALU.subtract),
                             reads=[rk_, f"B{8 + pc}"], writes=[f"B{8 + pc}"])
                for dch in range(2):
                    pb, pk = proj_bank()
                    for cc in range(2):
                        P.op("pe", lambda e, g=g, cc=cc, dch=dch, pb=pb: e.matmul(
                            pb[:], pwv[:, g, cc, dch * 128:(dch + 1) * 128], bufB[:, 8 + 2 * g + cc, :], start=(cc == 0), stop=(cc == 1)),
                            reads=AK("B", range(4)) + [f"B{8 + 2 * g + cc}"], writes=pk)
                    P.op("act", lambda e, g=g, dch=dch, pb=pb: e.activation(bufA[:, 8 + 2 * g + dch, :], pb[:], AF.Copy,
                                                                             scale=pscale[:, 2 * g + dch:2 * g + dch + 1]),
                         reads=CK + pk, writes=[f"A{8 + 2 * g + dch}"])

            for c in range(16 if dbg >= 4 else 0):
                pb, pk = project(("w_out", 0, c * 128), bufA, "A")
                P.op("dve", lambda e, c=c, pb=pb: e.scalar_tensor_tensor(bufB[:, c, :], f32(hT[:, c, :]), ALPHA, pb[:], ALU.mult, ALU.add),
                     reads=pk + [f"H{c}"], writes=[f"B{c}"])
            if dbg >= 4:
                layer_norm(bufB, "B", 0)

            if stop_after >= 2:
                sc = 512.0 ** -0.5

                def qproj_gen(hd):
                    for dc in range(4):
                        c = 4 * hd + dc
                        pb, pk = yield from project_gen(("xq_w", 0, c * 128), hT, "H")
                        if c % 2:
                            P.op("act", lambda e, c=c, pb=pb: e.copy(bufA[:, c, :], pb[:]), reads=pk, writes=[f"A{c}"])
                        else:
                            P.op("dve", lambda e, c=c, pb=pb: e.tensor_copy(bufA[:, c, :], pb[:]), reads=pk, writes=[f"A{c}"])
                        yield

                def attn_chain(hd, qb, PTs):
                    bank = 3 + qb
                    psc, psck = PQ(bank, 0, 2)
                    for dc in range(4):
                        P.op("pe", lambda e, psc=psc, hd=hd, dc=dc, qb=qb: e.matmul(
                            psc, bufA[:, 4 * hd + dc, qb * 128:(qb + 1) * 128], kxT[:, 4 * hd + dc, :], start=(dc == 0), stop=(dc == 3)),
                            reads=[f"A{4 * hd + dc}", "kx"], writes=psck)
                    yield
                    mx = small2[:, qb:qb + 1]; mxk = [f"mx{qb}"]
                    sm = small2[:, 4 + qb:5 + qb]; smk = [f"sm{qb}"]
                    pe_, pek = BQ(8 + qb, 0, 2)
                    P.op("dve", lambda e, mx=mx, psc=psc: e.reduce_max(mx, psc, AX.X), reads=psck, writes=mxk)
                    P.op("dve", lambda e, sm=sm: e.memset(sm, 0.0), writes=smk)
                    yield
                    P.op("dve", lambda e, mx=mx: e.tensor_scalar(mx, mx, -sc, None, ALU.mult), reads=mxk, writes=mxk)
                    yield
                    P.op("act", lambda e, pe_=pe_, psc=psc, mx=mx, sm=sm: e.activation(pe_, psc, AF.Exp, bias=mx, scale=sc, accum_out=sm),
                         reads=psck + mxk + smk, writes=pek + smk)
                    yield
                    P.op("dve", lambda e, sm=sm: e.reciprocal(sm, sm), reads=smk, writes=smk)
                    yield
                    P.op("pool", lambda e, pe_=pe_, sm=sm: e.tensor_scalar(pe_, pe_, sm, None, ALU.mult), reads=pek + smk, writes=pek)
                    yield
                    ptt, pttk = PQ(bank, 2, 2)
                    for mb in range(2):
                        P.op("pe", lambda e, ptt=ptt, pe_=pe_, mb=mb: e.transpose(ptt[:, mb * 128:(mb + 1) * 128], pe_[:, mb * 128:(mb + 1) * 128], ident[:]),
                             reads=CK + pek, writes=pttk)
                    yield
                    P.op("dve", lambda e, ptt=ptt, qb=qb, d=PTs[0][0]: e.tensor_copy(d[:, qb * 128:(qb + 1) * 128], ptt[:, 0:128]),
                         reads=pttk, writes=[PTs[0][1][qb]])
                    P.op("act", lambda e, ptt=ptt, qb=qb, d=PTs[1][0]: e.copy(d[:, qb * 128:(qb + 1) * 128], ptt[:, 128:256]),
                         reads=pttk, writes=[PTs[1][1][qb]])

                def attn_head(hd):
                    PTs = [(big[4 + hd % 2], [f"a{4 + hd % 2}_{q}" for q in range(4)]),
                           (big[6 + hd % 2], [f"a{6 + hd % 2}_{q}" for q in range(4)])]
                    chains = [attn_chain(hd, qb, PTs) for qb in range(4)]
                    while chains:
                        for g_ in list(chains):
                            try:
                                next(g_)
                            except StopIteration:
                                chains.remove(g_)
                        yield
                    for dhc in range(4):
                        pb, pk = (ps[2], ["p2"]) if dhc % 2 == 0 else (ps[7], ["p7"])
                        for mb in range(2):
                            P.op("pe", lambda e, pb=pb, mb=mb, hd=hd, dhc=dhc, d=PTs[mb][0]: e.matmul(
                                pb[:], f32(vx[:, mb, hd * 512 + dhc * 128: hd * 512 + (dhc + 1) * 128]), d[:], start=(mb == 0), stop=(mb == 1)),
                                reads=["vx"] + PTs[mb][1], writes=pk)
                        yield
                        oc = 4 * hd + dhc
                        if dhc % 2:
                            P.op("act", lambda e, oc=oc, pb=pb: e.copy(bufB[:, oc, :], pb[:]), reads=pk, writes=[f"B{oc}"])
                        else:
                            P.op("dve", lambda e, oc=oc, pb=pb: e.tensor_copy(bufB[:, oc, :], pb[:]), reads=pk, writes=[f"B{oc}"])
                        yield

                lockstep([qproj_gen(0)])
                for hd in range(4):
                    lockstep([attn_head(hd)] + ([qproj_gen(hd + 1)] if hd < 3 else []))
                for c in range(16):
                    pb, pk = project(("xo_w", 0, c * 128), bufB, "B")
                    P.op("dve", lambda e, c=c, pb=pb: e.scalar_tensor_tensor(bufA[:, c, :], f32(hT[:, c, :]), ALPHA, pb[:], ALU.mult, ALU.add),
                         reads=pk + [f"H{c}"], writes=[f"A{c}"])
                layer_norm(bufA, "A", 1)

            if stop_after >= 3:
                for g in range(4):
                    for j in range(16):
                        pb, pk = project(("w_up", 0, g * 2048 + j * 128), hT, "H")
                        rl, rlk = big[8 + j % 4], [f"a{8 + j % 4}_{q}" for q in range(4)]
                        P.op("act", lambda e, pb=pb, rl=rl: e.activation(rl[:], pb[:], AF.Relu), reads=pk, writes=rlk)
                        if j % 2:
                            P.op("act", lambda e, j=j, rl=rl: e.activation(bufB[:, j, :], rl[:], AF.Square), reads=rlk, writes=[f"B{j}"])
                        else:
                            P.op("dve", lambda e, j=j, rl=rl: e.tensor_tensor(bufB[:, j, :], rl[:], rl[:], ALU.mult), reads=rlk, writes=[f"B{j}"])
                    for j in range(16):
                        pb, pk = project(("w_down", g * 2048, j * 128), bufB, "B")
                        if g == 0:
                            P.op("dve", lambda e, j=j, pb=pb: e.scalar_tensor_tensor(bufA[:, j, :], f32(hT[:, j, :]), ALPHA, pb[:], ALU.mult, ALU.add),
                                 reads=pk + [f"H{j}"], writes=[f"A{j}"])
                        elif g < 3:
                            P.op("dve", lambda e, j=j, pb=pb: e.tensor_tensor(bufA[:, j, :], pb[:], f32(bufA[:, j, :]), ALU.add),
                                 reads=pk + [f"A{j}"], writes=[f"A{j}"])
                        else:
                            P.op("dve", lambda e, j=j, pb=pb: e.tensor_tensor(bufA[:, j, :], pb[:], f32(bufA[:, j, :]), ALU.add),
                                 reads=pk + [f"A{j}"], writes=[f"A{j}"])
                layer_norm(bufA, "A", 2)

            for tb in range(4):
                obig = (8, 9, 10, 11) if tb % 2 == 0 else (12, 13, 0, 1)
                for dq in range(4):
                    pb, pk = proj_bank()
                    for d4 in range(4):
                        dc = dq * 4 + d4
                        P.op("pe", lambda e, pb=pb, d4=d4, dc=dc, tb=tb: e.transpose(pb[:, d4 * 128:(d4 + 1) * 128], f32(hT[:, dc, tb * 128:(tb + 1) * 128]), ident[:]),
                             reads=CK + [f"H{dc}"], writes=pk)
                    dsto = big[obig[dq]]
                    dstk = [f"a{obig[dq]}_{q}" for q in range(4)]
                    if dq % 2:
                        P.op("act", lambda e, pb=pb, dsto=dsto: e.copy(dsto[:], pb[:]), reads=pk, writes=dstk)
                    else:
                        P.op("dve", lambda e, pb=pb, dsto=dsto: e.tensor_copy(dsto[:], pb[:]), reads=pk, writes=dstk)
                    t = P.dma(f"out{tb % 2}_{dq}", lambda e, tb=tb, dq=dq, dsto=dsto, t0=t0: e.dma_start(
                        out=out_d[t0 + tb * 128:t0 + (tb + 1) * 128, dq * 512:(dq + 1) * 512], in_=dsto[:]), reads=dstk)
                    out_toks.append(t)
        P.wait_tokens("sp", out_toks)
        P.emit()
    return nc


def host_inputs(inp, b, ntiles=4):
    T = ntiles * TT
    f = lambda a: np.ascontiguousarray(a, dtype=np.float32)
    m = {"x": f(inp["x"][b, :T]), "mem": f(inp["mem"][b])}
    for k in ("w_in", "w_out", "xq_w", "xk_w", "xv_w", "xo_w", "w_up", "w_down", "pool_w"):
        m[k] = f(inp[k][0])
    m["cw"] = f(np.asarray(inp["conv_w"][0]).T.reshape(24, 128, 4).transpose(1, 0, 2))
    m["alog"] = f(np.asarray(inp["a_log"][0]).reshape(8, 1))
    m["dtb"] = f(np.asarray(inp["dt_bias"][0]).reshape(8, 1))
    m["normw"] = f(np.broadcast_to(np.asarray(inp["gdn_norm_w"][0])[None, :], (128, 128)))
    m["pscale"] = f(np.asarray(inp["pool_scale"][0]).reshape(8, 128).T)
    lnp = np.stack([np.asarray(inp[k][0]).reshape(16, 128).T for k in ("ln1_g", "ln1_b", "ln2_g", "ln2_b", "ln3_g", "ln3_b")], axis=1)
    m["lnp"] = f(lnp)
    m.update(host_consts())
    return m


_NC_CACHE = {}


def kernel(**inputs):
    inp = {k: np.asarray(v) for k, v in inputs.items()}
    if "nc" not in _NC_CACHE:
        _NC_CACHE["nc"] = build(4, 3)
    nc = _NC_CACHE["nc"]
    in_maps = [host_inputs(inp, b) for b in range(8)]
    res = run_bass_kernel_spmd(nc, in_maps, core_ids=list(range(8)))
    out = np.stack([np.asarray(res.results[b]["out"]) for b in range(8)], axis=0)
    return out.astype(np.float32)
```
